# Optimizing a Trainium2 kernel written in Bass

```python
import math
import jax
import jax.numpy as jnp
from jax import lax
import numpy as np

D_MODEL = 2048
BATCH = 2
SEQ = 4096
DEPTH = 4

HEAD_DIM = 128
N_DIFF_HEADS = D_MODEL // (2 * HEAD_DIM)
DIFF_QK_DIM = HEAD_DIM // 2
N_NSA_HEADS = D_MODEL // (2 * HEAD_DIM)
N_NSA_KV = 2
NSA_HPG = N_NSA_HEADS // N_NSA_KV
NSA_CMP_LEN = 32
NSA_CMP_STRIDE = 16
NSA_CMP_HIDDEN = 256
NSA_SEL_BLOCK = 64
NSA_TOP_N = 16
NSA_WINDOW = 512
D_FF = 5632
ROPE_THETA = 500000.0
ROPE_FRACTION = 4
Q_BLOCK = 128
SEL_Q_BLOCK = 64
N_MOD = 9
RMS_EPS = 1e-6
NEG = -1e30
FORCE_BONUS = 1e6
DIFF_W = N_DIFF_HEADS * HEAD_DIM
NSA_W = N_NSA_HEADS * HEAD_DIM
NSA_KV_W = N_NSA_KV * HEAD_DIM
IN_COLS = 3 * DIFF_W + NSA_W + 6 * NSA_KV_W + 3 * N_NSA_HEADS

kernel_name = 'hybrid_diffattn_nsa_macaron_adaln'


def rms_norm(x, g):
    xf = x.astype(jnp.float32)
    y = xf * lax.rsqrt(jnp.mean(xf * xf, axis=-1, keepdims=True) + RMS_EPS)
    return (y * g.astype(jnp.float32)).astype(x.dtype)


def rope(x, rot_dim):
    S = x.shape[1]
    half = rot_dim // 2
    inv = jnp.exp(-math.log(ROPE_THETA) * jnp.arange(half, dtype=jnp.float32) / half)
    ang = jnp.arange(S, dtype=jnp.float32)[:, None] * inv[None, :]
    shp = (S,) + (1,) * (x.ndim - 3) + (half,)
    cos = jnp.cos(ang).reshape(shp).astype(x.dtype)
    sin = jnp.sin(ang).reshape(shp).astype(x.dtype)
    x1 = x[..., :half]
    x2 = x[..., half:rot_dim]
    return jnp.concatenate([x1 * cos - x2 * sin, x2 * cos + x1 * sin, x[..., rot_dim:]], axis=-1)


def swiglu(h, w_gu, w_d):
    g, u = jnp.split(h @ w_gu, 2, axis=-1)
    return (jax.nn.silu(g) * u) @ w_d


def gather_rows(a, idx):
    return jax.vmap(jax.vmap(lambda ai, ii: ai[ii]))(a, idx)


def diff_attention(q, k, v, lam, subln_g, lam_init):
    B, S, H = q.shape[0], q.shape[1], q.shape[2]
    nb = S // Q_BLOCK
    scale = DIFF_QK_DIM ** -0.5
    kpos = jnp.arange(S)
    vf = v.astype(jnp.float32)

    def block(i):
        t0 = i * Q_BLOCK
        qb = lax.dynamic_slice_in_dim(q, t0, Q_BLOCK, axis=1)
        s = jnp.einsum('bqhcd,bkhcd->bhcqk', qb, k).astype(jnp.float32) * scale
        mask = (t0 + jnp.arange(Q_BLOCK))[:, None] >= kpos[None, :]
        p = jax.nn.softmax(jnp.where(mask, s, NEG), axis=-1)
        a = p[:, :, 0] - lam * p[:, :, 1]
        return jnp.einsum('bhqk,bkhd->bqhd', a, vf)

    o = lax.map(block, jnp.arange(nb))
    o = o.transpose(1, 0, 2, 3, 4).reshape(B, S, H, HEAD_DIM)
    o = rms_norm(o, subln_g) * (1.0 - lam_init)
    return o.astype(v.dtype)


def nsa_attention(q, k_c, v_c, k_s, v_s, k_w, v_w, gates, cmp_pe, cmp_w1, cmp_w2):
    B, S, H, dk = q.shape
    G, J = N_NSA_KV, NSA_HPG
    qg = q.reshape(B, S, G, J, dk)
    scale = dk ** -0.5
    tpos = jnp.arange(S)

    nc = (S - NSA_CMP_LEN) // NSA_CMP_STRIDE + 1
    cidx = jnp.arange(nc)[:, None] * NSA_CMP_STRIDE + jnp.arange(NSA_CMP_LEN)[None, :]

    def compress(kv, j):
        blk = kv[:, cidx] + cmp_pe[j][None, None, :, None, :]
        blk = blk.transpose(0, 1, 3, 2, 4).reshape(B, nc, G, NSA_CMP_LEN * dk)
        return jax.nn.silu(blk @ cmp_w1[j]) @ cmp_w2[j]

    kc = compress(k_c, 0)
    vc = compress(v_c, 1)
    s_c = jnp.einsum('bsgjd,bngd->bgjsn', qg, kc).astype(jnp.float32) * scale
    cend = jnp.arange(nc) * NSA_CMP_STRIDE + NSA_CMP_LEN - 1
    cmask = cend[None, :] <= tpos[:, None]
    p_c = jnp.where(cmask, jax.nn.softmax(jnp.where(cmask, s_c, NEG), axis=-1), 0.0)
    o_c = jnp.einsum('bgjsn,bngd->bsgjd', p_c, vc.astype(jnp.float32))

    ns = S // NSA_SEL_BLOCK
    n_top = min(NSA_TOP_N, ns)
    cs = jnp.arange(nc) * NSA_CMP_STRIDE
    ss = jnp.arange(ns) * NSA_SEL_BLOCK
    ov = jnp.clip(jnp.minimum(cs[:, None] + NSA_CMP_LEN, ss[None, :] + NSA_SEL_BLOCK)
                  - jnp.maximum(cs[:, None], ss[None, :]), 0, None).astype(jnp.float32) / NSA_CMP_LEN
    imp = jnp.einsum('bgjsn,nm->bgsm', p_c, ov)
    sblk = jnp.arange(ns)[None, :]
    cur = (tpos // NSA_SEL_BLOCK)[:, None]
    valid = sblk * NSA_SEL_BLOCK <= tpos[:, None]
    forced = (sblk == 0) | (sblk == cur) | (sblk == cur - 1)
    score = jnp.where(valid, imp + jnp.where(forced, FORCE_BONUS, 0.0), NEG)
    top_val, top_idx = lax.top_k(score, n_top)
    top_ok = top_val > 0.5 * NEG

    kv_blk = jnp.concatenate([k_s, v_s], axis=-1).transpose(0, 2, 1, 3)
    kv_blk = kv_blk.reshape(B, G, ns, NSA_SEL_BLOCK, 2 * dk)

    def sel_block(i):
        t0 = i * SEL_Q_BLOCK
        qb = lax.dynamic_slice_in_dim(qg, t0, SEL_Q_BLOCK, axis=1)
        ib = lax.dynamic_slice_in_dim(top_idx, t0, SEL_Q_BLOCK, axis=2)
        okb = lax.dynamic_slice_in_dim(top_ok, t0, SEL_Q_BLOCK, axis=2)
        kvg = gather_rows(kv_blk, ib.reshape(B, G, SEL_Q_BLOCK * n_top))
        kvg = kvg.reshape(B, G, SEL_Q_BLOCK, n_top * NSA_SEL_BLOCK, 2 * dk)
        kg = kvg[..., :dk]
        vg = kvg[..., dk:].astype(jnp.float32)
        s = jnp.einsum('bqgjd,bgqkd->bgjqk', qb, kg).astype(jnp.float32) * scale
        tok = (ib[..., None] * NSA_SEL_BLOCK + jnp.arange(NSA_SEL_BLOCK)).reshape(B, G, SEL_Q_BLOCK, -1)
        tq = t0 + jnp.arange(SEL_Q_BLOCK)
        m = (tok <= tq[:, None]) & jnp.repeat(okb, NSA_SEL_BLOCK, axis=-1)
        p = jax.nn.softmax(jnp.where(m[:, :, None], s, NEG), axis=-1)
        return jnp.einsum('bgjqk,bgqkd->bqgjd', p, vg)

    o_s = lax.map(sel_block, jnp.arange(S // SEL_Q_BLOCK))
    o_s = o_s.transpose(1, 0, 2, 3, 4, 5).reshape(B, S, G, J, dk)

    nb = S // Q_BLOCK
    span = NSA_WINDOW + Q_BLOCK
    widx = jnp.arange(nb)[:, None] * Q_BLOCK + jnp.arange(span)[None, :]
    pad = ((0, 0), (NSA_WINDOW, 0), (0, 0), (0, 0))
    kb = jnp.pad(k_w, pad)[:, widx]
    vb = jnp.pad(v_w, pad)[:, widx].astype(jnp.float32)
    qb = qg.reshape(B, nb, Q_BLOCK, G, J, dk)
    s_w = jnp.einsum('bnqgjd,bnkgd->bngjqk', qb, kb).astype(jnp.float32) * scale
    kpos = widx - NSA_WINDOW
    qpos = jnp.arange(nb)[:, None] * Q_BLOCK + jnp.arange(Q_BLOCK)[None, :]
    dist = qpos[:, :, None] - kpos[:, None, :]
    wmask = (dist >= 0) & (dist < NSA_WINDOW) & (kpos[:, None, :] >= 0)
    p_w = jax.nn.softmax(jnp.where(wmask[None, :, None, None], s_w, NEG), axis=-1)
    o_w = jnp.einsum('bngjqk,bnkgd->bnqgjd', p_w, vb).reshape(B, S, G, J, dk)

    g = gates.reshape(B, S, G, J, 3)
    o = g[..., 0:1] * o_c + g[..., 1:2] * o_s + g[..., 2:3] * o_w
    return o.reshape(B, S, H * dk)


def token_mixer(h, w_in, w_o, diff_lam, diff_subln, cmp_pe, cmp_w1, cmp_w2, lam_init):
    B, S, _ = h.shape
    proj = h @ w_in
    sizes = [DIFF_W, DIFF_W, DIFF_W, NSA_W] + [NSA_KV_W] * 6 + [3 * N_NSA_HEADS]
    offs = []
    acc = 0
    for sz in sizes[:-1]:
        acc += sz
        offs.append(acc)
    dq, dk_, dv, nq, kc, vc, ks, vs, kw, vw, gl = jnp.split(proj, offs, axis=-1)

    dq = rope(dq.reshape(B, S, N_DIFF_HEADS, 2, DIFF_QK_DIM), DIFF_QK_DIM // ROPE_FRACTION)
    dk_ = rope(dk_.reshape(B, S, N_DIFF_HEADS, 2, DIFF_QK_DIM), DIFF_QK_DIM // ROPE_FRACTION)
    dv = dv.reshape(B, S, N_DIFF_HEADS, HEAD_DIM)
    lf = diff_lam.astype(jnp.float32)
    lam = jnp.exp(jnp.sum(lf[0] * lf[1])) - jnp.exp(jnp.sum(lf[2] * lf[3])) + lam_init
    o_diff = diff_attention(dq, dk_, dv, lam, diff_subln, lam_init)

    kvs = (B, S, N_NSA_KV, HEAD_DIM)
    rot = HEAD_DIM // ROPE_FRACTION
    nq = rope(nq.reshape(B, S, N_NSA_HEADS, HEAD_DIM), rot)
    ks = rope(ks.reshape(kvs), rot)
    kw = rope(kw.reshape(kvs), rot)
    gates = jax.nn.sigmoid(gl.reshape(B, S, N_NSA_HEADS, 3))
    o_nsa = nsa_attention(nq, kc.reshape(kvs), vc.reshape(kvs), ks, vs.reshape(kvs),
                          kw, vw.reshape(kvs), gates, cmp_pe, cmp_w1, cmp_w2)

    o = jnp.concatenate([o_diff.reshape(B, S, DIFF_W), o_nsa.astype(h.dtype)], axis=-1)
    return o @ w_o


def setup_inputs(seed: int = 0) -> dict:
    key = jax.random.key(seed)
    ks = jax.random.split(key, 15)
    f32 = jnp.float32
    D = D_MODEL
    cl_dk = NSA_CMP_LEN * HEAD_DIM
    return {
        'x': jax.random.normal(ks[0], (BATCH, SEQ, D), f32),
        'c': jax.random.normal(ks[1], (BATCH, D), f32),
        'w_ada': jax.random.normal(ks[2], (DEPTH, D, N_MOD * D), f32) * (0.5 * D ** -0.5),
        'b_ada': jax.random.normal(ks[3], (DEPTH, N_MOD * D), f32) * 0.02,
        'norm_g': 1.0 + 0.02 * jax.random.normal(ks[4], (DEPTH, 3, D), f32),
        'ffn_w_gu': jax.random.normal(ks[5], (DEPTH, 2, D, 2 * D_FF), f32) * D ** -0.5,
        'ffn_w_d': jax.random.normal(ks[6], (DEPTH, 2, D_FF, D), f32) * D_FF ** -0.5,
        'w_in': jax.random.normal(ks[7], (DEPTH, D, IN_COLS), f32) * D ** -0.5,
        'w_o': jax.random.normal(ks[8], (DEPTH, D, D), f32) * D ** -0.5,
        'diff_lam': jax.random.normal(ks[9], (DEPTH, 4, DIFF_QK_DIM), f32) * 0.1,
        'diff_subln': 1.0 + 0.02 * jax.random.normal(ks[10], (DEPTH, HEAD_DIM), f32),
        'cmp_pe': jax.random.normal(ks[11], (DEPTH, 2, NSA_CMP_LEN, HEAD_DIM), f32) * 0.1,
        'cmp_w1': jax.random.normal(ks[12], (DEPTH, 2, cl_dk, NSA_CMP_HIDDEN), f32) * cl_dk ** -0.5,
        'cmp_w2': jax.random.normal(ks[13], (DEPTH, 2, NSA_CMP_HIDDEN, HEAD_DIM), f32) * NSA_CMP_HIDDEN ** -0.5,
        'final_g': 1.0 + 0.02 * jax.random.normal(ks[14], (D,), f32),
    }


def reference(x, c, w_ada, b_ada, norm_g, ffn_w_gu, ffn_w_d, w_in, w_o, diff_lam,
              diff_subln, cmp_pe, cmp_w1, cmp_w2, final_g):
    B = x.shape[0]
    D = x.shape[-1]
    cs = jax.nn.silu(c)
    for l in range(DEPTH):
        mod = (cs @ w_ada[l] + b_ada[l]).reshape(B, N_MOD, 1, D)
        lam_init = 0.8 - 0.6 * math.exp(-0.3 * l)
        h = rms_norm(x, norm_g[l, 0]) * (1.0 + mod[:, 1]) + mod[:, 0]
        x = x + 0.5 * mod[:, 2] * swiglu(h, ffn_w_gu[l, 0], ffn_w_d[l, 0])
        h = rms_norm(x, norm_g[l, 1]) * (1.0 + mod[:, 4]) + mod[:, 3]
        x = x + mod[:, 5] * token_mixer(h, w_in[l], w_o[l], diff_lam[l], diff_subln[l],
                                        cmp_pe[l], cmp_w1[l], cmp_w2[l], lam_init)
        h = rms_norm(x, norm_g[l, 2]) * (1.0 + mod[:, 7]) + mod[:, 6]
        x = x + 0.5 * mod[:, 8] * swiglu(h, ffn_w_gu[l, 1], ffn_w_d[l, 1])
    return rms_norm(x, final_g)
```

```python
import math
from contextlib import ExitStack
import numpy as np
import ml_dtypes
import concourse.bass as bass
import concourse.mybir as mybir
import concourse.bass_utils as bass_utils

F32 = mybir.dt.float32
BF16 = mybir.dt.bfloat16
AF = mybir.ActivationFunctionType
ALU = mybir.AluOpType
NPBF = ml_dtypes.bfloat16

D = 2048
B = 2
S = 4096
DEPTH = 4
DFF = 5632
NCORE = 8
NT = 1024
KC = D // 128
INC = 5656
EPS = 1e-6
ROPE_THETA = 500000.0


class Prog:
    def __init__(self, nc, stack):
        self.nc = nc
        self.stack = stack
        self.E = dict(pe=nc.tensor, act=nc.scalar, dve=nc.vector, pool=nc.gpsimd, sp=nc.sync)
        self.semobj = {}
        self.semval = {}
        for k in self.E:
            self.semobj["e_" + k] = stack.enter_context(nc.semaphore("e_" + k))
            self.semval["e_" + k] = 0
        self.seen = {k: {} for k in self.E}
        self.R = {}
        self.nops = 0

    def sb(self, name, shape, dt):
        return self.stack.enter_context(self.nc.sbuf_tensor(name, shape, dt))

    def psum(self, name, shape, dt):
        return self.stack.enter_context(self.nc.psum_tensor(name, shape, dt))

    def dsem(self, name):
        if name not in self.semobj:
            self.semobj[name] = self.stack.enter_context(self.nc.semaphore(name))
            self.semval[name] = 0
        return name

    def _wait(self, eng, dep, raw):
        name, val = dep
        if name == "e_" + eng:
            if eng == "pe" or not raw:
                return
        if self.seen[eng].get(name, 0) >= val:
            return
        self.E[eng].wait_ge(self.semobj[name], val)
        self.seen[eng][name] = val
        self.nops += 1

    def _hazards(self, eng, r, w):
        for res in r:
            st = self.R.get(res)
            if st and st[0]:
                self._wait(eng, st[0], True)
            if st and res[0] == "ps":
                for nm, v in st[1].items():
                    self._wait(eng, (nm, v), False)
        for res in w:
            st = self.R.get(res)
            if st:
                if st[0]:
                    self._wait(eng, st[0], False)
                for nm, v in st[1].items():
                    self._wait(eng, (nm, v), False)

    def _record(self, tok, r, w):
        for res in r:
            st = self.R.setdefault(res, [None, {}])
            if st[1].get(tok[0], 0) < tok[1]:
                st[1][tok[0]] = tok[1]
        for res in w:
            self.R[res] = [tok, {}]

    def op(self, eng, fn, r=(), w=(), signal=True):
        self._hazards(eng, r, w)
        ins = fn(self.E[eng])
        self.nops += 1
        name = "e_" + eng
        if signal:
            self.semval[name] += 1
            ins.then_inc(self.semobj[name], 1)
            tok = (name, self.semval[name])
        else:
            tok = (name, self.semval[name] + 1)
        self._record(tok, r, w)

    def dma(self, eng, out, in_, sem, r=(), w=()):
        self._hazards(eng, r, w)
        self.dsem(sem)
        self.E[eng].dma_start(out=out, in_=in_).then_inc(self.semobj[sem], 16)
        self.nops += 1
        self.semval[sem] += 16
        self._record((sem, self.semval[sem]), r, w)

    def dma_batch(self, eng, sem, lst):
        self.dsem(sem)
        for (out, in_, r, w) in lst:
            self._hazards(eng, r, w)
        for (out, in_, r, w) in lst:
            self.E[eng].dma_start(out=out, in_=in_).then_inc(self.semobj[sem], 16)
            self.nops += 1
            self.semval[sem] += 16
        for (out, in_, r, w) in lst:
            self._record((sem, self.semval[sem]), r, w)

    def barrier(self, eng):
        for name, val in self.semval.items():
            if val > 0 and name != "e_" + eng and self.seen[eng].get(name, 0) < val:
                self.E[eng].wait_ge(self.semobj[name], val)
                self.seen[eng][name] = val
                self.nops += 1

    def finish(self, eng, sems):
        for s in sems:
            if self.semval.get(s, 0) > 0:
                self.E[eng].wait_ge(self.semobj[s], self.semval[s])


class Item:
    def __init__(self, cls, nslots, load, compute):
        self.cls, self.nslots, self.load, self.compute = cls, nslots, load, compute


def run_stream(rings, items):
    n = len(items)
    slot_of, preds = [], []
    for i, it in enumerate(items):
        pr = set()
        s = []
        if it.cls is not None:
            ring = rings[it.cls]
            for k in range(it.nslots):
                sl = (ring["next"] + k) % ring["size"]
                if ring["last"][sl] is not None:
                    pr.add(ring["last"][sl])
                ring["last"][sl] = i
                s.append(sl)
            ring["next"] = (ring["next"] + it.nslots) % ring["size"]
        slot_of.append(s)
        preds.append(pr)
    loaded = [False] * n
    state = {"j": 0}

    def try_loads(done):
        while state["j"] < n:
            j = state["j"]
            if all(p <= done for p in preds[j]):
                if items[j].load is not None:
                    items[j].load(slot_of[j])
                loaded[j] = True
                state["j"] += 1
            else:
                break

    try_loads(-1)
    for i in range(n):
        assert loaded[i]
        items[i].compute(slot_of[i])
        try_loads(i)


class TK:
    pass


def t_alloc(P, nt):
    T = TK()
    T.nt = nt
    T.nh = nt // 512
    T.xT = P.sb("xT", [128, KC, nt], F32)
    T.hT = P.sb("hT", [128, KC, nt], BF16)
    T.act = P.sb("act", [128, 2, 4, nt], BF16)
    T.wA = P.sb("wA", [128, 4, KC, 256], BF16)
    T.wD = P.sb("wD", [128, 4, D], BF16)
    T.mod = P.sb("mod", [128, 9, KC], F32)
    T.ng = P.sb("ng", [128, 3, KC], F32)
    T.Acoef = P.sb("Acoef", [128, 3, KC], F32)
    T.hgate = P.sb("hgate", [128, 3, KC], F32)
    T.rstd = P.sb("rstd", [128, 512], F32)
    T.sq = P.sb("sq", [128, 2, 512], F32)
    T.tmp = P.sb("tmp", [128, 2, 512], F32)
    T.ones = P.sb("ones", [128, 128], F32)
    T.epsb = P.sb("epsb", [128, 1], F32)
    T.ps = [P.psum("ps%d" % i, [128, 512], F32) for i in range(8)]
    T.rings = {"A": dict(size=4, next=0, last=[None] * 4), "D": dict(size=4, next=0, last=[None] * 4)}
    P.op("dve", lambda e: e.memset(T.ones[:], 1.0), w=[("ones",)])
    P.op("dve", lambda e: e.memset(T.epsb[:], EPS), w=[("epsb",)])
    return T


def t_load_mod(P, T, mod_d, ng_d):
    P.dma("sp", T.mod[:].rearrange("p a c -> p (a c)"), mod_d, "ld_mod", w=[("mod",)])
    P.dma("sp", T.ng[:].rearrange("p a c -> p (a c)"), ng_d, "ld_ng", w=[("ng",)])
    for j in range(3):
        P.op("dve", lambda e: e.scalar_tensor_tensor(out=T.Acoef[:, j, :], in0=T.mod[:, 3 * j + 1, :], scalar=1.0,
                                                     in1=T.ng[:, j, :], op0=ALU.add, op1=ALU.mult),
             r=[("mod",), ("ng",)], w=[("Acoef", j)])
        gs = 1.0 if j == 1 else 0.5
        P.op("dve", lambda e: e.tensor_scalar(out=T.hgate[:, j, :], in0=T.mod[:, 3 * j + 2, :], scalar1=gs, scalar2=None,
                                              op0=ALU.mult),
             r=[("mod",)], w=[("hgate", j)])


def t_norm(P, T, j, out_kind="h", gvec=None, outbuf=None):
    for half in range(T.nh):
        cols = slice(half * 512, (half + 1) * 512)
        bank = T.ps[4 + (half % 2)]
        bres = ("ps", 4 + (half % 2))
        for c in range(KC):
            sq = T.sq[:, c % 2, :]
            P.op("act", lambda e: e.activation(out=sq, in_=T.xT[:, c, cols], func=AF.Square),
                 r=[("x", c, half)], w=[("sq", c % 2)])
            P.op("pe", lambda e: e.matmul(bank[:], T.ones[:], sq, start=(c == 0), stop=(c == KC - 1)),
                 r=[("sq", c % 2), ("ones",)], w=[bres])
        P.op("act", lambda e: e.activation(out=T.rstd[:], in_=bank[:], func=AF.Ln, scale=1.0 / D, bias=T.epsb[:]),
             r=[bres, ("epsb",)], w=[("rstd",)])
        P.op("act", lambda e: e.activation(out=T.rstd[:], in_=T.rstd[:], func=AF.Exp, scale=-0.5),
             r=[("rstd",)], w=[("rstd",)])
        for c in range(KC):
            tmp = T.tmp[:, c % 2, :]
            if out_kind == "h":
                P.op("dve", lambda e: e.scalar_tensor_tensor(out=tmp, in0=T.xT[:, c, cols], scalar=T.Acoef[:, j, c:c + 1],
                                                             in1=T.rstd[:], op0=ALU.mult, op1=ALU.mult),
                     r=[("x", c, half), ("rstd",), ("Acoef", j)], w=[("tmp", c % 2)])
                P.op("act", lambda e: e.activation(out=T.hT[:, c, cols], in_=tmp, func=AF.Identity,
                                                   bias=T.mod[:, 3 * j, c:c + 1], scale=1.0),
                     r=[("tmp", c % 2), ("mod",)], w=[("h", c, half)])
            else:
                P.op("dve", lambda e: e.scalar_tensor_tensor(out=tmp, in0=T.xT[:, c, cols], scalar=gvec[:, c:c + 1],
                                                             in1=T.rstd[:], op0=ALU.mult, op1=ALU.mult),
                     r=[("x", c, half), ("rstd",), ("fg",)], w=[("tmp", c % 2)])
                P.dma("sp", outbuf[c * 128:(c + 1) * 128, cols], tmp, "st_tmp%d" % (c % 2), r=[("tmp", c % 2)])


def t_slabA_load(P, T, slot, src):
    ncols = src.shape[1]
    P.dma("pool", T.wA[:, slot, :, 0:ncols], src.rearrange("(kc p) n -> p kc n", p=128), "ldA%d" % slot,
          w=[("A", slot)])


def ffn_items(P, T, wgu, wd, j):
    items = []
    NP = DFF // 512
    nh = T.nh

    def mk_gu(p, s):
        f0 = (p * 4 + s * 2) * 128

        def load(slots):
            t_slabA_load(P, T, slots[0], wgu[:, f0:f0 + 256])
            t_slabA_load(P, T, slots[1], wgu[:, DFF + f0:DFF + f0 + 256])

        def compute(slots):
            for fc in range(2):
                cj = s * 2 + fc
                for kind in range(2):
                    sl = slots[kind]
                    for half in range(nh):
                        bank = kind * 2 + (half % 2)
                        cols = slice(half * 512, (half + 1) * 512)
                        for kc in range(KC):
                            P.op("pe", lambda e: e.matmul(T.ps[bank][:], T.wA[:, sl, kc, fc * 128:(fc + 1) * 128],
                                                          T.hT[:, kc, cols], start=(kc == 0), stop=(kc == KC - 1)),
                                 r=[("A", sl), ("h", kc, half)], w=[("ps", bank)], signal=(kc == KC - 1))
                        if kind == 0:
                            P.op("act", lambda e: e.activation(out=T.sq[:, half % 2, :], in_=T.ps[bank][:], func=AF.Silu),
                                 r=[("ps", bank)], w=[("sq", half % 2)])
                        else:
                            P.op("dve", lambda e: e.tensor_tensor(out=T.act[:, p % 2, cj, cols], in0=T.ps[bank][:],
                                                                  in1=T.sq[:, half % 2, :], op=ALU.mult),
                                 r=[("ps", bank), ("sq", half % 2)], w=[("act", p % 2, cj, half)])
        return Item("A", 2, load, compute)

    def mk_down(p):
        def load(slots):
            for jj in range(4):
                r0 = (p * 4 + jj) * 128
                P.dma("pool", T.wD[:, slots[jj], :], wd[r0:r0 + 128, :], "ldD%d" % slots[jj], w=[("D", slots[jj])])

        def compute(slots):
            ctr = 0
            for mo in range(KC):
                for half in range(nh):
                    bank = 4 + (ctr % 4)
                    ctr += 1
                    cols = slice(half * 512, (half + 1) * 512)
                    for jj in range(4):
                        P.op("pe", lambda e: e.matmul(T.ps[bank][:], T.wD[:, slots[jj], mo * 128:(mo + 1) * 128],
                                                      T.act[:, p % 2, jj, cols], start=(jj == 0), stop=(jj == 3)),
                             r=[("D", slots[jj]), ("act", p % 2, jj, half)], w=[("ps", bank)], signal=(jj == 3))
                    P.op("dve", lambda e: e.scalar_tensor_tensor(out=T.xT[:, mo, cols], in0=T.ps[bank][:],
                                                                 scalar=T.hgate[:, j, mo:mo + 1], in1=T.xT[:, mo, cols],
                                                                 op0=ALU.mult, op1=ALU.add),
                         r=[("ps", bank), ("hgate", j), ("x", mo, half)], w=[("x", mo, half)])
        return Item("D", 4, load, compute)

    items.append(Item(None, 0, None, lambda s: t_norm(P, T, j)))
    order = []
    for p in range(NP):
        order.append(("g", p))
        if p >= 1:
            order.append(("d", p - 1))
    order.append(("d", NP - 1))
    for kind, p in order:
        if kind == "g":
            items.append(mk_gu(p, 0))
            items.append(mk_gu(p, 1))
        else:
            items.append(mk_down(p))
    return items


PROJ_SLABS = ([("dq", "fm", "A", 0 + 2 * i) for i in range(4)] + [("dk", "fm", "A", 8 + 2 * i) for i in range(4)] +
              [("dv", "tm", None, 256 * i) for i in range(4)] + [("nq", "fm", "B", 16 + 2 * i) for i in range(4)] +
              [("kc", "fm", None, 24), ("vc", "fm", None, 30), ("ks", "fm", "B", 26), ("vs", "tm", None, 1024),
               ("kw", "fm", "B", 28), ("vw", "tm", None, 1280), ("gl", "gl", None, 0)])
NQK = 32
NV = 1536


def t_alloc_proj(P, T):
    nt = T.nt
    T.rope = P.sb("rope", [128, 4, nt], F32)
    T.Rm = P.sb("Rm", [128, 2, 128], BF16)
    T.xb = P.sb("xb", [128, 2, 512], BF16)
    T.t2 = P.sb("t2", [128, 2, 512], F32)
    T.ost = P.sb("ost", [128, 4, 512], BF16)
    T.vst = P.sb("vst", [128, 2, 256], BF16)
    T.cnt = dict(ost=0, vst=0, gst=0, xb=0, pa=0, pb=0)


def t_load_proj_consts(P, T, rope_d, rm_d):
    P.dma_batch("sp", "ld_rope", [(T.rope[:, i, :], rope_d[i], [], [("rope", i)]) for i in range(4)])
    P.dma_batch("sp", "ld_rm", [(T.Rm[:, i, :], rm_d[i], [], [("Rm", i)]) for i in range(2)])


def proj_items(P, T, w_in, qk_d, vv_d, gt_d):
    items = [Item(None, 0, None, lambda s: t_norm(P, T, 1))]
    nh = T.nh

    def mk(sidx):
        name, lay, rt, base = PROJ_SLABS[sidx]
        c0 = 256 * sidx
        ncols = min(256, INC - c0)

        def load(slots):
            t_slabA_load(P, T, slots[0], w_in[:, c0:c0 + ncols])

        def compute(slots):
            sl = slots[0]
            if lay == "tm":
                for t in range(T.nt // 128):
                    bank = T.cnt["pa"] % 4
                    T.cnt["pa"] += 1
                    half = t // 4
                    for kc in range(KC):
                        P.op("pe", lambda e: e.matmul(T.ps[bank][:, 0:256], T.hT[:, kc, t * 128:(t + 1) * 128],
                                                      T.wA[:, sl, kc, 0:256], start=(kc == 0), stop=(kc == KC - 1)),
                             r=[("A", sl), ("h", kc, half)], w=[("ps", bank)], signal=(kc == KC - 1))
                    vs = T.cnt["vst"] % 2
                    T.cnt["vst"] += 1
                    P.op("act", lambda e: e.activation(out=T.vst[:, vs, :], in_=T.ps[bank][:, 0:256], func=AF.Copy),
                         r=[("ps", bank)], w=[("vst", vs)])
                    P.dma("sp", vv_d[t * 128:(t + 1) * 128, base:base + 256], T.vst[:, vs, :], "st_v%d" % vs,
                          r=[("vst", vs)])
                return
            if lay == "gl":
                for half in range(nh):
                    bank = T.cnt["pa"] % 4
                    T.cnt["pa"] += 1
                    cols = slice(half * 512, (half + 1) * 512)
                    for kc in range(KC):
                        P.op("pe", lambda e: e.matmul(T.ps[bank][:], T.wA[:, sl, kc, 0:128], T.hT[:, kc, cols],
                                                      start=(kc == 0), stop=(kc == KC - 1)),
                             r=[("A", sl), ("h", kc, half)], w=[("ps", bank)], signal=(kc == KC - 1))
                    gs = T.cnt["gst"] % 2
                    T.cnt["gst"] += 1
                    P.op("act", lambda e: e.activation(out=T.t2[0:24, gs, :], in_=T.ps[bank][0:24, :], func=AF.Sigmoid),
                         r=[("ps", bank)], w=[("t2", gs)])
                    P.dma("sp", gt_d[:, cols], T.t2[0:24, gs, :], "st_g%d" % gs, r=[("t2", gs)])
                return
            for fc in range(2):
                for half in range(nh):
                    bank = T.cnt["pa"] % 4
                    T.cnt["pa"] += 1
                    cols = slice(half * 512, (half + 1) * 512)
                    for kc in range(KC):
                        P.op("pe", lambda e: e.matmul(T.ps[bank][:], T.wA[:, sl, kc, fc * 128:(fc + 1) * 128],
                                                      T.hT[:, kc, cols], start=(kc == 0), stop=(kc == KC - 1)),
                             r=[("A", sl), ("h", kc, half)], w=[("ps", bank)], signal=(kc == KC - 1))
                    os_ = T.cnt["ost"] % 4
                    T.cnt["ost"] += 1
                    if rt is None:
                        P.op("act", lambda e: e.activation(out=T.ost[:, os_, :], in_=T.ps[bank][:], func=AF.Copy),
                             r=[("ps", bank)], w=[("ost", os_)])
                    else:
                        ri = 0 if rt == "A" else 1
                        xs = T.cnt["xb"] % 2
                        T.cnt["xb"] += 1
                        b2 = 4 + (T.cnt["pb"] % 4)
                        T.cnt["pb"] += 1
                        P.op("act", lambda e: e.activation(out=T.xb[:, xs, :], in_=T.ps[bank][:], func=AF.Copy),
                             r=[("ps", bank)], w=[("xb", xs)])
                        import os
                        step = int(os.environ.get("ROPE_STEP", "4"))
                        if step >= 2:
                            P.op("pe", lambda e: e.matmul(T.ps[b2][:], T.Rm[:, ri, :], T.xb[:, xs, :], start=True, stop=True),
                                 r=[("xb", xs), ("Rm", ri)], w=[("ps", b2)])
                        if step >= 3:
                            P.op("dve", lambda e: e.tensor_tensor(out=T.tmp[:, xs, :], in0=T.ps[bank][:],
                                                                  in1=T.rope[:, 2 * ri, cols], op=ALU.mult),
                                 r=[("ps", bank), ("rope", 2 * ri), ("xb", xs)], w=[("tmp", xs)])
                            P.op("dve", lambda e: e.tensor_tensor(out=T.t2[:, xs, :], in0=T.ps[b2][:],
                                                                  in1=T.rope[:, 2 * ri + 1, cols], op=ALU.mult),
                                 r=[("ps", b2), ("rope", 2 * ri + 1)], w=[("t2", xs)])
                        if step >= 4:
                            P.op("dve", lambda e: e.tensor_tensor(out=T.ost[:, os_, :], in0=T.tmp[:, xs, :],
                                                                  in1=T.t2[:, xs, :], op=ALU.add),
                                 r=[("tmp", xs), ("t2", xs)], w=[("ost", os_)])
                        else:
                            P.op("act", lambda e: e.activation(out=T.ost[:, os_, :], in_=T.xb[:, xs, :], func=AF.Copy),
                                 r=[("xb", xs)], w=[("ost", os_)])
                    P.dma("sp", qk_d[(base + fc) * 128:(base + fc + 1) * 128, cols], T.ost[:, os_, :], "st_o%d" % os_, r=[("ost", os_)])
        return Item("A", 1, load, compute)

    import os
    kinds = os.environ.get("PROJ_KINDS", "fmrope,fmplain,tm,gl").split(",")
    for sidx in range(len(PROJ_SLABS)):
        name, lay, rt, base = PROJ_SLABS[sidx]
        k = "tm" if lay == "tm" else ("gl" if lay == "gl" else ("fmrope" if rt else "fmplain"))
        if k in kinds:
            items.append(mk(sidx))
    return items


def wo_items(P, T, w_o, oT_d):
    items = []

    def ld(s):
        P.dma_batch("sp", "ld_o", [(T.hT[:, c, :], oT_d[c * 128:(c + 1) * 128, :], [],
                                    [("h", c, half) for half in range(T.nh)]) for c in range(KC)])
    items.append(Item(None, 0, None, ld))

    def mk(si):
        def load(slots):
            t_slabA_load(P, T, slots[0], w_o[:, si * 256:(si + 1) * 256])

        def compute(slots):
            sl = slots[0]
            for fc in range(2):
                mo = si * 2 + fc
                for half in range(T.nh):
                    bank = 4 + (T.cnt["pb"] % 4)
                    T.cnt["pb"] += 1
                    cols = slice(half * 512, (half + 1) * 512)
                    for kc in range(KC):
                        P.op("pe", lambda e: e.matmul(T.ps[bank][:], T.wA[:, sl, kc, fc * 128:(fc + 1) * 128],
                                                      T.hT[:, kc, cols], start=(kc == 0), stop=(kc == KC - 1)),
                             r=[("A", sl), ("h", kc, half)], w=[("ps", bank)], signal=(kc == KC - 1))
                    P.op("dve", lambda e: e.scalar_tensor_tensor(out=T.xT[:, mo, cols], in0=T.ps[bank][:],
                                                                 scalar=T.hgate[:, 1, mo:mo + 1], in1=T.xT[:, mo, cols],
                                                                 op0=ALU.mult, op1=ALU.add),
                         r=[("ps", bank), ("hgate", 1), ("x", mo, half)], w=[("x", mo, half)])
        return Item("A", 1, load, compute)
    for si in range(8):
        items.append(mk(si))
    return items


def build_T(phases, nt=NT):
    nc = bass.Bass("TRN2", target_bir_lowering=False)
    stack = ExitStack()
    with stack:
        P = Prog(nc, stack)
        T = t_alloc(P, nt)
        T.cnt = dict(ost=0, vst=0, gst=0, xb=0, pa=0, pb=0)
        xin = nc.dram_tensor("xT_in", [D, nt], F32, kind="ExternalInput").ap()
        P.dma_batch("sp", "ld_x", [(T.xT[:, c, :], xin[c * 128:(c + 1) * 128, :], [],
                                    [("x", c, h) for h in range(T.nh)]) for c in range(KC)])
        items = []
        out_sems = []
        lay = 0
        if "wo" in phases or "ffn_b" in phases:
            modB = nc.dram_tensor("mod_b", [128, 9 * KC], F32, kind="ExternalInput").ap()
            ngB = nc.dram_tensor("ng_b", [128, 3 * KC], F32, kind="ExternalInput").ap()
            items.append(Item(None, 0, None, lambda s: t_load_mod(P, T, modB, ngB)))
        if "wo" in phases:
            w_o = nc.dram_tensor("w_o", [D, D], F32, kind="ExternalInput").ap()
            oT = nc.dram_tensor("oT", [D, nt], BF16, kind="ExternalInput").ap()
            items += wo_items(P, T, w_o, oT)
        if "ffn_b" in phases:
            wgu_b = nc.dram_tensor("wgu_b", [D, 2 * DFF], F32, kind="ExternalInput").ap()
            wd_b = nc.dram_tensor("wd_b", [DFF, D], F32, kind="ExternalInput").ap()
            items += ffn_items(P, T, wgu_b, wd_b, 2)
        if "ffn_a" in phases or "proj" in phases:
            modA = nc.dram_tensor("mod_a", [128, 9 * KC], F32, kind="ExternalInput").ap()
            ngA = nc.dram_tensor("ng_a", [128, 3 * KC], F32, kind="ExternalInput").ap()
            items.append(Item(None, 0, None, lambda s: t_load_mod(P, T, modA, ngA)))
        if "ffn_a" in phases:
            wgu_a = nc.dram_tensor("wgu_a", [D, 2 * DFF], F32, kind="ExternalInput").ap()
            wd_a = nc.dram_tensor("wd_a", [DFF, D], F32, kind="ExternalInput").ap()
            items += ffn_items(P, T, wgu_a, wd_a, 0)
        if "proj" in phases:
            t_alloc_proj(P, T)
            w_in = nc.dram_tensor("w_in", [D, INC], F32, kind="ExternalInput").ap()
            rope_d = nc.dram_tensor("rope_d", [4, 128, nt], F32, kind="ExternalInput").ap()
            rm_d = nc.dram_tensor("rm_d", [2, 128, 128], BF16, kind="ExternalInput").ap()
            qk_d = nc.dram_tensor("qk", [NQK * 128, nt], BF16, kind="ExternalOutput").ap()
            vv_d = nc.dram_tensor("vv", [nt, NV], BF16, kind="ExternalOutput").ap()
            gt_d = nc.dram_tensor("gt", [24, nt], F32, kind="ExternalOutput").ap()
            t_load_proj_consts(P, T, rope_d, rm_d)
            items += proj_items(P, T, w_in, qk_d, vv_d, gt_d)
        if "final" in phases:
            fg_d = nc.dram_tensor("fg_d", [128, KC], F32, kind="ExternalInput").ap()
            yT = nc.dram_tensor("yT", [D, nt], F32, kind="ExternalOutput").ap()
            T.fg = P.sb("fg", [128, KC], F32)
            P.dma("sp", T.fg[:], fg_d, "ld_fg", w=[("fg",)])
            items.append(Item(None, 0, None, lambda s: t_norm(P, T, 0, out_kind="final", gvec=T.fg, outbuf=yT)))
        if "xout" in phases:
            xout = nc.dram_tensor("xT_out", [D, nt], F32, kind="ExternalOutput").ap()

            def xo(s):
                P.dma_batch("sp", "st_x", [(xout[c * 128:(c + 1) * 128, :], T.xT[:, c, :],
                                            [("x", c, h) for h in range(T.nh)], []) for c in range(KC)])
            items.append(Item(None, 0, None, xo))
        run_stream(T.rings, items)
        P.finish("sp", [s for s in P.semobj if s.startswith("st_")])
        print("T kernel", phases, "ops", P.nops)
    return nc


def rope_tables(pos):
    nt = len(pos)
    out = np.zeros((4, 128, nt), np.float64)
    out[0] = 1.0
    out[2] = 1.0
    p = pos.astype(np.float64)
    for ti, (blk, half) in enumerate(((64, 8), (128, 16))):
        inv = np.exp(-math.log(ROPE_THETA) * np.arange(half, dtype=np.float32) / half).astype(np.float32)
        ang = (pos.astype(np.float32)[None, :] * inv[:, None]).astype(np.float32).astype(np.float64)
        for m in range(128):
            j = m % blk
            if j < half:
                out[2 * ti, m] = np.cos(ang[j])
                out[2 * ti + 1, m] = -np.sin(ang[j])
            elif j < 2 * half:
                out[2 * ti, m] = np.cos(ang[j - half])
                out[2 * ti + 1, m] = np.sin(ang[j - half])
    return out.astype(np.float32)


def rope_perm():
    rm = np.zeros((2, 128, 128), np.float32)
    for ti, (blk, half) in enumerate(((64, 8), (128, 16))):
        for m in range(128):
            j = m % blk
            if j < half:
                rm[ti, m + half, m] = 1.0
            elif j < 2 * half:
                rm[ti, m - half, m] = 1.0
    return rm.astype(NPBF)


def fm(v):
    v = np.asarray(v)
    lead = v.shape[:-1]
    n = v.shape[-1] // 128
    v = v.reshape(lead + (n, 128))
    return np.ascontiguousarray(np.moveaxis(v, -1, 0))


NSA_SCALE = 128 ** -0.5
DIFF_SCALE = 64 ** -0.5
NEGB = -30000.0
NCMP = 255
AX = mybir.AxisListType


def build_A(nqb=8):
    nc = bass.Bass("TRN2", target_bir_lowering=False)
    stack = ExitStack()
    SQ = nqb * 512
    with stack:
        P = Prog(nc, stack)

        def din(name, shape, dt):
            return nc.dram_tensor(name, shape, dt, kind="ExternalInput").ap()
        nq4_d = din("nq4", [512, S], BF16)
        kc_d = din("kc", [128, S], BF16)
        vc_d = din("vc", [128, S], BF16)
        ks_d = din("ks", [128, S], BF16)
        kw_d = din("kw", [128, S], BF16)
        vs_d = din("vs", [S, 128], BF16)
        vw_d = din("vw", [S, 128], BF16)
        dq_d = din("dq", [256, S], BF16)
        dk_d = din("dk", [256, S], BF16)
        dv_d = din("dv", [S, 256], BF16)
        g6_d = din("g6", [6, S], F32)
        w1_d = din("w1", [8192, 256], F32)
        w2_d = din("w2", [512, 128], F32)
        peT_d = din("peT", [128, 64], F32)
        lam_d = din("lam", [256], F32)
        subln_d = din("subln", [128, 1], F32)
        lamc_d = din("lamc", [128, 2], F32)
        cmask_d = din("cmask", [256, S], BF16)
        ov_d = din("ov", [256, 64], F32)
        btile_d = din("btile", [128, 32 * 64], F32)
        E_d = din("Emat", [128, 32 * 128], BF16)
        wm_d = din("wm", [128, 8 * 512], BF16)
        selG_d = din("selG", [32, 6 * 128], F32)
        ident_d = din("ident", [128, 128], F32)
        oT_d = nc.dram_tensor("oT", [512, S], BF16, kind="ExternalOutput").ap()

        arena = P.sb("arena", [128, 40960], BF16)
        nq4 = arena[:, 0:16384].rearrange("p (a b) -> p a b", a=4)
        kcT = arena[:, 16384:20480]
        vcT = arena[:, 20480:24576]
        ksT = arena[:, 24576:28672]
        kwT = arena[:, 28672:32768]
        vs = arena[:, 32768:36864].rearrange("p (a b) -> p a b", a=32)
        vw = arena[:, 36864:40960].rearrange("p (a b) -> p a b", a=32)
        dqz = arena[:, 0:16384].rearrange("p (h c b) -> p h c b", h=2, c=2)
        dkT = arena[:, 16384:24576].rearrange("p (a b) -> p a b", a=2)
        dv = arena[:, 24576:32768].rearrange("p (a b) -> p a b", a=32)

        oacc = P.sb("oacc", [128, 2, 2, 512], F32)
        biasT = P.sb("biasT", [128, 2, 512], BF16)
        cm_sb = P.sb("cm_sb", [128, 2, 2, 512], BF16)
        bt_sb = P.sb("bt_sb", [128, 2, 4, 64], F32)
        g6_sb = P.sb("g6_sb", [32, 2, 512], F32)
        wm = P.sb("wm_sb", [128, 8, 512], BF16)
        Em = P.sb("Em", [128, 32, 128], BF16)
        ov = P.sb("ov_sb", [128, 2, 64], F32)
        selG = P.sb("selG_sb", [32, 6, 128], F32)
        identf = P.sb("identf", [128, 128], F32)
        onesb = P.sb("onesb", [128, 128], BF16)
        onesf = P.sb("onesf", [128, 128], F32)
        epsb = P.sb("epsb", [128, 1], F32)
        w1 = P.sb("w1_sb", [128, 32, 256], BF16)
        w2 = P.sb("w2_sb", [128, 2, 2, 128], BF16)
        peT = P.sb("peT_sb", [128, 2, 32], BF16)
        hidT = P.sb("hidT", [128, 2, 2, 256], BF16)
        cbias = P.sb("cbias", [128, 2, 2], F32)
        kcmpT = P.sb("kcmpT", [128, 256], BF16)
        vcmp = P.sb("vcmp", [128, 2, 128], BF16)
        pT = P.sb("pT", [128, 4, 512], BF16)
        em = P.sb("em", [128, 2, 512], F32)
        pf = P.sb("pf", [128, 2, 512], F32)
        pb = P.sb("pb", [128, 2, 512], BF16)
        psumT = P.sb("psumT", [128, 2, 512], F32)
        rden = P.sb("rden", [128, 2, 512], F32)
        gsb = P.sb("gsb", [128, 2, 512], F32)
        tmpf = P.sb("tmpf", [128, 2, 512], F32)
        score = P.sb("score", [128, 64], F32)
        work = P.sb("work", [128, 64], F32)
        m8a = P.sb("m8a", [128, 8], F32)
        m8b = P.sb("m8b", [128, 8], F32)
        bval = P.sb("bval", [128, 128], F32)
        obf = P.sb("obf", [128, 2, 512], BF16)
        lamt = P.sb("lamt", [128, 256], F32)
        lprod = P.sb("lprod", [128, 2, 64], F32)
        lsc = P.sb("lsc", [128, 8], F32)
        subln = P.sb("subln_sb", [128, 1], F32)
        lamc = P.sb("lamc_sb", [128, 2], F32)
        ps = [P.psum("ps%d" % i, [128, 512], F32) for i in range(8)]
        cnt = dict(pT=0, ep=0, ob=0, st=0)

        def mm(out, lhsT, rhs, start, stop, r, w, signal=True):
            P.op("pe", lambda e: e.matmul(out, lhsT, rhs, start=start, stop=stop), r=r, w=w, signal=signal)

        P.op("dve", lambda e: e.memset(onesb[:], 1.0), w=[("onesb",)])
        P.op("dve", lambda e: e.memset(onesf[:], 1.0), w=[("onesf",)])
        P.op("dve", lambda e: e.memset(epsb[:], EPS), w=[("epsb",)])
        P.op("dve", lambda e: e.memset(hidT[:], 0.0), w=[("hidT",)])
        P.op("dve", lambda e: e.memset(bval[:], 0.0), w=[("bval",)])
        P.op("dve", lambda e: e.memset(g6_sb[:], 0.0), w=[("g6", 0), ("g6", 1)])
        P.dma("sp", wm[:].rearrange("p a b -> p (a b)"), wm_d, "ld_wm", w=[("wm",)])
        P.dma("sp", Em[:].rearrange("p a b -> p (a b)"), E_d, "ld_E", w=[("Em",)])
        P.dma("sp", ov[:], ov_d.rearrange("(c p) n -> p c n", p=128), "ld_ov", w=[("ov",)])
        P.dma("sp", selG[:].rearrange("p a b -> p (a b)"), selG_d, "ld_selG", w=[("selG",)])
        P.dma("sp", identf[:], ident_d, "ld_ident", w=[("identf",)])
        P.dma("sp", subln[:], subln_d, "ld_subln", w=[("subln",)])
        P.dma("sp", lamc[:], lamc_d, "ld_lamc", w=[("lamc",)])
        P.dma("sp", lamt[:], lam_d.partition_broadcast(128), "ld_lam", w=[("lamt",)])
        P.dma_batch("sp", "ld_nsa", [
            (nq4[:, j, :], nq4_d[j * 128:(j + 1) * 128, :], [], [("nq4", j)]) for j in range(4)] + [
            (kcT, kc_d, [], [("kcT",)]), (vcT, vc_d, [], [("vcT",)]), (ksT, ks_d, [], [("ksT",)]),
            (kwT, kw_d, [], [("kwT",)]),
            (vs, vs_d.rearrange("(t p) n -> p t n", p=128), [], [("vs",)]),
            (vw, vw_d.rearrange("(t p) n -> p t n", p=128), [], [("vw",)])])
        P.dma("pool", w2[:].rearrange("p j h n -> p (j h) n"), w2_d.rearrange("(a p) n -> p a n", p=128), "ld_w2",
              w=[("w2",)])
        P.dma("pool", peT[:].rearrange("p j l -> p (j l)"), peT_d, "ld_pe", w=[("peT",)])

        for i in range(2):
            P.op("dve", lambda e: e.tensor_tensor(out=lprod[:, i, :], in0=lamt[:, 128 * i:128 * i + 64],
                                                  in1=lamt[:, 128 * i + 64:128 * i + 128], op=ALU.mult),
                 r=[("lamt",)], w=[("lprod", i)])
            P.op("dve", lambda e: e.reduce_sum(out=lsc[:, i:i + 1], in_=lprod[:, i, :], axis=AX.X),
                 r=[("lprod", i)], w=[("lsc", i)])
            P.op("act", lambda e: e.activation(out=lsc[:, 2 + i:3 + i], in_=lsc[:, i:i + 1], func=AF.Exp),
                 r=[("lsc", i)], w=[("lsc", 2 + i)])
        P.op("dve", lambda e: e.tensor_tensor(out=lsc[:, 4:5], in0=lsc[:, 2:3], in1=lsc[:, 3:4], op=ALU.subtract),
             r=[("lsc", 2), ("lsc", 3)], w=[("lsc", 4)])
        P.op("dve", lambda e: e.tensor_tensor(out=lsc[:, 4:5], in0=lsc[:, 4:5], in1=lamc[:, 0:1], op=ALU.add),
             r=[("lsc", 4), ("lamc",)], w=[("lsc", 4)])
        P.op("dve", lambda e: e.tensor_scalar(out=lsc[:, 5:6], in0=lsc[:, 4:5], scalar1=-1.0, scalar2=None, op0=ALU.mult),
             r=[("lsc", 4)], w=[("lsc", 5)])
        P.op("dve", lambda e: e.tensor_tensor(out=lsc[:, 6:7], in0=subln[:], in1=lamc[:, 1:2], op=ALU.mult),
             r=[("subln",), ("lamc",)], w=[("lsc", 6)])

        for j in range(2):
            XT, xres = (kcT, ("kcT",)) if j == 0 else (vcT, ("vcT",))
            P.dma("pool", w1[:], w1_d[j * 4096:(j + 1) * 4096, :].rearrange("(l p) n -> p l n", p=128), "ld_w1",
                  w=[("w1",)])
            for hc in range(2):
                for l in range(32):
                    mm(ps[2][:, hc:hc + 1], w1[:, l, hc * 128:(hc + 1) * 128], peT[:, j, l:l + 1], l == 0, l == 31,
                       r=[("w1",), ("peT",)], w=[("ps", 2)], signal=(l == 31))
            P.op("act", lambda e: e.activation(out=cbias[:, j, :], in_=ps[2][:, 0:2], func=AF.Copy),
                 r=[("ps", 2)], w=[("cbias", j)])
            for hc in range(2):
                for l in range(32):
                    mm(ps[hc][:, 0:NCMP], w1[:, l, hc * 128:(hc + 1) * 128], XT[:, l:l + 16 * (NCMP - 1) + 1:16], l == 0, l == 31,
                       r=[("w1",), xres], w=[("ps", hc)], signal=(l == 31))
                P.op("act", lambda e: e.activation(out=hidT[:, j, hc, 0:NCMP], in_=ps[hc][:, 0:NCMP], func=AF.Silu,
                                                   bias=cbias[:, j, hc:hc + 1]),
                     r=[("ps", hc), ("cbias", j)], w=[("hidT",)])
            if j == 0:
                for hc in range(2):
                    mm(ps[3][:, 0:256], w2[:, 0, hc, :], hidT[:, 0, hc, :], hc == 0, hc == 1,
                       r=[("w2",), ("hidT",)], w=[("ps", 3)], signal=(hc == 1))
                P.op("act", lambda e: e.activation(out=kcmpT[:], in_=ps[3][:, 0:256], func=AF.Copy),
                     r=[("ps", 3)], w=[("kcmpT",)])
            else:
                for cc in range(2):
                    for hc in range(2):
                        mm(ps[3][:, cc * 128:(cc + 1) * 128], hidT[:, 1, hc, cc * 128:(cc + 1) * 128], w2[:, 1, hc, :],
                           hc == 0, hc == 1, r=[("w2",), ("hidT",)], w=[("ps", 3)], signal=(hc == 1))
                P.op("act", lambda e: e.activation(out=vcmp[:].rearrange("p a b -> p (a b)"), in_=ps[3][:, 0:256],
                                                   func=AF.Copy),
                     r=[("ps", 3)], w=[("vcmp",)])

        def branch_epilogue(hi, par, accO, accD, gidx, first, gbank):
            mm(ps[gbank][:], selG[:, hi * 3 + gidx, :], g6_sb[:, par, :], True, True,
               r=[("selG",), ("g6", par)], w=[("ps", gbank)])
            es = cnt["ep"] % 2
            cnt["ep"] += 1
            P.op("act", lambda e: e.activation(out=gsb[:, es, :], in_=ps[gbank][:], func=AF.Copy),
                 r=[("ps", gbank)], w=[("gsb", es)])
            if accD is not None:
                P.op("dve", lambda e: e.reciprocal(out=rden[:, es, :], in_=ps[accD][:]),
                     r=[("ps", accD)], w=[("rden", es)])
                P.op("dve", lambda e: e.tensor_tensor(out=gsb[:, es, :], in0=gsb[:, es, :], in1=rden[:, es, :], op=ALU.mult),
                     r=[("gsb", es), ("rden", es)], w=[("gsb", es)])
            if first:
                P.op("dve", lambda e: e.tensor_tensor(out=oacc[:, par, hi, :], in0=ps[accO][:], in1=gsb[:, es, :], op=ALU.mult),
                     r=[("ps", accO), ("gsb", es)], w=[("oacc", par, hi)])
            else:
                P.op("dve", lambda e: e.tensor_tensor(out=tmpf[:, es, :], in0=ps[accO][:], in1=gsb[:, es, :], op=ALU.mult),
                     r=[("ps", accO), ("gsb", es)], w=[("tmpf", es)])
                P.op("dve", lambda e: e.tensor_tensor(out=oacc[:, par, hi, :], in0=oacc[:, par, hi, :], in1=tmpf[:, es, :],
                                                      op=ALU.add),
                     r=[("oacc", par, hi), ("tmpf", es)], w=[("oacc", par, hi)])

        def attn_tiles(kts, qT_ap, kT, kres, vtile, vres, masks, scale, accO, accD, extra_bias=None):
            n = len(kts)
            for idx, kt in enumerate(kts):
                sbk = cnt["st"] % 2
                cnt["st"] += 1
                ksl = slice(kt * 128, (kt + 1) * 128)
                if extra_bias is None:
                    mm(ps[sbk][:], kT[:, ksl], qT_ap, True, True, r=[kres, ("q",)], w=[("ps", sbk)])
                else:
                    mm(ps[sbk][:], kT[:, ksl], qT_ap, True, False, r=[kres, ("q",)], w=[("ps", sbk)], signal=False)
                    mm(ps[sbk][:], Em[:, kt, :], extra_bias, False, True, r=[("Em",), ("biasT",)], w=[("ps", sbk)])
                sl = cnt["pT"] % 4
                cnt["pT"] += 1
                P.op("act", lambda e: e.activation(out=pT[:, sl, :], in_=ps[sbk][:], func=AF.Exp, scale=scale),
                     r=[("ps", sbk)], w=[("pT", sl)])
                mk = masks(kt)
                if mk is not None:
                    P.op("dve", lambda e: e.tensor_tensor(out=pT[:, sl, :], in0=pT[:, sl, :], in1=wm[:, mk, :], op=ALU.mult),
                         r=[("pT", sl), ("wm",)], w=[("pT", sl)])
                mm(ps[accO][:], vtile(kt), pT[:, sl, :], idx == 0, idx == n - 1, r=[vres, ("pT", sl)], w=[("ps", accO)],
                   signal=(idx == n - 1))
                mm(ps[accD][:], onesb[:], pT[:, sl, :], idx == 0, idx == n - 1, r=[("onesb",), ("pT", sl)],
                   w=[("ps", accD)])

        for qb in range(nqb):
            par = qb % 2
            qc = slice(qb * 512, (qb + 1) * 512)
            ncc = 1 if qb <= 3 else 2
            P.dma("sp", cm_sb[:, par, :, :], cmask_d[:, qc].rearrange("(c p) n -> p c n", p=128), "ld_cm%d" % par,
                  w=[("cm", par)])
            P.dma("sp", bt_sb[:, par, :, :].rearrange("p a b -> p (a b)"), btile_d[:, qb * 256:(qb + 1) * 256],
                  "ld_bt%d" % par, w=[("bt", par)])
            P.dma("sp", g6_sb[0:6, par, :], g6_d[:, qc], "ld_g6%d" % par, w=[("g6", par)])
            for j in range(4):
                for cc in range(ncc):
                    mm(ps[cc][:], kcmpT[:, cc * 128:(cc + 1) * 128], nq4[:, j, qc], True, True,
                       r=[("kcmpT",), ("nq4", j)], w=[("ps", cc)])
                    P.op("act", lambda e: e.activation(out=em[:, cc, :], in_=ps[cc][:], func=AF.Exp, scale=NSA_SCALE),
                         r=[("ps", cc)], w=[("em", cc)])
                    P.op("dve", lambda e: e.tensor_tensor(out=em[:, cc, :], in0=em[:, cc, :], in1=cm_sb[:, par, cc, :],
                                                          op=ALU.mult),
                         r=[("em", cc), ("cm", par)], w=[("em", cc)])
                for cc in range(ncc):
                    mm(ps[2][:], onesf[:], em[:, cc, :], cc == 0, cc == ncc - 1, r=[("onesf",), ("em", cc)],
                       w=[("ps", 2)], signal=(cc == ncc - 1))
                P.op("dve", lambda e: e.tensor_scalar(out=rden[:, 0, :], in0=ps[2][:], scalar1=1e-30, scalar2=None, op0=ALU.max),
                     r=[("ps", 2)], w=[("rden", 0)])
                P.op("dve", lambda e: e.reciprocal(out=rden[:, 0, :], in_=rden[:, 0, :]), r=[("rden", 0)], w=[("rden", 0)])
                for cc in range(ncc):
                    dst = psumT if j == 0 else pf
                    dres = ("psumT", cc) if j == 0 else ("pf", cc)
                    P.op("dve", lambda e: e.tensor_tensor(out=dst[:, cc, :], in0=em[:, cc, :], in1=rden[:, 0, :], op=ALU.mult),
                         r=[("em", cc), ("rden", 0)], w=[dres])
                    if j < 2:
                        P.op("act", lambda e: e.activation(out=pb[:, cc, :], in_=dst[:, cc, :], func=AF.Copy),
                             r=[dres], w=[("pb", cc)])
                    if j > 0:
                        P.op("dve", lambda e: e.tensor_tensor(out=psumT[:, cc, :], in0=psumT[:, cc, :], in1=pf[:, cc, :],
                                                              op=ALU.add),
                             r=[("psumT", cc), ("pf", cc)], w=[("psumT", cc)])
                if j < 2:
                    for cc in range(ncc):
                        mm(ps[3][:], vcmp[:, cc, :], pb[:, cc, :], cc == 0, cc == ncc - 1, r=[("vcmp",), ("pb", cc)],
                           w=[("ps", 3)], signal=(cc == ncc - 1))
                    branch_epilogue(j, par, 3, None, 0, True, 6)
            for qt in range(4):
                for cc in range(ncc):
                    mm(ps[4][:, qt * 128:qt * 128 + 64], psumT[:, cc, qt * 128:(qt + 1) * 128], ov[:, cc, :], cc == 0,
                       cc == ncc - 1, r=[("psumT", cc), ("ov",)], w=[("ps", 4)], signal=(cc == ncc - 1))
                P.op("dve", lambda e: e.tensor_tensor(out=score[:], in0=ps[4][:, qt * 128:qt * 128 + 64],
                                                      in1=bt_sb[:, par, qt, :], op=ALU.add),
                     r=[("ps", 4), ("bt", par)], w=[("score",)])
                P.op("dve", lambda e: e.max(out=m8a[:], in_=score[:]), r=[("score",)], w=[("m8a",)])
                P.op("dve", lambda e: e.match_replace(out=work[:], in_to_replace=m8a[:], in_values=score[:], imm_value=-3.0e38),
                     r=[("score",), ("m8a",)], w=[("work",)])
                P.op("dve", lambda e: e.max(out=m8b[:], in_=work[:]), r=[("work",)], w=[("m8b",)])
                P.op("dve", lambda e: e.tensor_scalar(out=bval[:, 0:64], in0=score[:], scalar1=m8b[:, 7:8], scalar2=NEGB,
                                                      op0=ALU.is_lt, op1=ALU.mult),
                     r=[("score",), ("m8b",)], w=[("bval",)])
                mm(ps[5][:, qt * 128:(qt + 1) * 128], bval[:], identf[:], True, True, r=[("bval",), ("identf",)],
                   w=[("ps", 5)])
            P.op("act", lambda e: e.activation(out=biasT[:, par, :], in_=ps[5][:], func=AF.Copy),
                 r=[("ps", 5)], w=[("biasT",)])
            for hi in range(2):
                ao, ad = (2, 3) if hi == 0 else (4, 5)
                P.R[("q",)] = P.R.get(("nq4", hi), [None, {}])
                attn_tiles(list(range(4 * qb + 4)), nq4[:, hi, qc], ksT, ("ksT",), lambda kt: vs[:, kt, :], ("vs",),
                           lambda kt: (4 + kt - 4 * qb) if kt >= 4 * qb else None, NSA_SCALE, ao, ad,
                           extra_bias=biasT[:, par, :])
                branch_epilogue(hi, par, ao, ad, 1, False, 6 + hi)
            for hi in range(2):
                ao, ad = (2, 3) if hi == 0 else (4, 5)
                P.R[("q",)] = P.R.get(("nq4", hi), [None, {}])
                kts = [kt for kt in range(4 * qb - 4, 4 * qb + 4) if kt >= 0]
                attn_tiles(kts, nq4[:, hi, qc], kwT, ("kwT",), lambda kt: vw[:, kt, :], ("vw",),
                           lambda kt: kt - (4 * qb - 4), NSA_SCALE, ao, ad)
                branch_epilogue(hi, par, ao, ad, 2, False, 6 + hi)
                os_ = cnt["ob"] % 2
                cnt["ob"] += 1
                P.op("act", lambda e: e.activation(out=obf[:, os_, :], in_=oacc[:, par, hi, :], func=AF.Copy),
                     r=[("oacc", par, hi)], w=[("obf", os_)])
                P.dma("sp", oT_d[(2 + hi) * 128:(3 + hi) * 128, qc], obf[:, os_, :], "st_ob%d" % os_, r=[("obf", os_)])

        P.barrier("sp")
        P.barrier("dve")
        P.op("dve", lambda e: e.memset(arena[:, 0:16384], 0.0), w=[("dqz",)])
        lst = []
        for h in range(2):
            for c in range(2):
                lst.append((dqz[64 * c:64 * c + 64, h, c, :], dq_d[h * 128 + 64 * c:h * 128 + 64 * c + 64, :], [], [("dqz",)]))
            lst.append((dkT[:, h, :], dk_d[h * 128:(h + 1) * 128, :], [], [("dkT",)]))
        lst.append((dv, dv_d.rearrange("(t p) n -> p t n", p=128), [], [("dv",)]))
        P.dma_batch("sp", "ld_diff", lst)
        for h in range(2):
            for qb in range(nqb):
                qc = slice(qb * 512, (qb + 1) * 512)
                for c in range(2):
                    P.R[("q",)] = P.R.get(("dqz",), [None, {}])
                    attn_tiles(list(range(4 * qb + 4)), dqz[:, h, c, qc], dkT[:, h, :], ("dkT",),
                               lambda kt: dv[:, kt, h * 128:(h + 1) * 128], ("dv",),
                               lambda kt: (4 + kt - 4 * qb) if kt >= 4 * qb else None, DIFF_SCALE, 2 + 2 * c, 3 + 2 * c)
                for c in range(2):
                    P.op("dve", lambda e: e.reciprocal(out=rden[:, c, :], in_=ps[3 + 2 * c][:]),
                         r=[("ps", 3 + 2 * c)], w=[("rden", c)])
                    P.op("dve", lambda e: e.tensor_tensor(out=tmpf[:, c, :], in0=ps[2 + 2 * c][:], in1=rden[:, c, :], op=ALU.mult),
                         r=[("ps", 2 + 2 * c), ("rden", c)], w=[("tmpf", c)])
                P.op("dve", lambda e: e.scalar_tensor_tensor(out=tmpf[:, 0, :], in0=tmpf[:, 1, :], scalar=lsc[:, 5:6],
                                                             in1=tmpf[:, 0, :], op0=ALU.mult, op1=ALU.add),
                     r=[("tmpf", 0), ("tmpf", 1), ("lsc", 5)], w=[("tmpf", 0)])
                P.op("act", lambda e: e.activation(out=em[:, 0, :], in_=tmpf[:, 0, :], func=AF.Square),
                     r=[("tmpf", 0)], w=[("em", 0)])
                mm(ps[6][:], onesf[:], em[:, 0, :], True, True, r=[("onesf",), ("em", 0)], w=[("ps", 6)])
                P.op("act", lambda e: e.activation(out=em[:, 1, :], in_=ps[6][:], func=AF.Ln, scale=1.0 / 128, bias=epsb[:]),
                     r=[("ps", 6), ("epsb",)], w=[("em", 1)])
                P.op("act", lambda e: e.activation(out=em[:, 1, :], in_=em[:, 1, :], func=AF.Exp, scale=-0.5),
                     r=[("em", 1)], w=[("em", 1)])
                os_ = cnt["ob"] % 2
                cnt["ob"] += 1
                P.op("dve", lambda e: e.scalar_tensor_tensor(out=obf[:, os_, :], in0=tmpf[:, 0, :], scalar=lsc[:, 6:7],
                                                             in1=em[:, 1, :], op0=ALU.mult, op1=ALU.mult),
                     r=[("tmpf", 0), ("lsc", 6), ("em", 1)], w=[("obf", os_)])
                P.dma("sp", oT_d[h * 128:(h + 1) * 128, qc], obf[:, os_, :], "st_ob%d" % os_, r=[("obf", os_)])
        P.finish("sp", [s for s in P.semobj if s.startswith("st_")])
        print("A kernel ops", P.nops)
    return nc


def attn_consts():
    cm = np.zeros((256, S), np.float32)
    t = np.arange(S)
    for c in range(NCMP):
        cm[c] = (16 * c + 31 <= t)
    cs = np.arange(NCMP) * 16
    ss = np.arange(64) * 64
    ovm = np.clip(np.minimum(cs[:, None] + 32, ss[None, :] + 64) - np.maximum(cs[:, None], ss[None, :]), 0, None) / 32.0
    ov = np.zeros((256, 64), np.float32)
    ov[:NCMP] = ovm
    bt = np.zeros((128, 32, 64), np.float32)
    for tile in range(32):
        tt = tile * 128 + np.arange(128)
        m = np.arange(64)[None, :]
        valid = m * 64 <= tt[:, None]
        cur = (tt // 64)[:, None]
        forced = (m == 0) | (m == cur) | (m == cur - 1)
        bt[:, tile, :] = np.where(valid, np.where(forced, 1e6, 0.0), -1e30)
    E = np.zeros((128, 32, 128), np.float32)
    for kt in range(32):
        E[2 * kt, kt, 0:64] = 1.0
        E[2 * kt + 1, kt, 64:128] = 1.0
    wmk = np.zeros((128, 8, 512), np.float32)
    p = np.arange(128)[:, None]
    q = np.arange(512)[None, :]
    for r in range(8):
        k = (r - 4) * 128 + p
        wmk[:, r, :] = ((q - k >= 0) & (q - k < 512))
    selG = np.zeros((32, 6, 128), np.float32)
    for r in range(6):
        selG[r, r, :] = 1.0
    return dict(cmask=cm.astype(NPBF), ov=ov, btile=bt.reshape(128, 32 * 64), Emat=E.reshape(128, 32 * 128).astype(NPBF),
                wm=wmk.reshape(128, 8 * 512).astype(NPBF), selG=selG.reshape(32, 6 * 128),
                ident=np.eye(128, dtype=np.float32))


MCOLS = 9 * D // NCORE
MCH = MCOLS // 128


def build_M():
    nc = bass.Bass("TRN2", target_bir_lowering=False)
    stack = ExitStack()
    with stack:
        P = Prog(nc, stack)
        cT_d = nc.dram_tensor("cT", [128, KC * B], F32, kind="ExternalInput").ap()
        wada_d = nc.dram_tensor("wada", [DEPTH * D, MCOLS], F32, kind="ExternalInput").ap()
        bT_d = nc.dram_tensor("bT", [128, DEPTH * MCH], F32, kind="ExternalInput").ap()
        modp_d = nc.dram_tensor("modp", [128, DEPTH * MCH * B], F32, kind="ExternalOutput").ap()
        cT = P.sb("cT_sb", [128, KC, B], F32)
        cs = P.sb("cs_sb", [128, KC, B], BF16)
        bT = P.sb("bT_sb", [128, DEPTH * MCH], F32)
        wA = P.sb("wA", [128, 4, KC, 256], BF16)
        osb = P.sb("osb", [128, DEPTH * MCH, B], F32)
        ps = [P.psum("ps%d" % i, [128, 512], F32) for i in range(2)]
        rings = {"A": dict(size=4, next=0, last=[None] * 4)}
        P.dma("sp", cT[:].rearrange("p a b -> p (a b)"), cT_d, "ld_c", w=[("cT",)])
        P.dma("sp", bT[:], bT_d, "ld_b", w=[("bT",)])
        P.op("act", lambda e: e.activation(out=cs[:], in_=cT[:], func=AF.Silu), r=[("cT",)], w=[("cs",)])
        items = []
        cntr = dict(b=0)

        def mk(l, s):
            def load(slots):
                P.dma("pool", wA[:, slots[0], :, :],
                      wada_d[l * D:(l + 1) * D, s * 256:(s + 1) * 256].rearrange("(kc p) n -> p kc n", p=128),
                      "ldA%d" % slots[0], w=[("A", slots[0])])

            def compute(slots):
                sl = slots[0]
                for fc in range(2):
                    ch = l * MCH + s * 2 + fc
                    bank = cntr["b"] % 2
                    cntr["b"] += 1
                    for kc in range(KC):
                        P.op("pe", lambda e: e.matmul(ps[bank][:, 0:B], wA[:, sl, kc, fc * 128:(fc + 1) * 128], cs[:, kc, :],
                                                      start=(kc == 0), stop=(kc == KC - 1)),
                             r=[("A", sl), ("cs",)], w=[("ps", bank)], signal=(kc == KC - 1))
                    P.op("act", lambda e: e.activation(out=osb[:, ch, :], in_=ps[bank][:, 0:B], func=AF.Identity,
                                                       bias=bT[:, ch:ch + 1], scale=1.0),
                         r=[("ps", bank), ("bT",)], w=[("osb",)])
            return Item("A", 1, load, compute)
        for l in range(DEPTH):
            for s in range(MCH // 2):
                items.append(mk(l, s))
        run_stream(rings, items)
        P.dma("sp", modp_d, osb[:].rearrange("p a b -> p (a b)"), "st_mod", r=[("osb",)])
        P.finish("sp", ["st_mod"])
        print("M kernel ops", P.nops)
    return nc


_CACHE = {}


def _prog(key, fn):
    if key not in _CACHE:
        _CACHE[key] = fn()
    return _CACHE[key]


def _run(nc, in_maps):
    res = bass_utils.run_bass_kernel_spmd(nc, in_maps, core_ids=list(range(NCORE)))
    return res.results


def kernel(x, c, w_ada, b_ada, norm_g, ffn_w_gu, ffn_w_d, w_in, w_o, diff_lam, diff_subln, cmp_pe, cmp_w1, cmp_w2, final_g):
    f32 = np.float32
    x = np.asarray(x, f32)
    c = np.asarray(c, f32)
    ncM = _prog("M", build_M)
    cT = np.ascontiguousarray(c.reshape(B, KC, 128).transpose(2, 1, 0)).reshape(128, KC * B)
    in_maps = []
    for core in range(NCORE):
        cols = slice(core * MCOLS, (core + 1) * MCOLS)
        wa = np.ascontiguousarray(np.asarray(w_ada)[:, :, cols]).reshape(DEPTH * D, MCOLS)
        bt = np.ascontiguousarray(np.asarray(b_ada)[:, cols].reshape(DEPTH, MCH, 128).transpose(2, 0, 1)).reshape(128, DEPTH * MCH)
        in_maps.append({"cT": cT, "wada": wa, "bT": bt})
    resM = _run(ncM, in_maps)
    mod = np.zeros((B, DEPTH, 9 * D), f32)
    for core in range(NCORE):
        mp = resM[core]["modp"].reshape(128, DEPTH, MCH, B)
        mod[:, :, core * MCOLS:(core + 1) * MCOLS] = mp.transpose(3, 1, 2, 0).reshape(B, DEPTH, MCOLS)
    modT = [[np.ascontiguousarray(fm(mod[b, l].reshape(9, D))).reshape(128, 9 * KC) for l in range(DEPTH)] for b in range(B)]
    ngT = [np.ascontiguousarray(fm(np.asarray(norm_g[l], f32))).reshape(128, 3 * KC) for l in range(DEPTH)]
    rm = rope_perm()
    ropes = [rope_tables(np.arange(i * NT, (i + 1) * NT)) for i in range(4)]
    aconst = attn_consts()

    xT = [np.ascontiguousarray(x[core // 4, (core % 4) * NT:(core % 4 + 1) * NT].T) for core in range(NCORE)]
    oT = None
    for l in range(DEPTH + 1):
        phases = []
        if l > 0:
            phases += ["wo", "ffn_b"]
        if l < DEPTH:
            phases += ["ffn_a", "proj", "xout"]
        else:
            phases += ["final"]
        ncT = _prog("T" + ",".join(phases), lambda: build_T(phases))
        in_maps = []
        for core in range(NCORE):
            b, i = core // 4, core % 4
            m = {"xT_in": xT[core]}
            if l > 0:
                m["mod_b"] = modT[b][l - 1]
                m["ng_b"] = ngT[l - 1]
                m["w_o"] = np.asarray(w_o[l - 1], f32)
                m["oT"] = oT[core]
                m["wgu_b"] = np.asarray(ffn_w_gu[l - 1, 1], f32)
                m["wd_b"] = np.asarray(ffn_w_d[l - 1, 1], f32)
            if l < DEPTH:
                m["mod_a"] = modT[b][l]
                m["ng_a"] = ngT[l]
                m["wgu_a"] = np.asarray(ffn_w_gu[l, 0], f32)
                m["wd_a"] = np.asarray(ffn_w_d[l, 0], f32)
                m["w_in"] = np.asarray(w_in[l], f32)
                m["rope_d"] = ropes[i]
                m["rm_d"] = rm
            else:
                m["fg_d"] = np.ascontiguousarray(fm(np.asarray(final_g, f32)))
            in_maps.append(m)
        resT = _run(ncT, in_maps)
        if l == DEPTH:
            out = np.zeros((B, S, D), f32)
            for core in range(NCORE):
                b, i = core // 4, core % 4
                out[b, i * NT:(i + 1) * NT, :] = resT[core]["yT"].T
            return out
        xT = [resT[core]["xT_out"] for core in range(NCORE)]
        ncA = _prog("A", build_A)
        lam_init = 0.8 - 0.6 * math.exp(-0.3 * l)
        in_maps = []
        for core in range(NCORE):
            b, hg = core // 4, core % 4
            g = hg // 2
            qk = np.concatenate([resT[b * 4 + i]["qk"] for i in range(4)], axis=1)
            vv = np.concatenate([resT[b * 4 + i]["vv"] for i in range(4)], axis=0)
            gt = np.concatenate([resT[b * 4 + i]["gt"] for i in range(4)], axis=1)

            def ch(i0, n=1):
                return qk[i0 * 128:(i0 + n) * 128]
            own = [2 * hg, 2 * hg + 1]
            oth = [h for h in range(4 * g, 4 * g + 4) if h not in own]
            m = dict(aconst)
            m["nq4"] = np.concatenate([ch(16 + h) for h in own + oth], 0)
            m["kc"] = ch(24 + g)
            m["vc"] = ch(30 + g)
            m["ks"] = ch(26 + g)
            m["kw"] = ch(28 + g)
            m["vs"] = vv[:, 1024 + 128 * g:1024 + 128 * (g + 1)]
            m["vw"] = vv[:, 1280 + 128 * g:1280 + 128 * (g + 1)]
            m["dq"] = ch(2 * hg, 2)
            m["dk"] = ch(8 + 2 * hg, 2)
            m["dv"] = vv[:, 256 * hg:256 * (hg + 1)]
            m["g6"] = gt[6 * hg:6 * hg + 6]
            m["w1"] = np.asarray(cmp_w1[l], f32).reshape(8192, 256)
            m["w2"] = np.asarray(cmp_w2[l], f32).reshape(512, 128)
            pe = np.asarray(cmp_pe[l], f32)
            m["peT"] = np.concatenate([pe[0].T, pe[1].T], 1)
            m["lam"] = np.asarray(diff_lam[l], f32).reshape(256)
            m["subln"] = np.asarray(diff_subln[l], f32).reshape(128, 1)
            m["lamc"] = np.tile(np.array([[lam_init, 1.0 - lam_init]], f32), (128, 1))
            in_maps.append({k: np.ascontiguousarray(v) for k, v in m.items()})
        resA = _run(ncA, in_maps)
        oT = []
        for core in range(NCORE):
            b, i = core // 4, core % 4
            full = np.zeros((D, NT), dtype=NPBF)
            for hg in range(4):
                o = resA[b * 4 + hg]["oT"][:, i * NT:(i + 1) * NT]
                full[256 * hg:256 * (hg + 1)] = o[0:256]
                full[1024 + 256 * hg:1024 + 256 * (hg + 1)] = o[256:512]
            oT.append(full)
```

```python
import math
from contextlib import ExitStack
import numpy as np
import ml_dtypes
import concourse.bass as bass
import concourse.mybir as mybir
import concourse.bass_utils as bass_utils

F32 = mybir.dt.float32
BF16 = mybir.dt.bfloat16
AF = mybir.ActivationFunctionType
ALU = mybir.AluOpType
NPBF = ml_dtypes.bfloat16

D = 2048
B = 2
S = 4096
DEPTH = 4
DFF = 5632
NCORE = 8
NT = 1024
KC = D // 128
INC = 5656
EPS = 1e-6
ROPE_THETA = 500000.0


class Prog:
    def __init__(self, nc, stack):
        self.nc = nc
        self.stack = stack
        self.E = dict(pe=nc.tensor, act=nc.scalar, dve=nc.vector, pool=nc.gpsimd, sp=nc.sync)
        self.semobj = {}
        self.semval = {}
        for k in self.E:
            self.semobj["e_" + k] = stack.enter_context(nc.semaphore("e_" + k))
            self.semval["e_" + k] = 0
        self.seen = {k: {} for k in self.E}
        self.R = {}
        self.nops = 0

    def sb(self, name, shape, dt):
        return self.stack.enter_context(self.nc.sbuf_tensor(name, shape, dt))

    def psum(self, name, shape, dt):
        return self.stack.enter_context(self.nc.psum_tensor(name, shape, dt))

    def dsem(self, name):
        if name not in self.semobj:
            self.semobj[name] = self.stack.enter_context(self.nc.semaphore(name))
            self.semval[name] = 0
        return name

    def _wait(self, eng, dep, raw):
        name, val = dep
        if name == "e_" + eng:
            if eng == "pe" or not raw:
                return
        if self.seen[eng].get(name, 0) >= val:
            return
        self.E[eng].wait_ge(self.semobj[name], val)
        self.seen[eng][name] = val
        self.nops += 1

    def _hazards(self, eng, r, w):
        for res in r:
            st = self.R.get(res)
            if st and st[0]:
                self._wait(eng, st[0], True)
            if st and res[0] == "ps":
                for nm, v in st[1].items():
                    self._wait(eng, (nm, v), False)
        for res in w:
            st = self.R.get(res)
            if st:
                if st[0]:
                    self._wait(eng, st[0], False)
                for nm, v in st[1].items():
                    self._wait(eng, (nm, v), False)

    def _record(self, tok, r, w):
        for res in r:
            st = self.R.setdefault(res, [None, {}])
            if st[1].get(tok[0], 0) < tok[1]:
                st[1][tok[0]] = tok[1]
        for res in w:
            self.R[res] = [tok, {}]

    def op(self, eng, fn, r=(), w=(), signal=True):
        self._hazards(eng, r, w)
        ins = fn(self.E[eng])
        self.nops += 1
        name = "e_" + eng
        if signal:
            self.semval[name] += 1
            ins.then_inc(self.semobj[name], 1)
            tok = (name, self.semval[name])
        else:
            tok = (name, self.semval[name] + 1)
        self._record(tok, r, w)

    def dma(self, eng, out, in_, sem, r=(), w=()):
        self._hazards(eng, r, w)
        self.dsem(sem)
        self.E[eng].dma_start(out=out, in_=in_).then_inc(self.semobj[sem], 16)
        self.nops += 1
        self.semval[sem] += 16
        self._record((sem, self.semval[sem]), r, w)

    def dma_batch(self, eng, sem, lst):
        self.dsem(sem)
        for (out, in_, r, w) in lst:
            self._hazards(eng, r, w)
        for (out, in_, r, w) in lst:
            self.E[eng].dma_start(out=out, in_=in_).then_inc(self.semobj[sem], 16)
            self.nops += 1
            self.semval[sem] += 16
        for (out, in_, r, w) in lst:
            self._record((sem, self.semval[sem]), r, w)

    def barrier(self, eng):
        for name, val in self.semval.items():
            if val > 0 and name != "e_" + eng and self.seen[eng].get(name, 0) < val:
                self.E[eng].wait_ge(self.semobj[name], val)
                self.seen[eng][name] = val
                self.nops += 1

    def finish(self, eng, sems):
        for s in sems:
            if self.semval.get(s, 0) > 0:
                self.E[eng].wait_ge(self.semobj[s], self.semval[s])


class Item:
    def __init__(self, cls, nslots, load, compute):
        self.cls, self.nslots, self.load, self.compute = cls, nslots, load, compute


def run_stream(rings, items):
    n = len(items)
    slot_of, preds = [], []
    for i, it in enumerate(items):
        pr = set()
        s = []
        if it.cls is not None:
            ring = rings[it.cls]
            for k in range(it.nslots):
                sl = (ring["next"] + k) % ring["size"]
                if ring["last"][sl] is not None:
                    pr.add(ring["last"][sl])
                ring["last"][sl] = i
                s.append(sl)
            ring["next"] = (ring["next"] + it.nslots) % ring["size"]
        slot_of.append(s)
        preds.append(pr)
    loaded = [False] * n
    state = {"j": 0}

    def try_loads(done):
        while state["j"] < n:
            j = state["j"]
            if all(p <= done for p in preds[j]):
                if items[j].load is not None:
                    items[j].load(slot_of[j])
                loaded[j] = True
                state["j"] += 1
            else:
                break

    try_loads(-1)
    for i in range(n):
        assert loaded[i]
        items[i].compute(slot_of[i])
        try_loads(i)


class TK:
    pass


def t_alloc(P, nt):
    T = TK()
    T.nt = nt
    T.nh = nt // 512
    T.xT = P.sb("xT", [128, KC, nt], F32)
    T.hT = P.sb("hT", [128, KC, nt], BF16)
    T.act = P.sb("act", [128, 2, 4, nt], BF16)
    T.wA = P.sb("wA", [128, 4, KC, 256], BF16)
    T.wD = P.sb("wD", [128, 4, D], BF16)
    T.mod = P.sb("mod", [128, 9, KC], F32)
    T.ng = P.sb("ng", [128, 3, KC], F32)
    T.Acoef = P.sb("Acoef", [128, 3, KC], F32)
    T.hgate = P.sb("hgate", [128, 3, KC], F32)
    T.rstd = P.sb("rstd", [128, 512], F32)
    T.sq = P.sb("sq", [128, 2, 512], F32)
    T.tmp = P.sb("tmp", [128, 2, 512], F32)
    T.ones = P.sb("ones", [128, 128], F32)
    T.epsb = P.sb("epsb", [128, 1], F32)
    T.ps = [P.psum("ps%d" % i, [128, 512], F32) for i in range(8)]
    T.rings = {"A": dict(size=4, next=0, last=[None] * 4), "D": dict(size=4, next=0, last=[None] * 4)}
    P.op("dve", lambda e: e.memset(T.ones[:], 1.0), w=[("ones",)])
    P.op("dve", lambda e: e.memset(T.epsb[:], EPS), w=[("epsb",)])
    return T


def t_load_mod(P, T, mod_d, ng_d):
    P.dma("sp", T.mod[:].rearrange("p a c -> p (a c)"), mod_d, "ld_mod", w=[("mod",)])
    P.dma("sp", T.ng[:].rearrange("p a c -> p (a c)"), ng_d, "ld_ng", w=[("ng",)])
    for j in range(3):
        P.op("dve", lambda e: e.scalar_tensor_tensor(out=T.Acoef[:, j, :], in0=T.mod[:, 3 * j + 1, :], scalar=1.0,
                                                     in1=T.ng[:, j, :], op0=ALU.add, op1=ALU.mult),
             r=[("mod",), ("ng",)], w=[("Acoef", j)])
        gs = 1.0 if j == 1 else 0.5
        P.op("dve", lambda e: e.tensor_scalar(out=T.hgate[:, j, :], in0=T.mod[:, 3 * j + 2, :], scalar1=gs, scalar2=None,
                                              op0=ALU.mult),
             r=[("mod",)], w=[("hgate", j)])


def t_norm(P, T, j, out_kind="h", gvec=None, outbuf=None):
    for half in range(T.nh):
        cols = slice(half * 512, (half + 1) * 512)
        bank = T.ps[4 + (half % 2)]
        bres = ("ps", 4 + (half % 2))
        for c in range(KC):
            sq = T.sq[:, c % 2, :]
            P.op("act", lambda e: e.activation(out=sq, in_=T.xT[:, c, cols], func=AF.Square),
                 r=[("x", c, half)], w=[("sq", c % 2)])
            P.op("pe", lambda e: e.matmul(bank[:], T.ones[:], sq, start=(c == 0), stop=(c == KC - 1)),
                 r=[("sq", c % 2), ("ones",)], w=[bres])
        P.op("act", lambda e: e.activation(out=T.rstd[:], in_=bank[:], func=AF.Ln, scale=1.0 / D, bias=T.epsb[:]),
             r=[bres, ("epsb",)], w=[("rstd",)])
        P.op("act", lambda e: e.activation(out=T.rstd[:], in_=T.rstd[:], func=AF.Exp, scale=-0.5),
             r=[("rstd",)], w=[("rstd",)])
        for c in range(KC):
            tmp = T.tmp[:, c % 2, :]
            if out_kind == "h":
                P.op("dve", lambda e: e.scalar_tensor_tensor(out=tmp, in0=T.xT[:, c, cols], scalar=T.Acoef[:, j, c:c + 1],
                                                             in1=T.rstd[:], op0=ALU.mult, op1=ALU.mult),
                     r=[("x", c, half), ("rstd",), ("Acoef", j)], w=[("tmp", c % 2)])
                P.op("act", lambda e: e.activation(out=T.hT[:, c, cols], in_=tmp, func=AF.Identity,
                                                   bias=T.mod[:, 3 * j, c:c + 1], scale=1.0),
                     r=[("tmp", c % 2), ("mod",)], w=[("h", c, half)])
            else:
                P.op("dve", lambda e: e.scalar_tensor_tensor(out=tmp, in0=T.xT[:, c, cols], scalar=gvec[:, c:c + 1],
                                                             in1=T.rstd[:], op0=ALU.mult, op1=ALU.mult),
                     r=[("x", c, half), ("rstd",), ("fg",)], w=[("tmp", c % 2)])
                P.dma("sp", outbuf[c * 128:(c + 1) * 128, cols], tmp, "st_tmp%d" % (c % 2), r=[("tmp", c % 2)])


def t_slabA_load(P, T, slot, src):
    ncols = src.shape[1]
    P.dma("pool", T.wA[:, slot, :, 0:ncols], src.rearrange("(kc p) n -> p kc n", p=128), "ldA%d" % slot,
          w=[("A", slot)])


def ffn_items(P, T, wgu, wd, j):
    items = []
    NP = DFF // 512
    nh = T.nh

    def mk_gu(p, s):
        f0 = (p * 4 + s * 2) * 128

        def load(slots):
            t_slabA_load(P, T, slots[0], wgu[:, f0:f0 + 256])
            t_slabA_load(P, T, slots[1], wgu[:, DFF + f0:DFF + f0 + 256])

        def compute(slots):
            for fc in range(2):
                cj = s * 2 + fc
                for kind in range(2):
                    sl = slots[kind]
                    for half in range(nh):
                        bank = kind * 2 + (half % 2)
                        cols = slice(half * 512, (half + 1) * 512)
                        for kc in range(KC):
                            P.op("pe", lambda e: e.matmul(T.ps[bank][:], T.wA[:, sl, kc, fc * 128:(fc + 1) * 128],
                                                          T.hT[:, kc, cols], start=(kc == 0), stop=(kc == KC - 1)),
                                 r=[("A", sl), ("h", kc, half)], w=[("ps", bank)], signal=(kc == KC - 1))
                        if kind == 0:
                            P.op("act", lambda e: e.activation(out=T.sq[:, half % 2, :], in_=T.ps[bank][:], func=AF.Silu),
                                 r=[("ps", bank)], w=[("sq", half % 2)])
                        else:
                            P.op("dve", lambda e: e.tensor_tensor(out=T.act[:, p % 2, cj, cols], in0=T.ps[bank][:],
                                                                  in1=T.sq[:, half % 2, :], op=ALU.mult),
                                 r=[("ps", bank), ("sq", half % 2)], w=[("act", p % 2, cj, half)])
        return Item("A", 2, load, compute)

    def mk_down(p):
        def load(slots):
            for jj in range(4):
                r0 = (p * 4 + jj) * 128
                P.dma("pool", T.wD[:, slots[jj], :], wd[r0:r0 + 128, :], "ldD%d" % slots[jj], w=[("D", slots[jj])])

        def compute(slots):
            ctr = 0
            for mo in range(KC):
                for half in range(nh):
                    bank = 4 + (ctr % 4)
                    ctr += 1
                    cols = slice(half * 512, (half + 1) * 512)
                    for jj in range(4):
                        P.op("pe", lambda e: e.matmul(T.ps[bank][:], T.wD[:, slots[jj], mo * 128:(mo + 1) * 128],
                                                      T.act[:, p % 2, jj, cols], start=(jj == 0), stop=(jj == 3)),
                             r=[("D", slots[jj]), ("act", p % 2, jj, half)], w=[("ps", bank)], signal=(jj == 3))
                    P.op("dve", lambda e: e.scalar_tensor_tensor(out=T.xT[:, mo, cols], in0=T.ps[bank][:],
                                                                 scalar=T.hgate[:, j, mo:mo + 1], in1=T.xT[:, mo, cols],
                                                                 op0=ALU.mult, op1=ALU.add),
                         r=[("ps", bank), ("hgate", j), ("x", mo, half)], w=[("x", mo, half)])
        return Item("D", 4, load, compute)

    items.append(Item(None, 0, None, lambda s: t_norm(P, T, j)))
    order = []
    for p in range(NP):
        order.append(("g", p))
        if p >= 1:
            order.append(("d", p - 1))
    order.append(("d", NP - 1))
    for kind, p in order:
        if kind == "g":
            items.append(mk_gu(p, 0))
            items.append(mk_gu(p, 1))
        else:
            items.append(mk_down(p))
    return items


PROJ_SLABS = ([("dq", "fm", "A", 0 + 2 * i) for i in range(4)] + [("dk", "fm", "A", 8 + 2 * i) for i in range(4)] +
              [("dv", "tm", None, 256 * i) for i in range(4)] + [("nq", "fm", "B", 16 + 2 * i) for i in range(4)] +
              [("kc", "fm", None, 24), ("vc", "fm", None, 30), ("ks", "fm", "B", 26), ("vs", "tm", None, 1024),
               ("kw", "fm", "B", 28), ("vw", "tm", None, 1280), ("gl", "gl", None, 0)])
NQK = 32
NV = 1536


def t_alloc_proj(P, T):
    nt = T.nt
    T.rope = P.sb("rope", [128, 4, nt], F32)
    T.Rm = P.sb("Rm", [128, 2, 128], BF16)
    T.xb = P.sb("xb", [128, 2, 512], BF16)
    T.t2 = P.sb("t2", [128, 2, 512], F32)
    T.ost = P.sb("ost", [128, 4, 512], BF16)
    T.vst = P.sb("vst", [128, 2, 256], BF16)
    T.cnt = dict(ost=0, vst=0, gst=0, xb=0, pa=0, pb=0)


def t_load_proj_consts(P, T, rope_d, rm_d):
    P.dma_batch("sp", "ld_rope", [(T.rope[:, i, :], rope_d[i], [], [("rope", i)]) for i in range(4)])
    P.dma_batch("sp", "ld_rm", [(T.Rm[:, i, :], rm_d[i], [], [("Rm", i)]) for i in range(2)])


def proj_items(P, T, w_in, qk_d, vv_d, gt_d):
    items = [Item(None, 0, None, lambda s: t_norm(P, T, 1))]
    nh = T.nh

    def mk(sidx):
        name, lay, rt, base = PROJ_SLABS[sidx]
        c0 = 256 * sidx
        ncols = min(256, INC - c0)

        def load(slots):
            t_slabA_load(P, T, slots[0], w_in[:, c0:c0 + ncols])

        def compute(slots):
            sl = slots[0]
            if lay == "tm":
                for t in range(T.nt // 128):
                    bank = T.cnt["pa"] % 4
                    T.cnt["pa"] += 1
                    half = t // 4
                    for kc in range(KC):
                        P.op("pe", lambda e: e.matmul(T.ps[bank][:, 0:256], T.hT[:, kc, t * 128:(t + 1) * 128],
                                                      T.wA[:, sl, kc, 0:256], start=(kc == 0), stop=(kc == KC - 1)),
                             r=[("A", sl), ("h", kc, half)], w=[("ps", bank)], signal=(kc == KC - 1))
                    vs = T.cnt["vst"] % 2
                    T.cnt["vst"] += 1
                    P.op("act", lambda e: e.activation(out=T.vst[:, vs, :], in_=T.ps[bank][:, 0:256], func=AF.Copy),
                         r=[("ps", bank)], w=[("vst", vs)])
                    P.dma("sp", vv_d[t * 128:(t + 1) * 128, base:base + 256], T.vst[:, vs, :], "st_v%d" % vs,
                          r=[("vst", vs)])
                return
            if lay == "gl":
                for half in range(nh):
                    bank = T.cnt["pa"] % 4
                    T.cnt["pa"] += 1
                    cols = slice(half * 512, (half + 1) * 512)
                    for kc in range(KC):
                        P.op("pe", lambda e: e.matmul(T.ps[bank][:], T.wA[:, sl, kc, 0:128], T.hT[:, kc, cols],
                                                      start=(kc == 0), stop=(kc == KC - 1)),
                             r=[("A", sl), ("h", kc, half)], w=[("ps", bank)], signal=(kc == KC - 1))
                    gs = T.cnt["gst"] % 2
                    T.cnt["gst"] += 1
                    P.op("act", lambda e: e.activation(out=T.t2[0:24, gs, :], in_=T.ps[bank][0:24, :], func=AF.Sigmoid),
                         r=[("ps", bank)], w=[("t2", gs)])
                    P.dma("sp", gt_d[:, cols], T.t2[0:24, gs, :], "st_g%d" % gs, r=[("t2", gs)])
                return
            for fc in range(2):
                for half in range(nh):
                    bank = T.cnt["pa"] % 4
                    T.cnt["pa"] += 1
                    cols = slice(half * 512, (half + 1) * 512)
                    for kc in range(KC):
                        P.op("pe", lambda e: e.matmul(T.ps[bank][:], T.wA[:, sl, kc, fc * 128:(fc + 1) * 128],
                                                      T.hT[:, kc, cols], start=(kc == 0), stop=(kc == KC - 1)),
                             r=[("A", sl), ("h", kc, half)], w=[("ps", bank)], signal=(kc == KC - 1))
                    os_ = T.cnt["ost"] % 4
                    T.cnt["ost"] += 1
                    if rt is None:
                        P.op("act", lambda e: e.activation(out=T.ost[:, os_, :], in_=T.ps[bank][:], func=AF.Copy),
                             r=[("ps", bank)], w=[("ost", os_)])
                    else:
                        ri = 0 if rt == "A" else 1
                        xs = T.cnt["xb"] % 2
                        T.cnt["xb"] += 1
                        b2 = 4 + (T.cnt["pb"] % 4)
                        T.cnt["pb"] += 1
                        P.op("act", lambda e: e.activation(out=T.xb[:, xs, :], in_=T.ps[bank][:], func=AF.Copy),
                             r=[("ps", bank)], w=[("xb", xs)])
                        import os
                        step = int(os.environ.get("ROPE_STEP", "4"))
                        if step >= 2:
                            P.op("pe", lambda e: e.matmul(T.ps[b2][:], T.Rm[:, ri, :], T.xb[:, xs, :], start=True, stop=True),
                                 r=[("xb", xs), ("Rm", ri)], w=[("ps", b2)])
                        if step >= 3:
                            P.op("dve", lambda e: e.tensor_tensor(out=T.tmp[:, xs, :], in0=T.ps[bank][:],
                                                                  in1=T.rope[:, 2 * ri, cols], op=ALU.mult),
                                 r=[("ps", bank), ("rope", 2 * ri), ("xb", xs)], w=[("tmp", xs)])
                            P.op("dve", lambda e: e.tensor_tensor(out=T.t2[:, xs, :], in0=T.ps[b2][:],
                                                                  in1=T.rope[:, 2 * ri + 1, cols], op=ALU.mult),
                                 r=[("ps", b2), ("rope", 2 * ri + 1)], w=[("t2", xs)])
                        if step >= 4:
                            P.op("dve", lambda e: e.tensor_tensor(out=T.ost[:, os_, :], in0=T.tmp[:, xs, :],
                                                                  in1=T.t2[:, xs, :], op=ALU.add),
                                 r=[("tmp", xs), ("t2", xs)], w=[("ost", os_)])
                        else:
                            P.op("act", lambda e: e.activation(out=T.ost[:, os_, :], in_=T.xb[:, xs, :], func=AF.Copy),
                                 r=[("xb", xs)], w=[("ost", os_)])
                    P.dma("sp", qk_d[(base + fc) * 128:(base + fc + 1) * 128, cols], T.ost[:, os_, :], "st_o%d" % os_, r=[("ost", os_)])
        return Item("A", 1, load, compute)

    import os
    kinds = os.environ.get("PROJ_KINDS", "fmrope,fmplain,tm,gl").split(",")
    for sidx in range(len(PROJ_SLABS)):
        name, lay, rt, base = PROJ_SLABS[sidx]
        k = "tm" if lay == "tm" else ("gl" if lay == "gl" else ("fmrope" if rt else "fmplain"))
        if k in kinds:
            items.append(mk(sidx))
    return items


def wo_items(P, T, w_o, oT_d):
    items = []

    def ld(s):
        P.dma_batch("sp", "ld_o", [(T.hT[:, c, :], oT_d[c * 128:(c + 1) * 128, :], [],
                                    [("h", c, half) for half in range(T.nh)]) for c in range(KC)])
    items.append(Item(None, 0, None, ld))

    def mk(si):
        def load(slots):
            t_slabA_load(P, T, slots[0], w_o[:, si * 256:(si + 1) * 256])

        def compute(slots):
            sl = slots[0]
            for fc in range(2):
                mo = si * 2 + fc
                for half in range(T.nh):
                    bank = 4 + (T.cnt["pb"] % 4)
                    T.cnt["pb"] += 1
                    cols = slice(half * 512, (half + 1) * 512)
                    for kc in range(KC):
                        P.op("pe", lambda e: e.matmul(T.ps[bank][:], T.wA[:, sl, kc, fc * 128:(fc + 1) * 128],
                                                      T.hT[:, kc, cols], start=(kc == 0), stop=(kc == KC - 1)),
                             r=[("A", sl), ("h", kc, half)], w=[("ps", bank)], signal=(kc == KC - 1))
                    P.op("dve", lambda e: e.scalar_tensor_tensor(out=T.xT[:, mo, cols], in0=T.ps[bank][:],
                                                                 scalar=T.hgate[:, 1, mo:mo + 1], in1=T.xT[:, mo, cols],
                                                                 op0=ALU.mult, op1=ALU.add),
                         r=[("ps", bank), ("hgate", 1), ("x", mo, half)], w=[("x", mo, half)])
        return Item("A", 1, load, compute)
    for si in range(8):
        items.append(mk(si))
    return items


def build_T(phases, nt=NT):
    nc = bass.Bass("TRN2", target_bir_lowering=False)
    stack = ExitStack()
    with stack:
        P = Prog(nc, stack)
        T = t_alloc(P, nt)
        T.cnt = dict(ost=0, vst=0, gst=0, xb=0, pa=0, pb=0)
        xin = nc.dram_tensor("xT_in", [D, nt], F32, kind="ExternalInput").ap()
        P.dma_batch("sp", "ld_x", [(T.xT[:, c, :], xin[c * 128:(c + 1) * 128, :], [],
                                    [("x", c, h) for h in range(T.nh)]) for c in range(KC)])
        items = []
        out_sems = []
        lay = 0
        if "wo" in phases or "ffn_b" in phases:
            modB = nc.dram_tensor("mod_b", [128, 9 * KC], F32, kind="ExternalInput").ap()
            ngB = nc.dram_tensor("ng_b", [128, 3 * KC], F32, kind="ExternalInput").ap()
            items.append(Item(None, 0, None, lambda s: t_load_mod(P, T, modB, ngB)))
        if "wo" in phases:
            w_o = nc.dram_tensor("w_o", [D, D], F32, kind="ExternalInput").ap()
            oT = nc.dram_tensor("oT", [D, nt], BF16, kind="ExternalInput").ap()
            items += wo_items(P, T, w_o, oT)
        if "ffn_b" in phases:
            wgu_b = nc.dram_tensor("wgu_b", [D, 2 * DFF], F32, kind="ExternalInput").ap()
            wd_b = nc.dram_tensor("wd_b", [DFF, D], F32, kind="ExternalInput").ap()
            items += ffn_items(P, T, wgu_b, wd_b, 2)
        if "ffn_a" in phases or "proj" in phases:
            modA = nc.dram_tensor("mod_a", [128, 9 * KC], F32, kind="ExternalInput").ap()
            ngA = nc.dram_tensor("ng_a", [128, 3 * KC], F32, kind="ExternalInput").ap()
            items.append(Item(None, 0, None, lambda s: t_load_mod(P, T, modA, ngA)))
        if "ffn_a" in phases:
            wgu_a = nc.dram_tensor("wgu_a", [D, 2 * DFF], F32, kind="ExternalInput").ap()
            wd_a = nc.dram_tensor("wd_a", [DFF, D], F32, kind="ExternalInput").ap()
            items += ffn_items(P, T, wgu_a, wd_a, 0)
        if "proj" in phases:
            t_alloc_proj(P, T)
            w_in = nc.dram_tensor("w_in", [D, INC], F32, kind="ExternalInput").ap()
            rope_d = nc.dram_tensor("rope_d", [4, 128, nt], F32, kind="ExternalInput").ap()
            rm_d = nc.dram_tensor("rm_d", [2, 128, 128], BF16, kind="ExternalInput").ap()
            qk_d = nc.dram_tensor("qk", [NQK * 128, nt], BF16, kind="ExternalOutput").ap()
            vv_d = nc.dram_tensor("vv", [nt, NV], BF16, kind="ExternalOutput").ap()
            gt_d = nc.dram_tensor("gt", [24, nt], F32, kind="ExternalOutput").ap()
            t_load_proj_consts(P, T, rope_d, rm_d)
            items += proj_items(P, T, w_in, qk_d, vv_d, gt_d)
        if "final" in phases:
            fg_d = nc.dram_tensor("fg_d", [128, KC], F32, kind="ExternalInput").ap()
            yT = nc.dram_tensor("yT", [D, nt], F32, kind="ExternalOutput").ap()
            T.fg = P.sb("fg", [128, KC], F32)
            P.dma("sp", T.fg[:], fg_d, "ld_fg", w=[("fg",)])
            items.append(Item(None, 0, None, lambda s: t_norm(P, T, 0, out_kind="final", gvec=T.fg, outbuf=yT)))
        if "xout" in phases:
            xout = nc.dram_tensor("xT_out", [D, nt], F32, kind="ExternalOutput").ap()

            def xo(s):
                P.dma_batch("sp", "st_x", [(xout[c * 128:(c + 1) * 128, :], T.xT[:, c, :],
                                            [("x", c, h) for h in range(T.nh)], []) for c in range(KC)])
            items.append(Item(None, 0, None, xo))
        run_stream(T.rings, items)
        P.finish("sp", [s for s in P.semobj if s.startswith("st_")])
        print("T kernel", phases, "ops", P.nops)
    return nc


def rope_tables(pos):
    nt = len(pos)
    out = np.zeros((4, 128, nt), np.float64)
    out[0] = 1.0
    out[2] = 1.0
    p = pos.astype(np.float64)
    for ti, (blk, half) in enumerate(((64, 8), (128, 16))):
        inv = np.exp(-math.log(ROPE_THETA) * np.arange(half, dtype=np.float32) / half).astype(np.float32)
        ang = (pos.astype(np.float32)[None, :] * inv[:, None]).astype(np.float32).astype(np.float64)
        for m in range(128):
            j = m % blk
            if j < half:
                out[2 * ti, m] = np.cos(ang[j])
                out[2 * ti + 1, m] = -np.sin(ang[j])
            elif j < 2 * half:
                out[2 * ti, m] = np.cos(ang[j - half])
                out[2 * ti + 1, m] = np.sin(ang[j - half])
    return out.astype(np.float32)


def rope_perm():
    rm = np.zeros((2, 128, 128), np.float32)
    for ti, (blk, half) in enumerate(((64, 8), (128, 16))):
        for m in range(128):
            j = m % blk
            if j < half:
                rm[ti, m + half, m] = 1.0
            elif j < 2 * half:
                rm[ti, m - half, m] = 1.0
    return rm.astype(NPBF)


def fm(v):
    v = np.asarray(v)
    lead = v.shape[:-1]
    n = v.shape[-1] // 128
    v = v.reshape(lead + (n, 128))
    return np.ascontiguousarray(np.moveaxis(v, -1, 0))


NSA_SCALE = 128 ** -0.5
DIFF_SCALE = 64 ** -0.5
NEGB = -30000.0
NCMP = 255
AX = mybir.AxisListType


def build_A(nqb=8):
    nc = bass.Bass("TRN2", target_bir_lowering=False)
    stack = ExitStack()
    SQ = nqb * 512
    with stack:
        P = Prog(nc, stack)

        def din(name, shape, dt):
            return nc.dram_tensor(name, shape, dt, kind="ExternalInput").ap()
        nq4_d = din("nq4", [512, S], BF16)
        kc_d = din("kc", [128, S], BF16)
        vc_d = din("vc", [128, S], BF16)
        ks_d = din("ks", [128, S], BF16)
        kw_d = din("kw", [128, S], BF16)
        vs_d = din("vs", [S, 128], BF16)
        vw_d = din("vw", [S, 128], BF16)
        dq_d = din("dq", [256, S], BF16)
        dk_d = din("dk", [256, S], BF16)
        dv_d = din("dv", [S, 256], BF16)
        g6_d = din("g6", [6, S], F32)
        w1_d = din("w1", [8192, 256], F32)
        w2_d = din("w2", [512, 128], F32)
        peT_d = din("peT", [128, 64], F32)
        lam_d = din("lam", [256], F32)
        subln_d = din("subln", [128, 1], F32)
        lamc_d = din("lamc", [128, 2], F32)
        cmask_d = din("cmask", [256, S], BF16)
        ov_d = din("ov", [256, 64], F32)
        btile_d = din("btile", [128, 32 * 64], F32)
        E_d = din("Emat", [128, 32 * 128], BF16)
        wm_d = din("wm", [128, 8 * 512], BF16)
        selG_d = din("selG", [32, 6 * 128], F32)
        ident_d = din("ident", [128, 128], F32)
        oT_d = nc.dram_tensor("oT", [512, S], BF16, kind="ExternalOutput").ap()

        arena = P.sb("arena", [128, 40960], BF16)
        nq4 = arena[:, 0:16384].rearrange("p (a b) -> p a b", a=4)
        kcT = arena[:, 16384:20480]
        vcT = arena[:, 20480:24576]
        ksT = arena[:, 24576:28672]
        kwT = arena[:, 28672:32768]
        vs = arena[:, 32768:36864].rearrange("p (a b) -> p a b", a=32)
        vw = arena[:, 36864:40960].rearrange("p (a b) -> p a b", a=32)
        dqz = arena[:, 0:16384].rearrange("p (h c b) -> p h c b", h=2, c=2)
        dkT = arena[:, 16384:24576].rearrange("p (a b) -> p a b", a=2)
        dv = arena[:, 24576:32768].rearrange("p (a b) -> p a b", a=32)

        oacc = P.sb("oacc", [128, 2, 2, 512], F32)
        biasT = P.sb("biasT", [128, 2, 512], BF16)
        cm_sb = P.sb("cm_sb", [128, 2, 2, 512], BF16)
        bt_sb = P.sb("bt_sb", [128, 2, 4, 64], F32)
        g6_sb = P.sb("g6_sb", [32, 2, 512], F32)
        wm = P.sb("wm_sb", [128, 8, 512], BF16)
        Em = P.sb("Em", [128, 32, 128], BF16)
        ov = P.sb("ov_sb", [128, 2, 64], F32)
        selG = P.sb("selG_sb", [32, 6, 128], F32)
        identf = P.sb("identf", [128, 128], F32)
        onesb = P.sb("onesb", [128, 128], BF16)
        onesf = P.sb("onesf", [128, 128], F32)
        epsb = P.sb("epsb", [128, 1], F32)
        w1 = P.sb("w1_sb", [128, 32, 256], BF16)
        w2 = P.sb("w2_sb", [128, 2, 2, 128], BF16)
        peT = P.sb("peT_sb", [128, 2, 32], BF16)
        hidT = P.sb("hidT", [128, 2, 2, 256], BF16)
        cbias = P.sb("cbias", [128, 2, 2], F32)
        kcmpT = P.sb("kcmpT", [128, 256], BF16)
        vcmp = P.sb("vcmp", [128, 2, 128], BF16)
        pT = P.sb("pT", [128, 4, 512], BF16)
        em = P.sb("em", [128, 2, 512], F32)
        pf = P.sb("pf", [128, 2, 512], F32)
        pb = P.sb("pb", [128, 2, 512], BF16)
        psumT = P.sb("psumT", [128, 2, 512], F32)
        rden = P.sb("rden", [128, 2, 512], F32)
        gsb = P.sb("gsb", [128, 2, 512], F32)
        tmpf = P.sb("tmpf", [128, 2, 512], F32)
        score = P.sb("score", [128, 64], F32)
        work = P.sb("work", [128, 64], F32)
        m8a = P.sb("m8a", [128, 8], F32)
        m8b = P.sb("m8b", [128, 8], F32)
        bval = P.sb("bval", [128, 128], F32)
        obf = P.sb("obf", [128, 2, 512], BF16)
        lamt = P.sb("lamt", [128, 256], F32)
        lprod = P.sb("lprod", [128, 2, 64], F32)
        lsc = P.sb("lsc", [128, 8], F32)
        subln = P.sb("subln_sb", [128, 1], F32)
        lamc = P.sb("lamc_sb", [128, 2], F32)
        ps = [P.psum("ps%d" % i, [128, 512], F32) for i in range(8)]
        cnt = dict(pT=0, ep=0, ob=0, st=0, ds=0)

        def mm(out, lhsT, rhs, start, stop, r, w, signal=True):
            P.op("pe", lambda e: e.matmul(out, lhsT, rhs, start=start, stop=stop), r=r, w=w, signal=signal)

        P.op("dve", lambda e: e.memset(onesb[:], 1.0), w=[("onesb",)])
        P.op("dve", lambda e: e.memset(onesf[:], 1.0), w=[("onesf",)])
        P.op("dve", lambda e: e.memset(epsb[:], EPS), w=[("epsb",)])
        P.op("dve", lambda e: e.memset(hidT[:], 0.0), w=[("hidT",)])
        P.op("dve", lambda e: e.memset(bval[:], 0.0), w=[("bval",)])
        P.op("dve", lambda e: e.memset(g6_sb[:], 0.0), w=[("g6", 0), ("g6", 1)])
        P.dma("sp", wm[:].rearrange("p a b -> p (a b)"), wm_d, "ld_wm", w=[("wm",)])
        P.dma("sp", Em[:].rearrange("p a b -> p (a b)"), E_d, "ld_E", w=[("Em",)])
        P.dma("sp", ov[:], ov_d.rearrange("(c p) n -> p c n", p=128), "ld_ov", w=[("ov",)])
        P.dma("sp", selG[:].rearrange("p a b -> p (a b)"), selG_d, "ld_selG", w=[("selG",)])
        P.dma("sp", identf[:], ident_d, "ld_ident", w=[("identf",)])
        P.dma("sp", subln[:], subln_d, "ld_subln", w=[("subln",)])
        P.dma("sp", lamc[:], lamc_d, "ld_lamc", w=[("lamc",)])
        P.dma("sp", lamt[:], lam_d.partition_broadcast(128), "ld_lam", w=[("lamt",)])
        P.dma_batch("sp", "ld_nsa", [
            (nq4[:, j, :], nq4_d[j * 128:(j + 1) * 128, :], [], [("nq4", j)]) for j in range(4)] + [
            (kcT, kc_d, [], [("kcT",)]), (vcT, vc_d, [], [("vcT",)]), (ksT, ks_d, [], [("ksT",)]),
            (kwT, kw_d, [], [("kwT",)]),
            (vs, vs_d.rearrange("(t p) n -> p t n", p=128), [], [("vs",)]),
            (vw, vw_d.rearrange("(t p) n -> p t n", p=128), [], [("vw",)])])
        P.dma("pool", w2[:].rearrange("p j h n -> p (j h) n"), w2_d.rearrange("(a p) n -> p a n", p=128), "ld_w2",
              w=[("w2",)])
        P.dma("pool", peT[:].rearrange("p j l -> p (j l)"), peT_d, "ld_pe", w=[("peT",)])

        for i in range(2):
            P.op("dve", lambda e: e.tensor_tensor(out=lprod[:, i, :], in0=lamt[:, 128 * i:128 * i + 64],
                                                  in1=lamt[:, 128 * i + 64:128 * i + 128], op=ALU.mult),
                 r=[("lamt",)], w=[("lprod", i)])
            P.op("dve", lambda e: e.reduce_sum(out=lsc[:, i:i + 1], in_=lprod[:, i, :], axis=AX.X),
                 r=[("lprod", i)], w=[("lsc", i)])
            P.op("act", lambda e: e.activation(out=lsc[:, 2 + i:3 + i], in_=lsc[:, i:i + 1], func=AF.Exp),
                 r=[("lsc", i)], w=[("lsc", 2 + i)])
        P.op("dve", lambda e: e.tensor_tensor(out=lsc[:, 4:5], in0=lsc[:, 2:3], in1=lsc[:, 3:4], op=ALU.subtract),
             r=[("lsc", 2), ("lsc", 3)], w=[("lsc", 4)])
        P.op("dve", lambda e: e.tensor_tensor(out=lsc[:, 4:5], in0=lsc[:, 4:5], in1=lamc[:, 0:1], op=ALU.add),
             r=[("lsc", 4), ("lamc",)], w=[("lsc", 4)])
        P.op("dve", lambda e: e.tensor_scalar(out=lsc[:, 5:6], in0=lsc[:, 4:5], scalar1=-1.0, scalar2=None, op0=ALU.mult),
             r=[("lsc", 4)], w=[("lsc", 5)])
        P.op("dve", lambda e: e.tensor_tensor(out=lsc[:, 6:7], in0=subln[:], in1=lamc[:, 1:2], op=ALU.mult),
             r=[("subln",), ("lamc",)], w=[("lsc", 6)])

        for j in range(2):
            XT, xres = (kcT, ("kcT",)) if j == 0 else (vcT, ("vcT",))
            P.dma("pool", w1[:], w1_d[j * 4096:(j + 1) * 4096, :].rearrange("(l p) n -> p l n", p=128), "ld_w1",
                  w=[("w1",)])
            for hc in range(2):
                for l in range(32):
                    mm(ps[2][:, hc:hc + 1], w1[:, l, hc * 128:(hc + 1) * 128], peT[:, j, l:l + 1], l == 0, l == 31,
                       r=[("w1",), ("peT",)], w=[("ps", 2)], signal=(l == 31))
            P.op("act", lambda e: e.activation(out=cbias[:, j, :], in_=ps[2][:, 0:2], func=AF.Copy),
                 r=[("ps", 2)], w=[("cbias", j)])
            for hc in range(2):
                for l in range(32):
                    mm(ps[hc][:, 0:NCMP], w1[:, l, hc * 128:(hc + 1) * 128], XT[:, l:l + 16 * (NCMP - 1) + 1:16], l == 0, l == 31,
                       r=[("w1",), xres], w=[("ps", hc)], signal=(l == 31))
                P.op("act", lambda e: e.activation(out=hidT[:, j, hc, 0:NCMP], in_=ps[hc][:, 0:NCMP], func=AF.Silu,
                                                   bias=cbias[:, j, hc:hc + 1]),
                     r=[("ps", hc), ("cbias", j)], w=[("hidT",)])
            if j == 0:
                for hc in range(2):
                    mm(ps[3][:, 0:256], w2[:, 0, hc, :], hidT[:, 0, hc, :], hc == 0, hc == 1,
                       r=[("w2",), ("hidT",)], w=[("ps", 3)], signal=(hc == 1))
                P.op("act", lambda e: e.activation(out=kcmpT[:], in_=ps[3][:, 0:256], func=AF.Copy),
                     r=[("ps", 3)], w=[("kcmpT",)])
            else:
                for cc in range(2):
                    for hc in range(2):
                        mm(ps[3][:, cc * 128:(cc + 1) * 128], hidT[:, 1, hc, cc * 128:(cc + 1) * 128], w2[:, 1, hc, :],
                           hc == 0, hc == 1, r=[("w2",), ("hidT",)], w=[("ps", 3)], signal=(hc == 1))
                P.op("act", lambda e: e.activation(out=vcmp[:].rearrange("p a b -> p (a b)"), in_=ps[3][:, 0:256],
                                                   func=AF.Copy),
                     r=[("ps", 3)], w=[("vcmp",)])

        def branch_epilogue(hi, par, accO, accD, gidx, first, gbank):
            mm(ps[gbank][:], selG[:, hi * 3 + gidx, :], g6_sb[:, par, :], True, True,
               r=[("selG",), ("g6", par)], w=[("ps", gbank)])
            es = cnt["ep"] % 2
            cnt["ep"] += 1
            P.op("act", lambda e: e.activation(out=gsb[:, es, :], in_=ps[gbank][:], func=AF.Copy),
                 r=[("ps", gbank)], w=[("gsb", es)])
            if accD is not None:
                P.op("dve", lambda e: e.reciprocal(out=rden[:, es, :], in_=ps[accD][:]),
                     r=[("ps", accD)], w=[("rden", es)])
                P.op("dve", lambda e: e.tensor_tensor(out=gsb[:, es, :], in0=gsb[:, es, :], in1=rden[:, es, :], op=ALU.mult),
                     r=[("gsb", es), ("rden", es)], w=[("gsb", es)])
            if first:
                P.op("dve", lambda e: e.tensor_tensor(out=oacc[:, par, hi, :], in0=ps[accO][:], in1=gsb[:, es, :], op=ALU.mult),
                     r=[("ps", accO), ("gsb", es)], w=[("oacc", par, hi)])
            else:
                P.op("dve", lambda e: e.tensor_tensor(out=tmpf[:, es, :], in0=ps[accO][:], in1=gsb[:, es, :], op=ALU.mult),
                     r=[("ps", accO), ("gsb", es)], w=[("tmpf", es)])
                P.op("dve", lambda e: e.tensor_tensor(out=oacc[:, par, hi, :], in0=oacc[:, par, hi, :], in1=tmpf[:, es, :],
                                                      op=ALU.add),
                     r=[("oacc", par, hi), ("tmpf", es)], w=[("oacc", par, hi)])

        def make_tiles(kts, qT_ap, qres, kT, kres, vtile, vres, masks, scale, accO, accD, extra_bias=None, epilogue=None):
            n = len(kts)
            assert n >= 2
            tiles = []
            st_g = {}
            for idx, kt in enumerate(kts):
                st = {}

                def emit_S(kt=kt, st=st):
                    sbk = cnt["st"] % 2
                    cnt["st"] += 1
                    st["sbk"] = sbk
                    ksl = slice(kt * 128, (kt + 1) * 128)
                    if extra_bias is None:
                        mm(ps[sbk][:], kT[:, ksl], qT_ap, True, True, r=[kres, qres], w=[("ps", sbk)])
                    else:
                        mm(ps[sbk][:], kT[:, ksl], qT_ap, True, False, r=[kres, qres], w=[("ps", sbk)], signal=False)
                        mm(ps[sbk][:], Em[:, kt, :], extra_bias, False, True, r=[("Em",), ("biasT",)], w=[("ps", sbk)])

                def emit_rest(kt=kt, st=st, idx=idx):
                    sbk = st["sbk"]
                    sl = cnt["pT"] % 4
                    cnt["pT"] += 1
                    P.op("act", lambda e: e.activation(out=pT[:, sl, :], in_=ps[sbk][:], func=AF.Exp, scale=scale),
                         r=[("ps", sbk)], w=[("pT", sl)])
                    mk = masks(kt)
                    if mk is not None:
                        P.op("dve", lambda e: e.tensor_tensor(out=pT[:, sl, :], in0=pT[:, sl, :], in1=wm[:, mk, :], op=ALU.mult),
                             r=[("pT", sl), ("wm",)], w=[("pT", sl)])
                    mm(ps[accO][:], vtile(kt), pT[:, sl, :], idx == 0, idx == n - 1, r=[vres, ("pT", sl)], w=[("ps", accO)],
                       signal=(idx == n - 1))
                    mm(ps[accD][:], onesb[:], pT[:, sl, :], idx == 0, idx == n - 1, r=[("onesb",), ("pT", sl)],
                       w=[("ps", accD)], signal=(idx == n - 1))
                    if idx == n - 1 and epilogue is not None:
                        epilogue()
                tiles.append((emit_S, emit_rest))
            return tiles

        def run_tiles(tiles):
            if not tiles:
                return
            tiles[0][0]()
            for i in range(len(tiles)):
                if i + 1 < len(tiles):
                    tiles[i + 1][0]()
                tiles[i][1]()

        for qb in range(nqb):
            par = qb % 2
            qc = slice(qb * 512, (qb + 1) * 512)
            ncc = 1 if qb <= 3 else 2
            P.dma("sp", cm_sb[:, par, :, :], cmask_d[:, qc].rearrange("(c p) n -> p c n", p=128), "ld_cm%d" % par,
                  w=[("cm", par)])
            P.dma("sp", bt_sb[:, par, :, :].rearrange("p a b -> p (a b)"), btile_d[:, qb * 256:(qb + 1) * 256],
                  "ld_bt%d" % par, w=[("bt", par)])
            P.dma("sp", g6_sb[0:6, par, :], g6_d[:, qc], "ld_g6%d" % par, w=[("g6", par)])
            for j in range(4):
                for cc in range(ncc):
                    mm(ps[cc][:], kcmpT[:, cc * 128:(cc + 1) * 128], nq4[:, j, qc], True, True,
                       r=[("kcmpT",), ("nq4", j)], w=[("ps", cc)])
                    P.op("act", lambda e: e.activation(out=em[:, cc, :], in_=ps[cc][:], func=AF.Exp, scale=NSA_SCALE),
                         r=[("ps", cc)], w=[("em", cc)])
                    P.op("dve", lambda e: e.tensor_tensor(out=em[:, cc, :], in0=em[:, cc, :], in1=cm_sb[:, par, cc, :],
                                                          op=ALU.mult),
                         r=[("em", cc), ("cm", par)], w=[("em", cc)])
                for cc in range(ncc):
                    mm(ps[2][:], onesf[:], em[:, cc, :], cc == 0, cc == ncc - 1, r=[("onesf",), ("em", cc)],
                       w=[("ps", 2)], signal=(cc == ncc - 1))
                P.op("dve", lambda e: e.tensor_scalar(out=rden[:, 0, :], in0=ps[2][:], scalar1=1e-30, scalar2=None, op0=ALU.max),
                     r=[("ps", 2)], w=[("rden", 0)])
                P.op("dve", lambda e: e.reciprocal(out=rden[:, 0, :], in_=rden[:, 0, :]), r=[("rden", 0)], w=[("rden", 0)])
                for cc in range(ncc):
                    dst = psumT if j == 0 else pf
                    dres = ("psumT", cc) if j == 0 else ("pf", cc)
                    P.op("dve", lambda e: e.tensor_tensor(out=dst[:, cc, :], in0=em[:, cc, :], in1=rden[:, 0, :], op=ALU.mult),
                         r=[("em", cc), ("rden", 0)], w=[dres])
                    if j < 2:
                        P.op("act", lambda e: e.activation(out=pb[:, cc, :], in_=dst[:, cc, :], func=AF.Copy),
                             r=[dres], w=[("pb", cc)])
                    if j > 0:
                        P.op("dve", lambda e: e.tensor_tensor(out=psumT[:, cc, :], in0=psumT[:, cc, :], in1=pf[:, cc, :],
                                                              op=ALU.add),
                             r=[("psumT", cc), ("pf", cc)], w=[("psumT", cc)])
                if j < 2:
                    for cc in range(ncc):
                        mm(ps[3][:], vcmp[:, cc, :], pb[:, cc, :], cc == 0, cc == ncc - 1, r=[("vcmp",), ("pb", cc)],
                           w=[("ps", 3)], signal=(cc == ncc - 1))
                    branch_epilogue(j, par, 3, None, 0, True, 6)
            for qt in range(4):
                for cc in range(ncc):
                    mm(ps[4][:, qt * 128:qt * 128 + 64], psumT[:, cc, qt * 128:(qt + 1) * 128], ov[:, cc, :], cc == 0,
                       cc == ncc - 1, r=[("psumT", cc), ("ov",)], w=[("ps", 4)], signal=(cc == ncc - 1))
                P.op("dve", lambda e: e.tensor_tensor(out=score[:], in0=ps[4][:, qt * 128:qt * 128 + 64],
                                                      in1=bt_sb[:, par, qt, :], op=ALU.add),
                     r=[("ps", 4), ("bt", par)], w=[("score",)])
                P.op("dve", lambda e: e.max(out=m8a[:], in_=score[:]), r=[("score",)], w=[("m8a",)])
                P.op("dve", lambda e: e.match_replace(out=work[:], in_to_replace=m8a[:], in_values=score[:], imm_value=-3.0e38),
                     r=[("score",), ("m8a",)], w=[("work",)])
                P.op("dve", lambda e: e.max(out=m8b[:], in_=work[:]), r=[("work",)], w=[("m8b",)])
                P.op("dve", lambda e: e.tensor_scalar(out=bval[:, 0:64], in0=score[:], scalar1=m8b[:, 7:8], scalar2=NEGB,
                                                      op0=ALU.is_lt, op1=ALU.mult),
                     r=[("score",), ("m8b",)], w=[("bval",)])
                mm(ps[5][:, qt * 128:(qt + 1) * 128], bval[:], identf[:], True, True, r=[("bval",), ("identf",)],
                   w=[("ps", 5)])
            P.op("act", lambda e: e.activation(out=biasT[:, par, :], in_=ps[5][:], func=AF.Copy),
                 r=[("ps", 5)], w=[("biasT",)])
            tl = []
            for hi in range(2):
                ao, ad = (2, 3) if hi == 0 else (4, 5)
                tl += make_tiles(list(range(4 * qb + 4)), nq4[:, hi, qc], ("nq4", hi), ksT, ("ksT",),
                                 lambda kt: vs[:, kt, :], ("vs",),
                                 lambda kt: (4 + kt - 4 * qb) if kt >= 4 * qb else None, NSA_SCALE, ao, ad,
                                 extra_bias=biasT[:, par, :],
                                 epilogue=(lambda hi=hi, ao=ao, ad=ad: branch_epilogue(hi, par, ao, ad, 1, False, 6 + hi)))
            for hi in range(2):
                ao, ad = (2, 3) if hi == 0 else (4, 5)
                kts = [kt for kt in range(4 * qb - 4, 4 * qb + 4) if kt >= 0]

                def epi(hi=hi, ao=ao, ad=ad):
                    branch_epilogue(hi, par, ao, ad, 2, False, 6 + hi)
                    os_ = cnt["ob"] % 2
                    cnt["ob"] += 1
                    P.op("act", lambda e: e.activation(out=obf[:, os_, :], in_=oacc[:, par, hi, :], func=AF.Copy),
                         r=[("oacc", par, hi)], w=[("obf", os_)])
                    P.dma("sp", oT_d[(2 + hi) * 128:(3 + hi) * 128, qc], obf[:, os_, :], "st_ob%d" % os_, r=[("obf", os_)])
                tl += make_tiles(kts, nq4[:, hi, qc], ("nq4", hi), kwT, ("kwT",), lambda kt: vw[:, kt, :], ("vw",),
                                 lambda kt, qb=qb: kt - (4 * qb - 4), NSA_SCALE, ao, ad, epilogue=epi)
            run_tiles(tl)

        P.barrier("sp")
        P.barrier("dve")
        P.op("dve", lambda e: e.memset(arena[:, 0:16384], 0.0), w=[("dqz",)])
        lst = []
        for h in range(2):
            for c in range(2):
                lst.append((dqz[64 * c:64 * c + 64, h, c, :], dq_d[h * 128 + 64 * c:h * 128 + 64 * c + 64, :], [], [("dqz",)]))
            lst.append((dkT[:, h, :], dk_d[h * 128:(h + 1) * 128, :], [], [("dkT",)]))
        lst.append((dv, dv_d.rearrange("(t p) n -> p t n", p=128), [], [("dv",)]))
        P.dma_batch("sp", "ld_diff", lst)
        tl = []
        pairs = [(2, 3), (4, 5)]
        pc = dict(i=0)
        for h in range(2):
            for qb in range(nqb):
                qc = slice(qb * 512, (qb + 1) * 512)
                acc = []
                for c in range(2):
                    acc.append(pairs[pc["i"] % 2])
                    pc["i"] += 1

                def epi(h=h, qb=qb, qc=qc, acc=acc):
                    for c in range(2):
                        ao, ad = acc[c]
                        P.op("dve", lambda e: e.reciprocal(out=rden[:, c, :], in_=ps[ad][:]),
                             r=[("ps", ad)], w=[("rden", c)])
                        P.op("dve", lambda e: e.tensor_tensor(out=tmpf[:, c, :], in0=ps[ao][:], in1=rden[:, c, :], op=ALU.mult),
                             r=[("ps", ao), ("rden", c)], w=[("tmpf", c)])
                    P.op("dve", lambda e: e.scalar_tensor_tensor(out=tmpf[:, 0, :], in0=tmpf[:, 1, :], scalar=lsc[:, 5:6],
                                                                 in1=tmpf[:, 0, :], op0=ALU.mult, op1=ALU.add),
                         r=[("tmpf", 0), ("tmpf", 1), ("lsc", 5)], w=[("tmpf", 0)])
                    P.op("act", lambda e: e.activation(out=em[:, 0, :], in_=tmpf[:, 0, :], func=AF.Square),
                         r=[("tmpf", 0)], w=[("em", 0)])
                    mb = 6
                    mm(ps[mb][:], onesf[:], em[:, 0, :], True, True, r=[("onesf",), ("em", 0)], w=[("ps", mb)])
                    P.op("act", lambda e: e.activation(out=em[:, 1, :], in_=ps[mb][:], func=AF.Ln, scale=1.0 / 128, bias=epsb[:]),
                         r=[("ps", mb), ("epsb",)], w=[("em", 1)])
                    P.op("act", lambda e: e.activation(out=em[:, 1, :], in_=em[:, 1, :], func=AF.Exp, scale=-0.5),
                         r=[("em", 1)], w=[("em", 1)])
                    os_ = cnt["ob"] % 2
                    cnt["ob"] += 1
                    P.op("dve", lambda e: e.scalar_tensor_tensor(out=obf[:, os_, :], in0=tmpf[:, 0, :], scalar=lsc[:, 6:7],
                                                                 in1=em[:, 1, :], op0=ALU.mult, op1=ALU.mult),
                         r=[("tmpf", 0), ("lsc", 6), ("em", 1)], w=[("obf", os_)])
                    P.dma("sp", oT_d[h * 128:(h + 1) * 128, qc], obf[:, os_, :], "st_ob%d" % os_, r=[("obf", os_)])
                for c in range(2):
                    ao, ad = acc[c]
                    tl += make_tiles(list(range(4 * qb + 4)), dqz[:, h, c, qc], ("dqz",), dkT[:, h, :], ("dkT",),
                                     lambda kt, h=h: dv[:, kt, h * 128:(h + 1) * 128], ("dv",),
                                     lambda kt, qb=qb: (4 + kt - 4 * qb) if kt >= 4 * qb else None, DIFF_SCALE, ao, ad,
                                     epilogue=(epi if c == 1 else None))
        run_tiles(tl)
        P.finish("sp", [s for s in P.semobj if s.startswith("st_")])
        print("A kernel ops", P.nops)
    return nc


def attn_consts():
    cm = np.zeros((256, S), np.float32)
    t = np.arange(S)
    for c in range(NCMP):
        cm[c] = (16 * c + 31 <= t)
    cs = np.arange(NCMP) * 16
    ss = np.arange(64) * 64
    ovm = np.clip(np.minimum(cs[:, None] + 32, ss[None, :] + 64) - np.maximum(cs[:, None], ss[None, :]), 0, None) / 32.0
    ov = np.zeros((256, 64), np.float32)
    ov[:NCMP] = ovm
    bt = np.zeros((128, 32, 64), np.float32)
    for tile in range(32):
        tt = tile * 128 + np.arange(128)
        m = np.arange(64)[None, :]
        valid = m * 64 <= tt[:, None]
        cur = (tt // 64)[:, None]
        forced = (m == 0) | (m == cur) | (m == cur - 1)
        bt[:, tile, :] = np.where(valid, np.where(forced, 1e6, 0.0), -1e30)
    E = np.zeros((128, 32, 128), np.float32)
    for kt in range(32):
        E[2 * kt, kt, 0:64] = 1.0
        E[2 * kt + 1, kt, 64:128] = 1.0
    wmk = np.zeros((128, 8, 512), np.float32)
    p = np.arange(128)[:, None]
    q = np.arange(512)[None, :]
    for r in range(8):
        k = (r - 4) * 128 + p
        wmk[:, r, :] = ((q - k >= 0) & (q - k < 512))
    selG = np.zeros((32, 6, 128), np.float32)
    for r in range(6):
        selG[r, r, :] = 1.0
    return dict(cmask=cm.astype(NPBF), ov=ov, btile=bt.reshape(128, 32 * 64), Emat=E.reshape(128, 32 * 128).astype(NPBF),
                wm=wmk.reshape(128, 8 * 512).astype(NPBF), selG=selG.reshape(32, 6 * 128),
                ident=np.eye(128, dtype=np.float32))


MCOLS = 9 * D // NCORE
MCH = MCOLS // 128


def build_M():
    nc = bass.Bass("TRN2", target_bir_lowering=False)
    stack = ExitStack()
    with stack:
        P = Prog(nc, stack)
        cT_d = nc.dram_tensor("cT", [128, KC * B], F32, kind="ExternalInput").ap()
        wada_d = nc.dram_tensor("wada", [DEPTH * D, MCOLS], F32, kind="ExternalInput").ap()
        bT_d = nc.dram_tensor("bT", [128, DEPTH * MCH], F32, kind="ExternalInput").ap()
        modp_d = nc.dram_tensor("modp", [128, DEPTH * MCH * B], F32, kind="ExternalOutput").ap()
        cT = P.sb("cT_sb", [128, KC, B], F32)
        cs = P.sb("cs_sb", [128, KC, B], BF16)
        bT = P.sb("bT_sb", [128, DEPTH * MCH], F32)
        wA = P.sb("wA", [128, 4, KC, 256], BF16)
        osb = P.sb("osb", [128, DEPTH * MCH, B], F32)
        ps = [P.psum("ps%d" % i, [128, 512], F32) for i in range(2)]
        rings = {"A": dict(size=4, next=0, last=[None] * 4)}
        P.dma("sp", cT[:].rearrange("p a b -> p (a b)"), cT_d, "ld_c", w=[("cT",)])
        P.dma("sp", bT[:], bT_d, "ld_b", w=[("bT",)])
        P.op("act", lambda e: e.activation(out=cs[:], in_=cT[:], func=AF.Silu), r=[("cT",)], w=[("cs",)])
        items = []
        cntr = dict(b=0)

        def mk(l, s):
            def load(slots):
                P.dma("pool", wA[:, slots[0], :, :],
                      wada_d[l * D:(l + 1) * D, s * 256:(s + 1) * 256].rearrange("(kc p) n -> p kc n", p=128),
                      "ldA%d" % slots[0], w=[("A", slots[0])])

            def compute(slots):
                sl = slots[0]
                for fc in range(2):
                    ch = l * MCH + s * 2 + fc
                    bank = cntr["b"] % 2
                    cntr["b"] += 1
                    for kc in range(KC):
                        P.op("pe", lambda e: e.matmul(ps[bank][:, 0:B], wA[:, sl, kc, fc * 128:(fc + 1) * 128], cs[:, kc, :],
                                                      start=(kc == 0), stop=(kc == KC - 1)),
                             r=[("A", sl), ("cs",)], w=[("ps", bank)], signal=(kc == KC - 1))
                    P.op("act", lambda e: e.activation(out=osb[:, ch, :], in_=ps[bank][:, 0:B], func=AF.Identity,
                                                       bias=bT[:, ch:ch + 1], scale=1.0),
                         r=[("ps", bank), ("bT",)], w=[("osb",)])
            return Item("A", 1, load, compute)
        for l in range(DEPTH):
            for s in range(MCH // 2):
                items.append(mk(l, s))
        run_stream(rings, items)
        P.dma("sp", modp_d, osb[:].rearrange("p a b -> p (a b)"), "st_mod", r=[("osb",)])
        P.finish("sp", ["st_mod"])
        print("M kernel ops", P.nops)
    return nc


_CACHE = {}


def _prog(key, fn):
    if key not in _CACHE:
        _CACHE[key] = fn()
    return _CACHE[key]


def _run(nc, in_maps):
    res = bass_utils.run_bass_kernel_spmd(nc, in_maps, core_ids=list(range(NCORE)))
    return res.results


def kernel(x, c, w_ada, b_ada, norm_g, ffn_w_gu, ffn_w_d, w_in, w_o, diff_lam, diff_subln, cmp_pe, cmp_w1, cmp_w2, final_g):
    f32 = np.float32
    x = np.asarray(x, f32)
    c = np.asarray(c, f32)
    ncM = _prog("M", build_M)
    cT = np.ascontiguousarray(c.reshape(B, KC, 128).transpose(2, 1, 0)).reshape(128, KC * B)
    in_maps = []
    for core in range(NCORE):
        cols = slice(core * MCOLS, (core + 1) * MCOLS)
        wa = np.ascontiguousarray(np.asarray(w_ada)[:, :, cols]).reshape(DEPTH * D, MCOLS)
        bt = np.ascontiguousarray(np.asarray(b_ada)[:, cols].reshape(DEPTH, MCH, 128).transpose(2, 0, 1)).reshape(128, DEPTH * MCH)
        in_maps.append({"cT": cT, "wada": wa, "bT": bt})
    resM = _run(ncM, in_maps)
    mod = np.zeros((B, DEPTH, 9 * D), f32)
    for core in range(NCORE):
        mp = resM[core]["modp"].reshape(128, DEPTH, MCH, B)
        mod[:, :, core * MCOLS:(core + 1) * MCOLS] = mp.transpose(3, 1, 2, 0).reshape(B, DEPTH, MCOLS)
    modT = [[np.ascontiguousarray(fm(mod[b, l].reshape(9, D))).reshape(128, 9 * KC) for l in range(DEPTH)] for b in range(B)]
    ngT = [np.ascontiguousarray(fm(np.asarray(norm_g[l], f32))).reshape(128, 3 * KC) for l in range(DEPTH)]
    rm = rope_perm()
    ropes = [rope_tables(np.arange(i * NT, (i + 1) * NT)) for i in range(4)]
    aconst = attn_consts()

    xT = [np.ascontiguousarray(x[core // 4, (core % 4) * NT:(core % 4 + 1) * NT].T) for core in range(NCORE)]
    oT = None
    for l in range(DEPTH + 1):
        phases = []
        if l > 0:
            phases += ["wo", "ffn_b"]
        if l < DEPTH:
            phases += ["ffn_a", "proj", "xout"]
        else:
            phases += ["final"]
        ncT = _prog("T" + ",".join(phases), lambda: build_T(phases))
        in_maps = []
        for core in range(NCORE):
            b, i = core // 4, core % 4
            m = {"xT_in": xT[core]}
            if l > 0:
                m["mod_b"] = modT[b][l - 1]
                m["ng_b"] = ngT[l - 1]
                m["w_o"] = np.asarray(w_o[l - 1], f32)
                m["oT"] = oT[core]
                m["wgu_b"] = np.asarray(ffn_w_gu[l - 1, 1], f32)
                m["wd_b"] = np.asarray(ffn_w_d[l - 1, 1], f32)
            if l < DEPTH:
                m["mod_a"] = modT[b][l]
                m["ng_a"] = ngT[l]
                m["wgu_a"] = np.asarray(ffn_w_gu[l, 0], f32)
                m["wd_a"] = np.asarray(ffn_w_d[l, 0], f32)
                m["w_in"] = np.asarray(w_in[l], f32)
                m["rope_d"] = ropes[i]
                m["rm_d"] = rm
            else:
                m["fg_d"] = np.ascontiguousarray(fm(np.asarray(final_g, f32)))
            in_maps.append(m)
        resT = _run(ncT, in_maps)
        if l == DEPTH:
            out = np.zeros((B, S, D), f32)
            for core in range(NCORE):
                b, i = core // 4, core % 4
                out[b, i * NT:(i + 1) * NT, :] = resT[core]["yT"].T
            return out
        xT = [resT[core]["xT_out"] for core in range(NCORE)]
        ncA = _prog("A", build_A)
        lam_init = 0.8 - 0.6 * math.exp(-0.3 * l)
        in_maps = []
        for core in range(NCORE):
            b, hg = core // 4, core % 4
            g = hg // 2
            qk = np.concatenate([resT[b * 4 + i]["qk"] for i in range(4)], axis=1)
            vv = np.concatenate([resT[b * 4 + i]["vv"] for i in range(4)], axis=0)
            gt = np.concatenate([resT[b * 4 + i]["gt"] for i in range(4)], axis=1)

            def ch(i0, n=1):
                return qk[i0 * 128:(i0 + n) * 128]
            own = [2 * hg, 2 * hg + 1]
            oth = [h for h in range(4 * g, 4 * g + 4) if h not in own]
            m = dict(aconst)
            m["nq4"] = np.concatenate([ch(16 + h) for h in own + oth], 0)
            m["kc"] = ch(24 + g)
            m["vc"] = ch(30 + g)
            m["ks"] = ch(26 + g)
            m["kw"] = ch(28 + g)
            m["vs"] = vv[:, 1024 + 128 * g:1024 + 128 * (g + 1)]
            m["vw"] = vv[:, 1280 + 128 * g:1280 + 128 * (g + 1)]
            m["dq"] = ch(2 * hg, 2)
            m["dk"] = ch(8 + 2 * hg, 2)
            m["dv"] = vv[:, 256 * hg:256 * (hg + 1)]
            m["g6"] = gt[6 * hg:6 * hg + 6]
            m["w1"] = np.asarray(cmp_w1[l], f32).reshape(8192, 256)
            m["w2"] = np.asarray(cmp_w2[l], f32).reshape(512, 128)
            pe = np.asarray(cmp_pe[l], f32)
            m["peT"] = np.concatenate([pe[0].T, pe[1].T], 1)
            m["lam"] = np.asarray(diff_lam[l], f32).reshape(256)
            m["subln"] = np.asarray(diff_subln[l], f32).reshape(128, 1)
            m["lamc"] = np.tile(np.array([[lam_init, 1.0 - lam_init]], f32), (128, 1))
            in_maps.append({k: np.ascontiguousarray(v) for k, v in m.items()})
        resA = _run(ncA, in_maps)
        oT = []
        for core in range(NCORE):
            b, i = core // 4, core % 4
            full = np.zeros((D, NT), dtype=NPBF)
            for hg in range(4):
                o = resA[b * 4 + hg]["oT"][:, i * NT:(i + 1) * NT]
                full[256 * hg:256 * (hg + 1)] = o[0:256]
                full[1024 + 256 * hg:1024 + 256 * (hg + 1)] = o[256:512]
            oT.append(full)
```

```python
import math
from contextlib import ExitStack
import numpy as np
import ml_dtypes
import concourse.bass as bass
import concourse.mybir as mybir
import concourse.bass_utils as bass_utils

F32 = mybir.dt.float32
BF16 = mybir.dt.bfloat16
AF = mybir.ActivationFunctionType
ALU = mybir.AluOpType
NPBF = ml_dtypes.bfloat16

D = 2048
B = 2
S = 4096
DEPTH = 4
DFF = 5632
NCORE = 8
NT = 1024
KC = D // 128
INC = 5656
EPS = 1e-6
ROPE_THETA = 500000.0


class Prog:
    def __init__(self, nc, stack):
        self.nc = nc
        self.stack = stack
        self.E = dict(pe=nc.tensor, act=nc.scalar, dve=nc.vector, pool=nc.gpsimd, sp=nc.sync)
        self.semobj = {}
        self.semval = {}
        for k in self.E:
            self.semobj["e_" + k] = stack.enter_context(nc.semaphore("e_" + k))
            self.semval["e_" + k] = 0
        self.seen = {k: {} for k in self.E}
        self.R = {}
        self.nops = 0

    def sb(self, name, shape, dt):
        return self.stack.enter_context(self.nc.sbuf_tensor(name, shape, dt))

    def psum(self, name, shape, dt):
        return self.stack.enter_context(self.nc.psum_tensor(name, shape, dt))

    def dsem(self, name):
        if name not in self.semobj:
            self.semobj[name] = self.stack.enter_context(self.nc.semaphore(name))
            self.semval[name] = 0
        return name

    def _wait(self, eng, dep, raw):
        name, val = dep
        if name == "e_" + eng:
            if eng == "pe" or not raw:
                return
        if self.seen[eng].get(name, 0) >= val:
            return
        self.E[eng].wait_ge(self.semobj[name], val)
        self.seen[eng][name] = val
        self.nops += 1

    def _hazards(self, eng, r, w):
        for res in r:
            st = self.R.get(res)
            if st and st[0]:
                self._wait(eng, st[0], True)
            if st and res[0] == "ps":
                for nm, v in st[1].items():
                    self._wait(eng, (nm, v), False)
        for res in w:
            st = self.R.get(res)
            if st:
                if st[0]:
                    self._wait(eng, st[0], False)
                for nm, v in st[1].items():
                    self._wait(eng, (nm, v), False)

    def _record(self, tok, r, w):
        for res in r:
            st = self.R.setdefault(res, [None, {}])
            if st[1].get(tok[0], 0) < tok[1]:
                st[1][tok[0]] = tok[1]
        for res in w:
            self.R[res] = [tok, {}]

    def op(self, eng, fn, r=(), w=(), signal=True):
        self._hazards(eng, r, w)
        ins = fn(self.E[eng])
        self.nops += 1
        name = "e_" + eng
        if signal:
            self.semval[name] += 1
            ins.then_inc(self.semobj[name], 1)
            tok = (name, self.semval[name])
        else:
            tok = (name, self.semval[name] + 1)
        self._record(tok, r, w)

    def dma(self, eng, out, in_, sem, r=(), w=()):
        self._hazards(eng, r, w)
        self.dsem(sem)
        self.E[eng].dma_start(out=out, in_=in_).then_inc(self.semobj[sem], 16)
        self.nops += 1
        self.semval[sem] += 16
        self._record((sem, self.semval[sem]), r, w)

    def dma_batch(self, eng, sem, lst):
        self.dsem(sem)
        for (out, in_, r, w) in lst:
            self._hazards(eng, r, w)
        for (out, in_, r, w) in lst:
            self.E[eng].dma_start(out=out, in_=in_).then_inc(self.semobj[sem], 16)
            self.nops += 1
            self.semval[sem] += 16
        for (out, in_, r, w) in lst:
            self._record((sem, self.semval[sem]), r, w)

    def barrier(self, eng):
        for name, val in self.semval.items():
            if val > 0 and name != "e_" + eng and self.seen[eng].get(name, 0) < val:
                self.E[eng].wait_ge(self.semobj[name], val)
                self.seen[eng][name] = val
                self.nops += 1

    def finish(self, eng, sems):
        for s in sems:
            if self.semval.get(s, 0) > 0:
                self.E[eng].wait_ge(self.semobj[s], self.semval[s])


class Item:
    def __init__(self, cls, nslots, load, compute):
        self.cls, self.nslots, self.load, self.compute = cls, nslots, load, compute


def run_stream(rings, items):
    n = len(items)
    slot_of, preds = [], []
    for i, it in enumerate(items):
        pr = set()
        s = []
        if it.cls is not None:
            ring = rings[it.cls]
            for k in range(it.nslots):
                sl = (ring["next"] + k) % ring["size"]
                if ring["last"][sl] is not None:
                    pr.add(ring["last"][sl])
                ring["last"][sl] = i
                s.append(sl)
            ring["next"] = (ring["next"] + it.nslots) % ring["size"]
        slot_of.append(s)
        preds.append(pr)
    loaded = [False] * n
    state = {"j": 0}

    def try_loads(done):
        while state["j"] < n:
            j = state["j"]
            if all(p <= done for p in preds[j]):
                if items[j].load is not None:
                    items[j].load(slot_of[j])
                loaded[j] = True
                state["j"] += 1
            else:
                break

    try_loads(-1)
    for i in range(n):
        assert loaded[i]
        items[i].compute(slot_of[i])
        try_loads(i)


class TK:
    pass


def t_alloc(P, nt):
    T = TK()
    T.nt = nt
    T.nh = nt // 512
    T.xT = P.sb("xT", [128, KC, nt], F32)
    T.hT = P.sb("hT", [128, KC, nt], BF16)
    T.act = P.sb("act", [128, 2, 4, nt], BF16)
    T.wA = P.sb("wA", [128, 4, KC, 256], BF16)
    T.wD = P.sb("wD", [128, 4, D], BF16)
    T.mod = P.sb("mod", [128, 9, KC], F32)
    T.ng = P.sb("ng", [128, 3, KC], F32)
    T.Acoef = P.sb("Acoef", [128, 3, KC], F32)
    T.hgate = P.sb("hgate", [128, 3, KC], F32)
    T.rstd = P.sb("rstd", [128, 512], F32)
    T.sq = P.sb("sq", [128, 2, 512], F32)
    T.tmp = P.sb("tmp", [128, 2, 512], F32)
    T.ones = P.sb("ones", [128, 128], F32)
    T.epsb = P.sb("epsb", [128, 1], F32)
    T.ps = [P.psum("ps%d" % i, [128, 512], F32) for i in range(8)]
    T.rings = {"A": dict(size=4, next=0, last=[None] * 4), "D": dict(size=4, next=0, last=[None] * 4)}
    P.op("dve", lambda e: e.memset(T.ones[:], 1.0), w=[("ones",)])
    P.op("dve", lambda e: e.memset(T.epsb[:], EPS), w=[("epsb",)])
    return T


def t_load_mod(P, T, mod_d, ng_d):
    P.dma("sp", T.mod[:].rearrange("p a c -> p (a c)"), mod_d, "ld_mod", w=[("mod",)])
    P.dma("sp", T.ng[:].rearrange("p a c -> p (a c)"), ng_d, "ld_ng", w=[("ng",)])
    for j in range(3):
        P.op("dve", lambda e: e.scalar_tensor_tensor(out=T.Acoef[:, j, :], in0=T.mod[:, 3 * j + 1, :], scalar=1.0,
                                                     in1=T.ng[:, j, :], op0=ALU.add, op1=ALU.mult),
             r=[("mod",), ("ng",)], w=[("Acoef", j)])
        gs = 1.0 if j == 1 else 0.5
        P.op("dve", lambda e: e.tensor_scalar(out=T.hgate[:, j, :], in0=T.mod[:, 3 * j + 2, :], scalar1=gs, scalar2=None,
                                              op0=ALU.mult),
             r=[("mod",)], w=[("hgate", j)])


def t_norm(P, T, j, out_kind="h", gvec=None, outbuf=None):
    for half in range(T.nh):
        cols = slice(half * 512, (half + 1) * 512)
        bank = T.ps[4 + (half % 2)]
        bres = ("ps", 4 + (half % 2))
        for c in range(KC):
            sq = T.sq[:, c % 2, :]
            P.op("act", lambda e: e.activation(out=sq, in_=T.xT[:, c, cols], func=AF.Square),
                 r=[("x", c, half)], w=[("sq", c % 2)])
            P.op("pe", lambda e: e.matmul(bank[:], T.ones[:], sq, start=(c == 0), stop=(c == KC - 1)),
                 r=[("sq", c % 2), ("ones",)], w=[bres])
        P.op("act", lambda e: e.activation(out=T.rstd[:], in_=bank[:], func=AF.Ln, scale=1.0 / D, bias=T.epsb[:]),
             r=[bres, ("epsb",)], w=[("rstd",)])
        P.op("act", lambda e: e.activation(out=T.rstd[:], in_=T.rstd[:], func=AF.Exp, scale=-0.5),
             r=[("rstd",)], w=[("rstd",)])
        for c in range(KC):
            tmp = T.tmp[:, c % 2, :]
            if out_kind == "h":
                P.op("dve", lambda e: e.scalar_tensor_tensor(out=tmp, in0=T.xT[:, c, cols], scalar=T.Acoef[:, j, c:c + 1],
                                                             in1=T.rstd[:], op0=ALU.mult, op1=ALU.mult),
                     r=[("x", c, half), ("rstd",), ("Acoef", j)], w=[("tmp", c % 2)])
                P.op("act", lambda e: e.activation(out=T.hT[:, c, cols], in_=tmp, func=AF.Identity,
                                                   bias=T.mod[:, 3 * j, c:c + 1], scale=1.0),
                     r=[("tmp", c % 2), ("mod",)], w=[("h", c, half)])
            else:
                P.op("dve", lambda e: e.scalar_tensor_tensor(out=tmp, in0=T.xT[:, c, cols], scalar=gvec[:, c:c + 1],
                                                             in1=T.rstd[:], op0=ALU.mult, op1=ALU.mult),
                     r=[("x", c, half), ("rstd",), ("fg",)], w=[("tmp", c % 2)])
                P.dma("sp", outbuf[c * 128:(c + 1) * 128, cols], tmp, "st_tmp%d" % (c % 2), r=[("tmp", c % 2)])


def t_slabA_load(P, T, slot, src):
    ncols = src.shape[1]
    P.dma("pool", T.wA[:, slot, :, 0:ncols], src.rearrange("(kc p) n -> p kc n", p=128), "ldA%d" % slot,
          w=[("A", slot)])


def ffn_items(P, T, wgu, wd, j):
    items = []
    NP = DFF // 512
    nh = T.nh

    def mk_gu(p, s):
        f0 = (p * 4 + s * 2) * 128

        def load(slots):
            t_slabA_load(P, T, slots[0], wgu[:, f0:f0 + 256])
            t_slabA_load(P, T, slots[1], wgu[:, DFF + f0:DFF + f0 + 256])

        def compute(slots):
            for fc in range(2):
                cj = s * 2 + fc
                for kind in range(2):
                    sl = slots[kind]
                    for half in range(nh):
                        bank = kind * 2 + (half % 2)
                        cols = slice(half * 512, (half + 1) * 512)
                        for kc in range(KC):
                            P.op("pe", lambda e: e.matmul(T.ps[bank][:], T.wA[:, sl, kc, fc * 128:(fc + 1) * 128],
                                                          T.hT[:, kc, cols], start=(kc == 0), stop=(kc == KC - 1)),
                                 r=[("A", sl), ("h", kc, half)], w=[("ps", bank)], signal=(kc == KC - 1))
                        if kind == 0:
                            P.op("act", lambda e: e.activation(out=T.sq[:, half % 2, :], in_=T.ps[bank][:], func=AF.Silu),
                                 r=[("ps", bank)], w=[("sq", half % 2)])
                        else:
                            P.op("dve", lambda e: e.tensor_tensor(out=T.act[:, p % 2, cj, cols], in0=T.ps[bank][:],
                                                                  in1=T.sq[:, half % 2, :], op=ALU.mult),
                                 r=[("ps", bank), ("sq", half % 2)], w=[("act", p % 2, cj, half)])
        return Item("A", 2, load, compute)

    def mk_down(p):
        def load(slots):
            for jj in range(4):
                r0 = (p * 4 + jj) * 128
                P.dma("pool", T.wD[:, slots[jj], :], wd[r0:r0 + 128, :], "ldD%d" % slots[jj], w=[("D", slots[jj])])

        def compute(slots):
            ctr = 0
            for mo in range(KC):
                for half in range(nh):
                    bank = 4 + (ctr % 4)
                    ctr += 1
                    cols = slice(half * 512, (half + 1) * 512)
                    for jj in range(4):
                        P.op("pe", lambda e: e.matmul(T.ps[bank][:], T.wD[:, slots[jj], mo * 128:(mo + 1) * 128],
                                                      T.act[:, p % 2, jj, cols], start=(jj == 0), stop=(jj == 3)),
                             r=[("D", slots[jj]), ("act", p % 2, jj, half)], w=[("ps", bank)], signal=(jj == 3))
                    P.op("dve", lambda e: e.scalar_tensor_tensor(out=T.xT[:, mo, cols], in0=T.ps[bank][:],
                                                                 scalar=T.hgate[:, j, mo:mo + 1], in1=T.xT[:, mo, cols],
                                                                 op0=ALU.mult, op1=ALU.add),
                         r=[("ps", bank), ("hgate", j), ("x", mo, half)], w=[("x", mo, half)])
        return Item("D", 4, load, compute)

    items.append(Item(None, 0, None, lambda s: t_norm(P, T, j)))
    order = []
    for p in range(NP):
        order.append(("g", p))
        if p >= 1:
            order.append(("d", p - 1))
    order.append(("d", NP - 1))
    for kind, p in order:
        if kind == "g":
            items.append(mk_gu(p, 0))
            items.append(mk_gu(p, 1))
        else:
            items.append(mk_down(p))
    return items


PROJ_SLABS = ([("dq", "fm", "A", 0 + 2 * i) for i in range(4)] + [("dk", "fm", "A", 8 + 2 * i) for i in range(4)] +
              [("dv", "tm", None, 256 * i) for i in range(4)] + [("nq", "fm", "B", 16 + 2 * i) for i in range(4)] +
              [("kc", "fm", None, 24), ("vc", "fm", None, 30), ("ks", "fm", "B", 26), ("vs", "tm", None, 1024),
               ("kw", "fm", "B", 28), ("vw", "tm", None, 1280), ("gl", "gl", None, 0)])
NQK = 32
NV = 1536


def t_alloc_proj(P, T):
    nt = T.nt
    T.rope = P.sb("rope", [128, 4, nt], F32)
    T.Rm = P.sb("Rm", [128, 2, 128], BF16)
    T.xb = P.sb("xb", [128, 2, 512], BF16)
    T.t2 = P.sb("t2", [128, 2, 512], F32)
    T.ost = P.sb("ost", [128, 4, 512], BF16)
    T.vst = P.sb("vst", [128, 2, 256], BF16)
    T.cnt = dict(ost=0, vst=0, gst=0, xb=0, pa=0, pb=0)


def t_load_proj_consts(P, T, rope_d, rm_d):
    P.dma_batch("sp", "ld_rope", [(T.rope[:, i, :], rope_d[i], [], [("rope", i)]) for i in range(4)])
    P.dma_batch("sp", "ld_rm", [(T.Rm[:, i, :], rm_d[i], [], [("Rm", i)]) for i in range(2)])


def proj_items(P, T, w_in, qk_d, vv_d, gt_d):
    items = [Item(None, 0, None, lambda s: t_norm(P, T, 1))]
    nh = T.nh

    def mk(sidx):
        name, lay, rt, base = PROJ_SLABS[sidx]
        c0 = 256 * sidx
        ncols = min(256, INC - c0)

        def load(slots):
            t_slabA_load(P, T, slots[0], w_in[:, c0:c0 + ncols])

        def compute(slots):
            sl = slots[0]
            if lay == "tm":
                for t in range(T.nt // 128):
                    bank = T.cnt["pa"] % 4
                    T.cnt["pa"] += 1
                    half = t // 4
                    for kc in range(KC):
                        P.op("pe", lambda e: e.matmul(T.ps[bank][:, 0:256], T.hT[:, kc, t * 128:(t + 1) * 128],
                                                      T.wA[:, sl, kc, 0:256], start=(kc == 0), stop=(kc == KC - 1)),
                             r=[("A", sl), ("h", kc, half)], w=[("ps", bank)], signal=(kc == KC - 1))
                    vs = T.cnt["vst"] % 2
                    T.cnt["vst"] += 1
                    P.op("act", lambda e: e.activation(out=T.vst[:, vs, :], in_=T.ps[bank][:, 0:256], func=AF.Copy),
                         r=[("ps", bank)], w=[("vst", vs)])
                    P.dma("sp", vv_d[t * 128:(t + 1) * 128, base:base + 256], T.vst[:, vs, :], "st_v%d" % vs,
                          r=[("vst", vs)])
                return
            if lay == "gl":
                for half in range(nh):
                    bank = T.cnt["pa"] % 4
                    T.cnt["pa"] += 1
                    cols = slice(half * 512, (half + 1) * 512)
                    for kc in range(KC):
                        P.op("pe", lambda e: e.matmul(T.ps[bank][:], T.wA[:, sl, kc, 0:128], T.hT[:, kc, cols],
                                                      start=(kc == 0), stop=(kc == KC - 1)),
                             r=[("A", sl), ("h", kc, half)], w=[("ps", bank)], signal=(kc == KC - 1))
                    gs = T.cnt["gst"] % 2
                    T.cnt["gst"] += 1
                    P.op("act", lambda e: e.activation(out=T.t2[0:24, gs, :], in_=T.ps[bank][0:24, :], func=AF.Sigmoid),
                         r=[("ps", bank)], w=[("t2", gs)])
                    P.dma("sp", gt_d[:, cols], T.t2[0:24, gs, :], "st_g%d" % gs, r=[("t2", gs)])
                return
            for fc in range(2):
                for half in range(nh):
                    bank = T.cnt["pa"] % 4
                    T.cnt["pa"] += 1
                    cols = slice(half * 512, (half + 1) * 512)
                    for kc in range(KC):
                        P.op("pe", lambda e: e.matmul(T.ps[bank][:], T.wA[:, sl, kc, fc * 128:(fc + 1) * 128],
                                                      T.hT[:, kc, cols], start=(kc == 0), stop=(kc == KC - 1)),
                             r=[("A", sl), ("h", kc, half)], w=[("ps", bank)], signal=(kc == KC - 1))
                    os_ = T.cnt["ost"] % 4
                    T.cnt["ost"] += 1
                    if rt is None:
                        P.op("act", lambda e: e.activation(out=T.ost[:, os_, :], in_=T.ps[bank][:], func=AF.Copy),
                             r=[("ps", bank)], w=[("ost", os_)])
                    else:
                        ri = 0 if rt == "A" else 1
                        xs = T.cnt["xb"] % 2
                        T.cnt["xb"] += 1
                        b2 = 4 + (T.cnt["pb"] % 4)
                        T.cnt["pb"] += 1
                        P.op("act", lambda e: e.activation(out=T.xb[:, xs, :], in_=T.ps[bank][:], func=AF.Copy),
                             r=[("ps", bank)], w=[("xb", xs)])
                        import os
                        step = int(os.environ.get("ROPE_STEP", "4"))
                        if step >= 2:
                            P.op("pe", lambda e: e.matmul(T.ps[b2][:], T.Rm[:, ri, :], T.xb[:, xs, :], start=True, stop=True),
                                 r=[("xb", xs), ("Rm", ri)], w=[("ps", b2)])
                        if step >= 3:
                            P.op("dve", lambda e: e.tensor_tensor(out=T.tmp[:, xs, :], in0=T.ps[bank][:],
                                                                  in1=T.rope[:, 2 * ri, cols], op=ALU.mult),
                                 r=[("ps", bank), ("rope", 2 * ri), ("xb", xs)], w=[("tmp", xs)])
                            P.op("dve", lambda e: e.tensor_tensor(out=T.t2[:, xs, :], in0=T.ps[b2][:],
                                                                  in1=T.rope[:, 2 * ri + 1, cols], op=ALU.mult),
                                 r=[("ps", b2), ("rope", 2 * ri + 1)], w=[("t2", xs)])
                        if step >= 4:
                            P.op("dve", lambda e: e.tensor_tensor(out=T.ost[:, os_, :], in0=T.tmp[:, xs, :],
                                                                  in1=T.t2[:, xs, :], op=ALU.add),
                                 r=[("tmp", xs), ("t2", xs)], w=[("ost", os_)])
                        else:
                            P.op("act", lambda e: e.activation(out=T.ost[:, os_, :], in_=T.xb[:, xs, :], func=AF.Copy),
                                 r=[("xb", xs)], w=[("ost", os_)])
                    P.dma("sp", qk_d[(base + fc) * 128:(base + fc + 1) * 128, cols], T.ost[:, os_, :], "st_o%d" % os_, r=[("ost", os_)])
        return Item("A", 1, load, compute)

    import os
    kinds = os.environ.get("PROJ_KINDS", "fmrope,fmplain,tm,gl").split(",")
    for sidx in range(len(PROJ_SLABS)):
        name, lay, rt, base = PROJ_SLABS[sidx]
        k = "tm" if lay == "tm" else ("gl" if lay == "gl" else ("fmrope" if rt else "fmplain"))
        if k in kinds:
            items.append(mk(sidx))
    return items


def wo_items(P, T, w_o, oT_d):
    items = []

    def ld(s):
        P.dma_batch("sp", "ld_o", [(T.hT[:, c, :], oT_d[c * 128:(c + 1) * 128, :], [],
                                    [("h", c, half) for half in range(T.nh)]) for c in range(KC)])
    items.append(Item(None, 0, None, ld))

    def mk(si):
        def load(slots):
            t_slabA_load(P, T, slots[0], w_o[:, si * 256:(si + 1) * 256])

        def compute(slots):
            sl = slots[0]
            for fc in range(2):
                mo = si * 2 + fc
                for half in range(T.nh):
                    bank = 4 + (T.cnt["pb"] % 4)
                    T.cnt["pb"] += 1
                    cols = slice(half * 512, (half + 1) * 512)
                    for kc in range(KC):
                        P.op("pe", lambda e: e.matmul(T.ps[bank][:], T.wA[:, sl, kc, fc * 128:(fc + 1) * 128],
                                                      T.hT[:, kc, cols], start=(kc == 0), stop=(kc == KC - 1)),
                             r=[("A", sl), ("h", kc, half)], w=[("ps", bank)], signal=(kc == KC - 1))
                    P.op("dve", lambda e: e.scalar_tensor_tensor(out=T.xT[:, mo, cols], in0=T.ps[bank][:],
                                                                 scalar=T.hgate[:, 1, mo:mo + 1], in1=T.xT[:, mo, cols],
                                                                 op0=ALU.mult, op1=ALU.add),
                         r=[("ps", bank), ("hgate", 1), ("x", mo, half)], w=[("x", mo, half)])
        return Item("A", 1, load, compute)
    for si in range(8):
        items.append(mk(si))
    return items


def build_T(phases, nt=NT):
    nc = bass.Bass("TRN2", target_bir_lowering=False)
    stack = ExitStack()
    with stack:
        P = Prog(nc, stack)
        T = t_alloc(P, nt)
        T.cnt = dict(ost=0, vst=0, gst=0, xb=0, pa=0, pb=0)
        xin = nc.dram_tensor("xT_in", [D, nt], F32, kind="ExternalInput").ap()
        P.dma_batch("sp", "ld_x", [(T.xT[:, c, :], xin[c * 128:(c + 1) * 128, :], [],
                                    [("x", c, h) for h in range(T.nh)]) for c in range(KC)])
        items = []
        out_sems = []
        lay = 0
        if "wo" in phases or "ffn_b" in phases:
            modB = nc.dram_tensor("mod_b", [128, 9 * KC], F32, kind="ExternalInput").ap()
            ngB = nc.dram_tensor("ng_b", [128, 3 * KC], F32, kind="ExternalInput").ap()
            items.append(Item(None, 0, None, lambda s: t_load_mod(P, T, modB, ngB)))
        if "wo" in phases:
            w_o = nc.dram_tensor("w_o", [D, D], F32, kind="ExternalInput").ap()
            oT = nc.dram_tensor("oT", [D, nt], BF16, kind="ExternalInput").ap()
            items += wo_items(P, T, w_o, oT)
        if "ffn_b" in phases:
            wgu_b = nc.dram_tensor("wgu_b", [D, 2 * DFF], F32, kind="ExternalInput").ap()
            wd_b = nc.dram_tensor("wd_b", [DFF, D], F32, kind="ExternalInput").ap()
            items += ffn_items(P, T, wgu_b, wd_b, 2)
        if "ffn_a" in phases or "proj" in phases:
            modA = nc.dram_tensor("mod_a", [128, 9 * KC], F32, kind="ExternalInput").ap()
            ngA = nc.dram_tensor("ng_a", [128, 3 * KC], F32, kind="ExternalInput").ap()
            items.append(Item(None, 0, None, lambda s: t_load_mod(P, T, modA, ngA)))
        if "ffn_a" in phases:
            wgu_a = nc.dram_tensor("wgu_a", [D, 2 * DFF], F32, kind="ExternalInput").ap()
            wd_a = nc.dram_tensor("wd_a", [DFF, D], F32, kind="ExternalInput").ap()
            items += ffn_items(P, T, wgu_a, wd_a, 0)
        if "proj" in phases:
            t_alloc_proj(P, T)
            w_in = nc.dram_tensor("w_in", [D, INC], F32, kind="ExternalInput").ap()
            rope_d = nc.dram_tensor("rope_d", [4, 128, nt], F32, kind="ExternalInput").ap()
            rm_d = nc.dram_tensor("rm_d", [2, 128, 128], BF16, kind="ExternalInput").ap()
            qk_d = nc.dram_tensor("qk", [NQK * 128, nt], BF16, kind="ExternalOutput").ap()
            vv_d = nc.dram_tensor("vv", [nt, NV], BF16, kind="ExternalOutput").ap()
            gt_d = nc.dram_tensor("gt", [24, nt], F32, kind="ExternalOutput").ap()
            t_load_proj_consts(P, T, rope_d, rm_d)
            items += proj_items(P, T, w_in, qk_d, vv_d, gt_d)
        if "final" in phases:
            fg_d = nc.dram_tensor("fg_d", [128, KC], F32, kind="ExternalInput").ap()
            yT = nc.dram_tensor("yT", [D, nt], F32, kind="ExternalOutput").ap()
            T.fg = P.sb("fg", [128, KC], F32)
            P.dma("sp", T.fg[:], fg_d, "ld_fg", w=[("fg",)])
            items.append(Item(None, 0, None, lambda s: t_norm(P, T, 0, out_kind="final", gvec=T.fg, outbuf=yT)))
        if "xout" in phases:
            xout = nc.dram_tensor("xT_out", [D, nt], F32, kind="ExternalOutput").ap()

            def xo(s):
                P.dma_batch("sp", "st_x", [(xout[c * 128:(c + 1) * 128, :], T.xT[:, c, :],
                                            [("x", c, h) for h in range(T.nh)], []) for c in range(KC)])
            items.append(Item(None, 0, None, xo))
        run_stream(T.rings, items)
        P.finish("sp", [s for s in P.semobj if s.startswith("st_")])
        print("T kernel", phases, "ops", P.nops)
    return nc


def rope_tables(pos):
    nt = len(pos)
    out = np.zeros((4, 128, nt), np.float64)
    out[0] = 1.0
    out[2] = 1.0
    p = pos.astype(np.float64)
    for ti, (blk, half) in enumerate(((64, 8), (128, 16))):
        inv = np.exp(-math.log(ROPE_THETA) * np.arange(half, dtype=np.float32) / half).astype(np.float32)
        ang = (pos.astype(np.float32)[None, :] * inv[:, None]).astype(np.float32).astype(np.float64)
        for m in range(128):
            j = m % blk
            if j < half:
                out[2 * ti, m] = np.cos(ang[j])
                out[2 * ti + 1, m] = -np.sin(ang[j])
            elif j < 2 * half:
                out[2 * ti, m] = np.cos(ang[j - half])
                out[2 * ti + 1, m] = np.sin(ang[j - half])
    return out.astype(np.float32)


def rope_perm():
    rm = np.zeros((2, 128, 128), np.float32)
    for ti, (blk, half) in enumerate(((64, 8), (128, 16))):
        for m in range(128):
            j = m % blk
            if j < half:
                rm[ti, m + half, m] = 1.0
            elif j < 2 * half:
                rm[ti, m - half, m] = 1.0
    return rm.astype(NPBF)


def fm(v):
    v = np.asarray(v)
    lead = v.shape[:-1]
    n = v.shape[-1] // 128
    v = v.reshape(lead + (n, 128))
    return np.ascontiguousarray(np.moveaxis(v, -1, 0))


NSA_SCALE = 128 ** -0.5
DIFF_SCALE = 64 ** -0.5
NEGB = -30000.0
NCMP = 255
AX = mybir.AxisListType


def build_A(nqb=8):
    nc = bass.Bass("TRN2", target_bir_lowering=False)
    stack = ExitStack()
    SQ = nqb * 512
    with stack:
        P = Prog(nc, stack)

        def din(name, shape, dt):
            return nc.dram_tensor(name, shape, dt, kind="ExternalInput").ap()
        nq4_d = din("nq4", [512, S], BF16)
        kc_d = din("kc", [128, S], BF16)
        vc_d = din("vc", [128, S], BF16)
        ks_d = din("ks", [128, S], BF16)
        kw_d = din("kw", [128, S], BF16)
        vs_d = din("vs", [S, 128], BF16)
        vw_d = din("vw", [S, 128], BF16)
        dq_d = din("dq", [256, S], BF16)
        dk_d = din("dk", [256, S], BF16)
        dv_d = din("dv", [S, 256], BF16)
        g6_d = din("g6", [6, S], F32)
        w1_d = din("w1", [8192, 256], F32)
        w2_d = din("w2", [512, 128], F32)
        peT_d = din("peT", [128, 64], F32)
        lam_d = din("lam", [256], F32)
        subln_d = din("subln", [128, 1], F32)
        lamc_d = din("lamc", [128, 2], F32)
        cmask_d = din("cmask", [256, S], BF16)
        ov_d = din("ov", [256, 64], F32)
        btile_d = din("btile", [128, 32 * 64], F32)
        E_d = din("Emat", [128, 32 * 128], BF16)
        wm_d = din("wm", [128, 8 * 512], BF16)
        selG_d = din("selG", [32, 6 * 128], F32)
        ident_d = din("ident", [128, 128], F32)
        oT_d = nc.dram_tensor("oT", [512, S], BF16, kind="ExternalOutput").ap()

        arena = P.sb("arena", [128, 40960], BF16)
        nq4 = arena[:, 0:16384].rearrange("p (a b) -> p a b", a=4)
        kcT = arena[:, 16384:20480]
        vcT = arena[:, 20480:24576]
        ksT = arena[:, 24576:28672]
        kwT = arena[:, 28672:32768]
        vs = arena[:, 32768:36864].rearrange("p (a b) -> p a b", a=32)
        vw = arena[:, 36864:40960].rearrange("p (a b) -> p a b", a=32)
        dqz = arena[:, 0:16384].rearrange("p (h c b) -> p h c b", h=2, c=2)
        dkT = arena[:, 16384:24576].rearrange("p (a b) -> p a b", a=2)
        dv = arena[:, 24576:32768].rearrange("p (a b) -> p a b", a=32)

        oacc = P.sb("oacc", [128, 2, 2, 512], F32)
        biasT = P.sb("biasT", [128, 2, 512], BF16)
        cm_sb = P.sb("cm_sb", [128, 2, 2, 512], BF16)
        bt_sb = P.sb("bt_sb", [128, 2, 4, 64], F32)
        g6_sb = P.sb("g6_sb", [32, 2, 512], F32)
        wm = P.sb("wm_sb", [128, 8, 512], BF16)
        Em = P.sb("Em", [128, 32, 128], BF16)
        ov = P.sb("ov_sb", [128, 2, 64], F32)
        selG = P.sb("selG_sb", [32, 6, 128], F32)
        identf = P.sb("identf", [128, 128], F32)
        onesb = P.sb("onesb", [128, 128], BF16)
        onesf = P.sb("onesf", [128, 128], F32)
        epsb = P.sb("epsb", [128, 1], F32)
        w1 = P.sb("w1_sb", [128, 32, 256], BF16)
        w2 = P.sb("w2_sb", [128, 2, 2, 128], BF16)
        peT = P.sb("peT_sb", [128, 2, 32], BF16)
        hidT = P.sb("hidT", [128, 2, 2, 256], BF16)
        cbias = P.sb("cbias", [128, 2, 2], F32)
        kcmpT = P.sb("kcmpT", [128, 256], BF16)
        vcmp = P.sb("vcmp", [128, 2, 128], BF16)
        pT = P.sb("pT", [128, 4, 512], BF16)
        em = P.sb("em", [128, 2, 512], F32)
        pf = P.sb("pf", [128, 2, 512], F32)
        pb = P.sb("pb", [128, 2, 512], BF16)
        psumT = P.sb("psumT", [128, 2, 512], F32)
        rden = P.sb("rden", [128, 2, 512], F32)
        tmpf = P.sb("tmpf", [128, 2, 512], F32)
        score = P.sb("score", [128, 64], F32)
        work = P.sb("work", [128, 64], F32)
        m8a = P.sb("m8a", [128, 8], F32)
        m8b = P.sb("m8b", [128, 8], F32)
        bval = P.sb("bval", [128, 128], F32)
        obf = P.sb("obf", [128, 2, 512], BF16)
        lamt = P.sb("lamt", [128, 256], F32)
        lprod = P.sb("lprod", [128, 2, 64], F32)
        lsc = P.sb("lsc", [128, 8], F32)
        subln = P.sb("subln_sb", [128, 1], F32)
        lamc = P.sb("lamc_sb", [128, 2], F32)
        ps = [P.psum("ps%d" % i, [128, 512], F32) for i in range(8)]
        cnt = dict(pT=0, ep=0, ob=0, st=0, ds=0)

        def mm(out, lhsT, rhs, start, stop, r, w, signal=True):
            P.op("pe", lambda e: e.matmul(out, lhsT, rhs, start=start, stop=stop), r=r, w=w, signal=signal)

        P.op("dve", lambda e: e.memset(onesb[:], 1.0), w=[("onesb",)])
        P.op("dve", lambda e: e.memset(onesf[:], 1.0), w=[("onesf",)])
        P.op("dve", lambda e: e.memset(epsb[:], EPS), w=[("epsb",)])
        P.op("dve", lambda e: e.memset(hidT[:], 0.0), w=[("hidT",)])
        P.op("dve", lambda e: e.memset(bval[:], 0.0), w=[("bval",)])
        P.op("dve", lambda e: e.memset(g6_sb[:], 0.0), w=[("g6", 0), ("g6", 1)])
        P.dma("sp", wm[:].rearrange("p a b -> p (a b)"), wm_d, "ld_wm", w=[("wm",)])
        P.dma("sp", Em[:].rearrange("p a b -> p (a b)"), E_d, "ld_E", w=[("Em",)])
        P.dma("sp", ov[:], ov_d.rearrange("(c p) n -> p c n", p=128), "ld_ov", w=[("ov",)])
        P.dma("sp", selG[:].rearrange("p a b -> p (a b)"), selG_d, "ld_selG", w=[("selG",)])
        P.dma("sp", identf[:], ident_d, "ld_ident", w=[("identf",)])
        P.dma("sp", subln[:], subln_d, "ld_subln", w=[("subln",)])
        P.dma("sp", lamc[:], lamc_d, "ld_lamc", w=[("lamc",)])
        P.dma("sp", lamt[:], lam_d.partition_broadcast(128), "ld_lam", w=[("lamt",)])
        P.dma_batch("sp", "ld_nsa", [
            (nq4[:, j, :], nq4_d[j * 128:(j + 1) * 128, :], [], [("nq4", j)]) for j in range(4)] + [
            (kcT, kc_d, [], [("kcT",)]), (vcT, vc_d, [], [("vcT",)]), (ksT, ks_d, [], [("ksT",)]),
            (kwT, kw_d, [], [("kwT",)]),
            (vs, vs_d.rearrange("(t p) n -> p t n", p=128), [], [("vs",)]),
            (vw, vw_d.rearrange("(t p) n -> p t n", p=128), [], [("vw",)])])
        P.dma("pool", w2[:].rearrange("p j h n -> p (j h) n"), w2_d.rearrange("(a p) n -> p a n", p=128), "ld_w2",
              w=[("w2",)])
        P.dma("pool", peT[:].rearrange("p j l -> p (j l)"), peT_d, "ld_pe", w=[("peT",)])

        for i in range(2):
            P.op("dve", lambda e: e.tensor_tensor(out=lprod[:, i, :], in0=lamt[:, 128 * i:128 * i + 64],
                                                  in1=lamt[:, 128 * i + 64:128 * i + 128], op=ALU.mult),
                 r=[("lamt",)], w=[("lprod", i)])
            P.op("dve", lambda e: e.reduce_sum(out=lsc[:, i:i + 1], in_=lprod[:, i, :], axis=AX.X),
                 r=[("lprod", i)], w=[("lsc", i)])
            P.op("act", lambda e: e.activation(out=lsc[:, 2 + i:3 + i], in_=lsc[:, i:i + 1], func=AF.Exp),
                 r=[("lsc", i)], w=[("lsc", 2 + i)])
        P.op("dve", lambda e: e.tensor_tensor(out=lsc[:, 4:5], in0=lsc[:, 2:3], in1=lsc[:, 3:4], op=ALU.subtract),
             r=[("lsc", 2), ("lsc", 3)], w=[("lsc", 4)])
        P.op("dve", lambda e: e.tensor_tensor(out=lsc[:, 4:5], in0=lsc[:, 4:5], in1=lamc[:, 0:1], op=ALU.add),
             r=[("lsc", 4), ("lamc",)], w=[("lsc", 4)])
        P.op("dve", lambda e: e.tensor_scalar(out=lsc[:, 5:6], in0=lsc[:, 4:5], scalar1=-1.0, scalar2=None, op0=ALU.mult),
             r=[("lsc", 4)], w=[("lsc", 5)])
        P.op("dve", lambda e: e.tensor_tensor(out=lsc[:, 6:7], in0=subln[:], in1=lamc[:, 1:2], op=ALU.mult),
             r=[("subln",), ("lamc",)], w=[("lsc", 6)])

        for j in range(2):
            XT, xres = (kcT, ("kcT",)) if j == 0 else (vcT, ("vcT",))
            P.dma("pool", w1[:], w1_d[j * 4096:(j + 1) * 4096, :].rearrange("(l p) n -> p l n", p=128), "ld_w1",
                  w=[("w1",)])
            for hc in range(2):
                for l in range(32):
                    mm(ps[2][:, hc:hc + 1], w1[:, l, hc * 128:(hc + 1) * 128], peT[:, j, l:l + 1], l == 0, l == 31,
                       r=[("w1",), ("peT",)], w=[("ps", 2)], signal=(l == 31))
            P.op("act", lambda e: e.activation(out=cbias[:, j, :], in_=ps[2][:, 0:2], func=AF.Copy),
                 r=[("ps", 2)], w=[("cbias", j)])
            for hc in range(2):
                for l in range(32):
                    mm(ps[hc][:, 0:NCMP], w1[:, l, hc * 128:(hc + 1) * 128], XT[:, l:l + 16 * (NCMP - 1) + 1:16], l == 0, l == 31,
                       r=[("w1",), xres], w=[("ps", hc)], signal=(l == 31))
                P.op("act", lambda e: e.activation(out=hidT[:, j, hc, 0:NCMP], in_=ps[hc][:, 0:NCMP], func=AF.Silu,
                                                   bias=cbias[:, j, hc:hc + 1]),
                     r=[("ps", hc), ("cbias", j)], w=[("hidT",)])
            if j == 0:
                for hc in range(2):
                    mm(ps[3][:, 0:256], w2[:, 0, hc, :], hidT[:, 0, hc, :], hc == 0, hc == 1,
                       r=[("w2",), ("hidT",)], w=[("ps", 3)], signal=(hc == 1))
                P.op("act", lambda e: e.activation(out=kcmpT[:], in_=ps[3][:, 0:256], func=AF.Copy),
                     r=[("ps", 3)], w=[("kcmpT",)])
            else:
                for cc in range(2):
                    for hc in range(2):
                        mm(ps[3][:, cc * 128:(cc + 1) * 128], hidT[:, 1, hc, cc * 128:(cc + 1) * 128], w2[:, 1, hc, :],
                           hc == 0, hc == 1, r=[("w2",), ("hidT",)], w=[("ps", 3)], signal=(hc == 1))
                P.op("act", lambda e: e.activation(out=vcmp[:].rearrange("p a b -> p (a b)"), in_=ps[3][:, 0:256],
                                                   func=AF.Copy),
                     r=[("ps", 3)], w=[("vcmp",)])

        gsb4 = P.sb("gsb4", [128, 4, 512], F32)
        rdenB = P.sb("rdenB", [128, 512], F32)
        deferred = []
        tile_clock = dict(i=0)

        def defer(delay, fn):
            deferred.append([tile_clock["i"] + delay, fn])

        def flush_deferred(all_=False):
            keep = []
            for due, fn in deferred:
                if all_ or due <= tile_clock["i"]:
                    fn()
                else:
                    keep.append([due, fn])
            deferred[:] = keep

        def gate_prefetch(hi, par, gidx, gslot):
            mm(ps[7][:], selG[:, hi * 3 + gidx, :], g6_sb[:, par, :], True, True,
               r=[("selG",), ("g6", par)], w=[("ps", 7)])
            P.op("act", lambda e: e.activation(out=gsb4[:, gslot, :], in_=ps[7][:], func=AF.Copy),
                 r=[("ps", 7)], w=[("gsb4", gslot)])

        def branch_finish(hi, par, accO, accD, gslot, first, es):
            if accD is not None:
                P.op("dve", lambda e: e.reciprocal(out=rden[:, es, :], in_=ps[accD][:]),
                     r=[("ps", accD)], w=[("rden", es)])
                P.op("dve", lambda e: e.tensor_tensor(out=rden[:, es, :], in0=rden[:, es, :], in1=gsb4[:, gslot, :], op=ALU.mult),
                     r=[("gsb4", gslot), ("rden", es)], w=[("rden", es)])
                fac, fres = rden[:, es, :], ("rden", es)
            else:
                fac, fres = gsb4[:, gslot, :], ("gsb4", gslot)
            if first:
                P.op("dve", lambda e: e.tensor_tensor(out=oacc[:, par, hi, :], in0=ps[accO][:], in1=fac, op=ALU.mult),
                     r=[("ps", accO), fres], w=[("oacc", par, hi)])
            else:
                P.op("dve", lambda e: e.tensor_tensor(out=tmpf[:, es, :], in0=ps[accO][:], in1=fac, op=ALU.mult),
                     r=[("ps", accO), fres], w=[("tmpf", es)])
                P.op("dve", lambda e: e.tensor_tensor(out=oacc[:, par, hi, :], in0=oacc[:, par, hi, :], in1=tmpf[:, es, :],
                                                      op=ALU.add),
                     r=[("oacc", par, hi), ("tmpf", es)], w=[("oacc", par, hi)])

        def make_tiles(kts, qT_ap, qres, kT, kres, vtile, vres, masks, scale, accO, accD, extra_bias=None, epilogue=None,
                       prologue=None, bias_res=None):
            n = len(kts)
            tiles = []
            for idx, kt in enumerate(kts):
                st = {}

                def emit_S(kt=kt, st=st, idx=idx):
                    if idx == 0 and prologue is not None:
                        prologue()
                    sbk = (0, 1, 6)[cnt["st"] % 3]
                    cnt["st"] += 1
                    st["sbk"] = sbk
                    ksl = slice(kt * 128, (kt + 1) * 128)
                    if extra_bias is None:
                        mm(ps[sbk][:], kT[:, ksl], qT_ap, True, True, r=[kres, qres], w=[("ps", sbk)])
                    else:
                        mm(ps[sbk][:], kT[:, ksl], qT_ap, True, False, r=[kres, qres], w=[("ps", sbk)], signal=False)
                        mm(ps[sbk][:], Em[:, kt, :], extra_bias, False, True, r=[("Em",), bias_res], w=[("ps", sbk)])

                def emit_rest(kt=kt, st=st, idx=idx):
                    sbk = st["sbk"]
                    sl = cnt["pT"] % 4
                    cnt["pT"] += 1
                    P.op("act", lambda e: e.activation(out=pT[:, sl, :], in_=ps[sbk][:], func=AF.Exp, scale=scale),
                         r=[("ps", sbk)], w=[("pT", sl)])
                    mk = masks(kt)
                    if mk is not None:
                        P.op("dve", lambda e: e.tensor_tensor(out=pT[:, sl, :], in0=pT[:, sl, :], in1=wm[:, mk, :], op=ALU.mult),
                             r=[("pT", sl), ("wm",)], w=[("pT", sl)])
                    mm(ps[accO][:], vtile(kt), pT[:, sl, :], idx == 0, idx == n - 1, r=[vres, ("pT", sl)], w=[("ps", accO)],
                       signal=(idx == n - 1))
                    mm(ps[accD][:], onesb[:], pT[:, sl, :], idx == 0, idx == n - 1, r=[("onesb",), ("pT", sl)],
                       w=[("ps", accD)], signal=(idx == n - 1))
                    if idx == n - 1 and epilogue is not None:
                        epilogue()
                tiles.append((emit_S, emit_rest))
            return tiles

        def run_tiles(tiles, extra=()):
            extra = list(extra)
            ne, nt_ = len(extra), len(tiles)
            done = 0
            if nt_ == 0:
                for fn in extra:
                    fn()
                return
            tiles[0][0]()
            if nt_ > 1:
                tiles[1][0]()
            for i in range(nt_):
                if i + 2 < nt_:
                    tiles[i + 2][0]()
                tiles[i][1]()
                tile_clock["i"] += 1
                flush_deferred()
                want = ((i + 1) * ne + nt_ - 1) // nt_
                while done < min(want, ne):
                    extra[done]()
                    done += 1
            while done < ne:
                extra[done]()
                done += 1
            flush_deferred(all_=True)

        def phaseB_stages(qb):
            par = qb % 2
            qc = slice(qb * 512, (qb + 1) * 512)
            ncc = 1 if qb <= 3 else 2
            stg = []

            def ld():
                P.dma("sp", cm_sb[:, par, :, :], cmask_d[:, qc].rearrange("(c p) n -> p c n", p=128), "ld_cm%d" % par,
                      w=[("cm", par)])
                P.dma("sp", bt_sb[:, par, :, :].rearrange("p a b -> p (a b)"), btile_d[:, qb * 256:(qb + 1) * 256],
                      "ld_bt%d" % par, w=[("bt", par)])
                P.dma("sp", g6_sb[0:6, par, :], g6_d[:, qc], "ld_g6%d" % par, w=[("g6", par)])
            stg.append(ld)
            for j in range(4):
                for cc in range(ncc):
                    def s1(j=j, cc=cc):
                        mm(ps[7][:], kcmpT[:, cc * 128:(cc + 1) * 128], nq4[:, j, qc], True, True,
                           r=[("kcmpT",), ("nq4", j)], w=[("ps", 7)])
                        P.op("act", lambda e: e.activation(out=em[:, cc, :], in_=ps[7][:], func=AF.Exp, scale=NSA_SCALE),
                             r=[("ps", 7)], w=[("em", cc)])
                        P.op("dve", lambda e: e.tensor_tensor(out=em[:, cc, :], in0=em[:, cc, :], in1=cm_sb[:, par, cc, :],
                                                              op=ALU.mult),
                             r=[("em", cc), ("cm", par)], w=[("em", cc)])
                    stg.append(s1)

                def s2(j=j):
                    for cc in range(ncc):
                        mm(ps[7][:], onesf[:], em[:, cc, :], cc == 0, cc == ncc - 1, r=[("onesf",), ("em", cc)],
                           w=[("ps", 7)], signal=(cc == ncc - 1))
                    P.op("dve", lambda e: e.tensor_scalar(out=rdenB[:], in0=ps[7][:], scalar1=1e-30, scalar2=None, op0=ALU.max),
                         r=[("ps", 7)], w=[("rdenB",)])
                    P.op("dve", lambda e: e.reciprocal(out=rdenB[:], in_=rdenB[:]), r=[("rdenB",)], w=[("rdenB",)])
                stg.append(s2)

                def s3(j=j):
                    for cc in range(ncc):
                        dst = psumT if j == 0 else pf
                        dres = ("psumT", cc) if j == 0 else ("pf", cc)
                        P.op("dve", lambda e: e.tensor_tensor(out=dst[:, cc, :], in0=em[:, cc, :], in1=rdenB[:], op=ALU.mult),
                             r=[("em", cc), ("rdenB",)], w=[dres])
                        if j < 2:
                            P.op("dve", lambda e: e.tensor_copy(out=pb[:, cc, :], in_=dst[:, cc, :]), r=[dres], w=[("pb", cc)])
                        if j > 0:
                            P.op("dve", lambda e: e.tensor_tensor(out=psumT[:, cc, :], in0=psumT[:, cc, :], in1=pf[:, cc, :],
                                                                  op=ALU.add),
                                 r=[("psumT", cc), ("pf", cc)], w=[("psumT", cc)])
                stg.append(s3)
                if j < 2:
                    def s4(j=j):
                        gate_prefetch(j, par, 0, 2 + j)
                        for cc in range(ncc):
                            mm(ps[7][:], vcmp[:, cc, :], pb[:, cc, :], cc == 0, cc == ncc - 1, r=[("vcmp",), ("pb", cc)],
                               w=[("ps", 7)], signal=(cc == ncc - 1))
                        branch_finish(j, par, 7, None, 2 + j, True, 0)
                    stg.append(s4)
            for qt in range(4):
                def s5(qt=qt):
                    for cc in range(ncc):
                        mm(ps[7][:, 0:64], psumT[:, cc, qt * 128:(qt + 1) * 128], ov[:, cc, :], cc == 0,
                           cc == ncc - 1, r=[("psumT", cc), ("ov",)], w=[("ps", 7)], signal=(cc == ncc - 1))
                    P.op("dve", lambda e: e.tensor_tensor(out=score[:], in0=ps[7][:, 0:64],
                                                          in1=bt_sb[:, par, qt, :], op=ALU.add),
                         r=[("ps", 7), ("bt", par)], w=[("score",)])
                    P.op("dve", lambda e: e.max(out=m8a[:], in_=score[:]), r=[("score",)], w=[("m8a",)])
                    P.op("dve", lambda e: e.match_replace(out=work[:], in_to_replace=m8a[:], in_values=score[:], imm_value=-3.0e38),
                         r=[("score",), ("m8a",)], w=[("work",)])
                    P.op("dve", lambda e: e.max(out=m8b[:], in_=work[:]), r=[("work",)], w=[("m8b",)])
                    P.op("dve", lambda e: e.tensor_scalar(out=bval[:, 0:64], in0=score[:], scalar1=m8b[:, 7:8], scalar2=NEGB,
                                                          op0=ALU.is_lt, op1=ALU.mult),
                         r=[("score",), ("m8b",)], w=[("bval",)])
                stg.append(s5)

                def s6(qt=qt):
                    mm(ps[7][:, 128:256], bval[:], identf[:], True, True, r=[("bval",), ("identf",)], w=[("ps", 7)])
                    P.op("act", lambda e: e.activation(out=biasT[:, par, qt * 128:(qt + 1) * 128], in_=ps[7][:, 128:256],
                                                       func=AF.Copy),
                         r=[("ps", 7)], w=[("biasT", par)])
                stg.append(s6)
            return stg

        def cd_tiles(qb):
            par = qb % 2
            qc = slice(qb * 512, (qb + 1) * 512)
            tl = []
            for hi in range(2):
                ao, ad = (2, 3) if hi == 0 else (4, 5)
                tl += make_tiles(list(range(4 * qb + 4)), nq4[:, hi, qc], ("nq4", hi), ksT, ("ksT",),
                                 lambda kt: vs[:, kt, :], ("vs",),
                                 lambda kt, qb=qb: (4 + kt - 4 * qb) if kt >= 4 * qb else None, NSA_SCALE, ao, ad,
                                 extra_bias=biasT[:, par, :], bias_res=("biasT", par),
                                 prologue=(lambda hi=hi, par=par: gate_prefetch(hi, par, 1, hi)),
                                 epilogue=(lambda hi=hi, ao=ao, ad=ad, par=par: branch_finish(hi, par, ao, ad, hi, False, hi)))
            for hi in range(2):
                ao, ad = (2, 3) if hi == 0 else (4, 5)
                kts = [kt for kt in range(4 * qb - 4, 4 * qb + 4) if kt >= 0]

                def epi(hi=hi, ao=ao, ad=ad, par=par, qc=qc):
                    branch_finish(hi, par, ao, ad, hi, False, hi)
                    os_ = cnt["ob"] % 2
                    cnt["ob"] += 1
                    P.op("dve", lambda e: e.tensor_copy(out=obf[:, os_, :], in_=oacc[:, par, hi, :]),
                         r=[("oacc", par, hi)], w=[("obf", os_)])
                    P.dma("sp", oT_d[(2 + hi) * 128:(3 + hi) * 128, qc], obf[:, os_, :], "st_ob%d" % os_, r=[("obf", os_)])
                tl += make_tiles(kts, nq4[:, hi, qc], ("nq4", hi), kwT, ("kwT",), lambda kt: vw[:, kt, :], ("vw",),
                                 lambda kt, qb=qb: kt - (4 * qb - 4), NSA_SCALE, ao, ad,
                                 prologue=(lambda hi=hi, par=par: gate_prefetch(hi, par, 2, hi)), epilogue=epi)
            return tl

        for fn in phaseB_stages(0):
            fn()
        for qb in range(nqb):
            run_tiles(cd_tiles(qb), extra=(phaseB_stages(qb + 1) if qb + 1 < nqb else ()))

        P.barrier("sp")
        P.barrier("dve")
        P.op("dve", lambda e: e.memset(arena[:, 0:16384], 0.0), w=[("dqz",)])
        lst = []
        for h in range(2):
            for c in range(2):
                lst.append((dqz[64 * c:64 * c + 64, h, c, :], dq_d[h * 128 + 64 * c:h * 128 + 64 * c + 64, :], [], [("dqz",)]))
            lst.append((dkT[:, h, :], dk_d[h * 128:(h + 1) * 128, :], [], [("dkT",)]))
        lst.append((dv, dv_d.rearrange("(t p) n -> p t n", p=128), [], [("dv",)]))
        P.dma_batch("sp", "ld_diff", lst)
        tl = []
        for h in range(2):
            for qb in range(nqb):
                qc = slice(qb * 512, (qb + 1) * 512)
                acc = [(2, 3), (4, 5)]

                def epi(h=h, qb=qb, qc=qc, acc=acc):
                    flush_deferred(all_=True)
                    for c in range(2):
                        ao, ad = acc[c]
                        P.op("dve", lambda e: e.reciprocal(out=rden[:, c, :], in_=ps[ad][:]),
                             r=[("ps", ad)], w=[("rden", c)])
                        P.op("dve", lambda e: e.tensor_tensor(out=tmpf[:, c, :], in0=ps[ao][:], in1=rden[:, c, :], op=ALU.mult),
                             r=[("ps", ao), ("rden", c)], w=[("tmpf", c)])
                    P.op("dve", lambda e: e.scalar_tensor_tensor(out=tmpf[:, 0, :], in0=tmpf[:, 1, :], scalar=lsc[:, 5:6],
                                                                 in1=tmpf[:, 0, :], op0=ALU.mult, op1=ALU.add),
                         r=[("tmpf", 0), ("tmpf", 1), ("lsc", 5)], w=[("tmpf", 0)])

                    def d1():
                        P.op("act", lambda e: e.activation(out=em[:, 0, :], in_=tmpf[:, 0, :], func=AF.Square),
                             r=[("tmpf", 0)], w=[("em", 0)])

                    def d2():
                        mm(ps[7][:], onesf[:], em[:, 0, :], True, True, r=[("onesf",), ("em", 0)], w=[("ps", 7)])

                    def d3():
                        P.op("act", lambda e: e.activation(out=em[:, 1, :], in_=ps[7][:], func=AF.Ln, scale=1.0 / 128, bias=epsb[:]),
                             r=[("ps", 7), ("epsb",)], w=[("em", 1)])
                        P.op("act", lambda e: e.activation(out=em[:, 1, :], in_=em[:, 1, :], func=AF.Exp, scale=-0.5),
                             r=[("em", 1)], w=[("em", 1)])

                    def d4():
                        os_ = cnt["ob"] % 2
                        cnt["ob"] += 1
                        P.op("dve", lambda e: e.scalar_tensor_tensor(out=obf[:, os_, :], in0=tmpf[:, 0, :], scalar=lsc[:, 6:7],
                                                                     in1=em[:, 1, :], op0=ALU.mult, op1=ALU.mult),
                             r=[("tmpf", 0), ("lsc", 6), ("em", 1)], w=[("obf", os_)])
                        P.dma("sp", oT_d[h * 128:(h + 1) * 128, qc], obf[:, os_, :], "st_ob%d" % os_, r=[("obf", os_)])
                    defer(3, d1)
                    defer(4, d2)
                    defer(6, d3)
                    defer(7, d4)
                for c in range(2):
                    ao, ad = acc[c]
                    tl += make_tiles(list(range(4 * qb + 4)), dqz[:, h, c, qc], ("dqz",), dkT[:, h, :], ("dkT",),
                                     lambda kt, h=h: dv[:, kt, h * 128:(h + 1) * 128], ("dv",),
                                     lambda kt, qb=qb: (4 + kt - 4 * qb) if kt >= 4 * qb else None, DIFF_SCALE, ao, ad,
                                     epilogue=(epi if c == 1 else None))
        run_tiles(tl)
        P.finish("sp", [s for s in P.semobj if s.startswith("st_")])
        print("A kernel ops", P.nops)
    return nc


def attn_consts():
    cm = np.zeros((256, S), np.float32)
    t = np.arange(S)
    for c in range(NCMP):
        cm[c] = (16 * c + 31 <= t)
    cs = np.arange(NCMP) * 16
    ss = np.arange(64) * 64
    ovm = np.clip(np.minimum(cs[:, None] + 32, ss[None, :] + 64) - np.maximum(cs[:, None], ss[None, :]), 0, None) / 32.0
    ov = np.zeros((256, 64), np.float32)
    ov[:NCMP] = ovm
    bt = np.zeros((128, 32, 64), np.float32)
    for tile in range(32):
        tt = tile * 128 + np.arange(128)
        m = np.arange(64)[None, :]
        valid = m * 64 <= tt[:, None]
        cur = (tt // 64)[:, None]
        forced = (m == 0) | (m == cur) | (m == cur - 1)
        bt[:, tile, :] = np.where(valid, np.where(forced, 1e6, 0.0), -1e30)
    E = np.zeros((128, 32, 128), np.float32)
    for kt in range(32):
        E[2 * kt, kt, 0:64] = 1.0
        E[2 * kt + 1, kt, 64:128] = 1.0
    wmk = np.zeros((128, 8, 512), np.float32)
    p = np.arange(128)[:, None]
    q = np.arange(512)[None, :]
    for r in range(8):
        k = (r - 4) * 128 + p
        wmk[:, r, :] = ((q - k >= 0) & (q - k < 512))
    selG = np.zeros((32, 6, 128), np.float32)
    for r in range(6):
        selG[r, r, :] = 1.0
    return dict(cmask=cm.astype(NPBF), ov=ov, btile=bt.reshape(128, 32 * 64), Emat=E.reshape(128, 32 * 128).astype(NPBF),
                wm=wmk.reshape(128, 8 * 512).astype(NPBF), selG=selG.reshape(32, 6 * 128),
                ident=np.eye(128, dtype=np.float32))


MCOLS = 9 * D // NCORE
MCH = MCOLS // 128


def build_M():
    nc = bass.Bass("TRN2", target_bir_lowering=False)
    stack = ExitStack()
    with stack:
        P = Prog(nc, stack)
        cT_d = nc.dram_tensor("cT", [128, KC * B], F32, kind="ExternalInput").ap()
        wada_d = nc.dram_tensor("wada", [DEPTH * D, MCOLS], F32, kind="ExternalInput").ap()
        bT_d = nc.dram_tensor("bT", [128, DEPTH * MCH], F32, kind="ExternalInput").ap()
        modp_d = nc.dram_tensor("modp", [128, DEPTH * MCH * B], F32, kind="ExternalOutput").ap()
        cT = P.sb("cT_sb", [128, KC, B], F32)
        cs = P.sb("cs_sb", [128, KC, B], BF16)
        bT = P.sb("bT_sb", [128, DEPTH * MCH], F32)
        wA = P.sb("wA", [128, 4, KC, 256], BF16)
        osb = P.sb("osb", [128, DEPTH * MCH, B], F32)
        ps = [P.psum("ps%d" % i, [128, 512], F32) for i in range(2)]
        rings = {"A": dict(size=4, next=0, last=[None] * 4)}
        P.dma("sp", cT[:].rearrange("p a b -> p (a b)"), cT_d, "ld_c", w=[("cT",)])
        P.dma("sp", bT[:], bT_d, "ld_b", w=[("bT",)])
        P.op("act", lambda e: e.activation(out=cs[:], in_=cT[:], func=AF.Silu), r=[("cT",)], w=[("cs",)])
        items = []
        cntr = dict(b=0)

        def mk(l, s):
            def load(slots):
                P.dma("pool", wA[:, slots[0], :, :],
                      wada_d[l * D:(l + 1) * D, s * 256:(s + 1) * 256].rearrange("(kc p) n -> p kc n", p=128),
                      "ldA%d" % slots[0], w=[("A", slots[0])])

            def compute(slots):
                sl = slots[0]
                for fc in range(2):
                    ch = l * MCH + s * 2 + fc
                    bank = cntr["b"] % 2
                    cntr["b"] += 1
                    for kc in range(KC):
                        P.op("pe", lambda e: e.matmul(ps[bank][:, 0:B], wA[:, sl, kc, fc * 128:(fc + 1) * 128], cs[:, kc, :],
                                                      start=(kc == 0), stop=(kc == KC - 1)),
                             r=[("A", sl), ("cs",)], w=[("ps", bank)], signal=(kc == KC - 1))
                    P.op("act", lambda e: e.activation(out=osb[:, ch, :], in_=ps[bank][:, 0:B], func=AF.Identity,
                                                       bias=bT[:, ch:ch + 1], scale=1.0),
                         r=[("ps", bank), ("bT",)], w=[("osb",)])
            return Item("A", 1, load, compute)
        for l in range(DEPTH):
            for s in range(MCH // 2):
                items.append(mk(l, s))
        run_stream(rings, items)
        P.dma("sp", modp_d, osb[:].rearrange("p a b -> p (a b)"), "st_mod", r=[("osb",)])
        P.finish("sp", ["st_mod"])
        print("M kernel ops", P.nops)
    return nc


_CACHE = {}


def _prog(key, fn):
    if key not in _CACHE:
        _CACHE[key] = fn()
    return _CACHE[key]


def _run(nc, in_maps):
    res = bass_utils.run_bass_kernel_spmd(nc, in_maps, core_ids=list(range(NCORE)))
    return res.results


def kernel(x, c, w_ada, b_ada, norm_g, ffn_w_gu, ffn_w_d, w_in, w_o, diff_lam, diff_subln, cmp_pe, cmp_w1, cmp_w2, final_g):
    f32 = np.float32
    x = np.asarray(x, f32)
    c = np.asarray(c, f32)
    ncM = _prog("M", build_M)
    cT = np.ascontiguousarray(c.reshape(B, KC, 128).transpose(2, 1, 0)).reshape(128, KC * B)
    in_maps = []
    for core in range(NCORE):
        cols = slice(core * MCOLS, (core + 1) * MCOLS)
        wa = np.ascontiguousarray(np.asarray(w_ada)[:, :, cols]).reshape(DEPTH * D, MCOLS)
        bt = np.ascontiguousarray(np.asarray(b_ada)[:, cols].reshape(DEPTH, MCH, 128).transpose(2, 0, 1)).reshape(128, DEPTH * MCH)
        in_maps.append({"cT": cT, "wada": wa, "bT": bt})
    resM = _run(ncM, in_maps)
    mod = np.zeros((B, DEPTH, 9 * D), f32)
    for core in range(NCORE):
        mp = resM[core]["modp"].reshape(128, DEPTH, MCH, B)
        mod[:, :, core * MCOLS:(core + 1) * MCOLS] = mp.transpose(3, 1, 2, 0).reshape(B, DEPTH, MCOLS)
    modT = [[np.ascontiguousarray(fm(mod[b, l].reshape(9, D))).reshape(128, 9 * KC) for l in range(DEPTH)] for b in range(B)]
    ngT = [np.ascontiguousarray(fm(np.asarray(norm_g[l], f32))).reshape(128, 3 * KC) for l in range(DEPTH)]
    rm = rope_perm()
    ropes = [rope_tables(np.arange(i * NT, (i + 1) * NT)) for i in range(4)]
    aconst = attn_consts()

    xT = [np.ascontiguousarray(x[core // 4, (core % 4) * NT:(core % 4 + 1) * NT].T) for core in range(NCORE)]
    oT = None
    for l in range(DEPTH + 1):
        phases = []
        if l > 0:
            phases += ["wo", "ffn_b"]
        if l < DEPTH:
            phases += ["ffn_a", "proj", "xout"]
        else:
            phases += ["final"]
        ncT = _prog("T" + ",".join(phases), lambda: build_T(phases))
        in_maps = []
        for core in range(NCORE):
            b, i = core // 4, core % 4
            m = {"xT_in": xT[core]}
            if l > 0:
                m["mod_b"] = modT[b][l - 1]
                m["ng_b"] = ngT[l - 1]
                m["w_o"] = np.asarray(w_o[l - 1], f32)
                m["oT"] = oT[core]
                m["wgu_b"] = np.asarray(ffn_w_gu[l - 1, 1], f32)
                m["wd_b"] = np.asarray(ffn_w_d[l - 1, 1], f32)
            if l < DEPTH:
                m["mod_a"] = modT[b][l]
                m["ng_a"] = ngT[l]
                m["wgu_a"] = np.asarray(ffn_w_gu[l, 0], f32)
                m["wd_a"] = np.asarray(ffn_w_d[l, 0], f32)
                m["w_in"] = np.asarray(w_in[l], f32)
                m["rope_d"] = ropes[i]
                m["rm_d"] = rm
            else:
                m["fg_d"] = np.ascontiguousarray(fm(np.asarray(final_g, f32)))
            in_maps.append(m)
        resT = _run(ncT, in_maps)
        if l == DEPTH:
            out = np.zeros((B, S, D), f32)
            for core in range(NCORE):
                b, i = core // 4, core % 4
                out[b, i * NT:(i + 1) * NT, :] = resT[core]["yT"].T
            return out
        xT = [resT[core]["xT_out"] for core in range(NCORE)]
        ncA = _prog("A", build_A)
        lam_init = 0.8 - 0.6 * math.exp(-0.3 * l)
        in_maps = []
        for core in range(NCORE):
            b, hg = core // 4, core % 4
            g = hg // 2
            qk = np.concatenate([resT[b * 4 + i]["qk"] for i in range(4)], axis=1)
            vv = np.concatenate([resT[b * 4 + i]["vv"] for i in range(4)], axis=0)
            gt = np.concatenate([resT[b * 4 + i]["gt"] for i in range(4)], axis=1)

            def ch(i0, n=1):
                return qk[i0 * 128:(i0 + n) * 128]
            own = [2 * hg, 2 * hg + 1]
            oth = [h for h in range(4 * g, 4 * g + 4) if h not in own]
            m = dict(aconst)
            m["nq4"] = np.concatenate([ch(16 + h) for h in own + oth], 0)
            m["kc"] = ch(24 + g)
            m["vc"] = ch(30 + g)
            m["ks"] = ch(26 + g)
            m["kw"] = ch(28 + g)
            m["vs"] = vv[:, 1024 + 128 * g:1024 + 128 * (g + 1)]
            m["vw"] = vv[:, 1280 + 128 * g:1280 + 128 * (g + 1)]
            m["dq"] = ch(2 * hg, 2)
            m["dk"] = ch(8 + 2 * hg, 2)
            m["dv"] = vv[:, 256 * hg:256 * (hg + 1)]
            m["g6"] = gt[6 * hg:6 * hg + 6]
            m["w1"] = np.asarray(cmp_w1[l], f32).reshape(8192, 256)
            m["w2"] = np.asarray(cmp_w2[l], f32).reshape(512, 128)
            pe = np.asarray(cmp_pe[l], f32)
            m["peT"] = np.concatenate([pe[0].T, pe[1].T], 1)
            m["lam"] = np.asarray(diff_lam[l], f32).reshape(256)
            m["subln"] = np.asarray(diff_subln[l], f32).reshape(128, 1)
            m["lamc"] = np.tile(np.array([[lam_init, 1.0 - lam_init]], f32), (128, 1))
            in_maps.append({k: np.ascontiguousarray(v) for k, v in m.items()})
        resA = _run(ncA, in_maps)
        oT = []
        for core in range(NCORE):
            b, i = core // 4, core % 4
            full = np.zeros((D, NT), dtype=NPBF)
            for hg in range(4):
                o = resA[b * 4 + hg]["oT"][:, i * NT:(i + 1) * NT]
                full[256 * hg:256 * (hg + 1)] = o[0:256]
                full[1024 + 256 * hg:1024 + 256 * (hg + 1)] = o[256:512]
            oT.append(full)
```

```python
import math
from contextlib import ExitStack
import numpy as np
import ml_dtypes
import concourse.bass as bass
import concourse.mybir as mybir
import concourse.bass_utils as bass_utils

F32 = mybir.dt.float32
BF16 = mybir.dt.bfloat16
AF = mybir.ActivationFunctionType
ALU = mybir.AluOpType
NPBF = ml_dtypes.bfloat16

D = 2048
B = 2
S = 4096
DEPTH = 4
DFF = 5632
NCORE = 8
NT = 1024
KC = D // 128
INC = 5656
EPS = 1e-6
ROPE_THETA = 500000.0


class Prog:
    def __init__(self, nc, stack):
        self.nc = nc
        self.stack = stack
        self.E = dict(pe=nc.tensor, act=nc.scalar, dve=nc.vector, pool=nc.gpsimd, sp=nc.sync)
        self.semobj = {}
        self.semval = {}
        for k in self.E:
            self.semobj["e_" + k] = stack.enter_context(nc.semaphore("e_" + k))
            self.semval["e_" + k] = 0
        self.seen = {k: {} for k in self.E}
        self.R = {}
        self.nops = 0

    def sb(self, name, shape, dt):
        return self.stack.enter_context(self.nc.sbuf_tensor(name, shape, dt))

    def psum(self, name, shape, dt):
        return self.stack.enter_context(self.nc.psum_tensor(name, shape, dt))

    def dsem(self, name):
        if name not in self.semobj:
            self.semobj[name] = self.stack.enter_context(self.nc.semaphore(name))
            self.semval[name] = 0
        return name

    def _wait(self, eng, dep, raw):
        name, val = dep
        if name == "e_" + eng:
            if eng == "pe" or not raw:
                return
        if self.seen[eng].get(name, 0) >= val:
            return
        self.E[eng].wait_ge(self.semobj[name], val)
        self.seen[eng][name] = val
        self.nops += 1

    def _hazards(self, eng, r, w):
        for res in r:
            st = self.R.get(res)
            if st and st[0]:
                self._wait(eng, st[0], True)
            if st and res[0] == "ps":
                for nm, v in st[1].items():
                    self._wait(eng, (nm, v), False)
        for res in w:
            st = self.R.get(res)
            if st:
                if st[0]:
                    self._wait(eng, st[0], False)
                for nm, v in st[1].items():
                    self._wait(eng, (nm, v), False)

    def _record(self, tok, r, w):
        for res in r:
            st = self.R.setdefault(res, [None, {}])
            if st[1].get(tok[0], 0) < tok[1]:
                st[1][tok[0]] = tok[1]
        for res in w:
            self.R[res] = [tok, {}]

    def op(self, eng, fn, r=(), w=(), signal=True):
        self._hazards(eng, r, w)
        ins = fn(self.E[eng])
        self.nops += 1
        name = "e_" + eng
        if signal:
            self.semval[name] += 1
            ins.then_inc(self.semobj[name], 1)
            tok = (name, self.semval[name])
        else:
            tok = (name, self.semval[name] + 1)
        self._record(tok, r, w)

    def dma(self, eng, out, in_, sem, r=(), w=()):
        self._hazards(eng, r, w)
        self.dsem(sem)
        self.E[eng].dma_start(out=out, in_=in_).then_inc(self.semobj[sem], 16)
        self.nops += 1
        self.semval[sem] += 16
        self._record((sem, self.semval[sem]), r, w)

    def dma_batch(self, eng, sem, lst):
        self.dsem(sem)
        for (out, in_, r, w) in lst:
            self._hazards(eng, r, w)
        for (out, in_, r, w) in lst:
            self.E[eng].dma_start(out=out, in_=in_).then_inc(self.semobj[sem], 16)
            self.nops += 1
            self.semval[sem] += 16
        for (out, in_, r, w) in lst:
            self._record((sem, self.semval[sem]), r, w)

    def barrier(self, eng):
        for name, val in self.semval.items():
            if val > 0 and name != "e_" + eng and self.seen[eng].get(name, 0) < val:
                self.E[eng].wait_ge(self.semobj[name], val)
                self.seen[eng][name] = val
                self.nops += 1

    def finish(self, eng, sems):
        for s in sems:
            if self.semval.get(s, 0) > 0:
                self.E[eng].wait_ge(self.semobj[s], self.semval[s])


class Item:
    def __init__(self, cls, nslots, load, compute):
        self.cls, self.nslots, self.load, self.compute = cls, nslots, load, compute


def run_stream(rings, items):
    n = len(items)
    slot_of, preds = [], []
    for i, it in enumerate(items):
        pr = set()
        s = []
        if it.cls is not None:
            ring = rings[it.cls]
            for k in range(it.nslots):
                sl = (ring["next"] + k) % ring["size"]
                if ring["last"][sl] is not None:
                    pr.add(ring["last"][sl])
                ring["last"][sl] = i
                s.append(sl)
            ring["next"] = (ring["next"] + it.nslots) % ring["size"]
        slot_of.append(s)
        preds.append(pr)
    loaded = [False] * n
    state = {"j": 0}

    def try_loads(done):
        while state["j"] < n:
            j = state["j"]
            if all(p <= done for p in preds[j]):
                if items[j].load is not None:
                    items[j].load(slot_of[j])
                loaded[j] = True
                state["j"] += 1
            else:
                break

    try_loads(-1)
    for i in range(n):
        assert loaded[i]
        items[i].compute(slot_of[i])
        try_loads(i)


class TK:
    pass


def t_alloc(P, nt):
    T = TK()
    T.nt = nt
    T.nh = nt // 512
    T.xT = P.sb("xT", [128, KC, nt], F32)
    T.hT = P.sb("hT", [128, KC, nt], BF16)
    T.act = P.sb("act", [128, 2, 4, nt], BF16)
    T.wA = P.sb("wA", [128, 4, KC, 256], BF16)
    T.wD = P.sb("wD", [128, 4, D], BF16)
    T.mod = P.sb("mod", [128, 9, KC], F32)
    T.ng = P.sb("ng", [128, 3, KC], F32)
    T.Acoef = P.sb("Acoef", [128, 3, KC], F32)
    T.hgate = P.sb("hgate", [128, 3, KC], F32)
    T.rstd = P.sb("rstd", [128, 512], F32)
    T.sq = P.sb("sq", [128, 2, 512], F32)
    T.tmp = P.sb("tmp", [128, 2, 512], F32)
    T.ones = P.sb("ones", [128, 128], F32)
    T.epsb = P.sb("epsb", [128, 1], F32)
    T.ps = [P.psum("ps%d" % i, [128, 512], F32) for i in range(8)]
    T.rings = {"A": dict(size=4, next=0, last=[None] * 4), "D": dict(size=4, next=0, last=[None] * 4)}
    P.op("dve", lambda e: e.memset(T.ones[:], 1.0), w=[("ones",)])
    P.op("dve", lambda e: e.memset(T.epsb[:], EPS), w=[("epsb",)])
    return T


def t_load_mod(P, T, mod_d, ng_d):
    P.dma("sp", T.mod[:].rearrange("p a c -> p (a c)"), mod_d, "ld_mod", w=[("mod",)])
    P.dma("sp", T.ng[:].rearrange("p a c -> p (a c)"), ng_d, "ld_ng", w=[("ng",)])
    for j in range(3):
        P.op("dve", lambda e: e.scalar_tensor_tensor(out=T.Acoef[:, j, :], in0=T.mod[:, 3 * j + 1, :], scalar=1.0,
                                                     in1=T.ng[:, j, :], op0=ALU.add, op1=ALU.mult),
             r=[("mod",), ("ng",)], w=[("Acoef", j)])
        gs = 1.0 if j == 1 else 0.5
        P.op("dve", lambda e: e.tensor_scalar(out=T.hgate[:, j, :], in0=T.mod[:, 3 * j + 2, :], scalar1=gs, scalar2=None,
                                              op0=ALU.mult),
             r=[("mod",)], w=[("hgate", j)])


def t_norm(P, T, j, out_kind="h", gvec=None, outbuf=None):
    for half in range(T.nh):
        cols = slice(half * 512, (half + 1) * 512)
        bank = T.ps[4 + (half % 2)]
        bres = ("ps", 4 + (half % 2))
        for c in range(KC):
            sq = T.sq[:, c % 2, :]
            P.op("act", lambda e: e.activation(out=sq, in_=T.xT[:, c, cols], func=AF.Square),
                 r=[("x", c, half)], w=[("sq", c % 2)])
            P.op("pe", lambda e: e.matmul(bank[:], T.ones[:], sq, start=(c == 0), stop=(c == KC - 1)),
                 r=[("sq", c % 2), ("ones",)], w=[bres])
        P.op("act", lambda e: e.activation(out=T.rstd[:], in_=bank[:], func=AF.Ln, scale=1.0 / D, bias=T.epsb[:]),
             r=[bres, ("epsb",)], w=[("rstd",)])
        P.op("act", lambda e: e.activation(out=T.rstd[:], in_=T.rstd[:], func=AF.Exp, scale=-0.5),
             r=[("rstd",)], w=[("rstd",)])
        for c in range(KC):
            tmp = T.tmp[:, c % 2, :]
            if out_kind == "h":
                P.op("dve", lambda e: e.scalar_tensor_tensor(out=tmp, in0=T.xT[:, c, cols], scalar=T.Acoef[:, j, c:c + 1],
                                                             in1=T.rstd[:], op0=ALU.mult, op1=ALU.mult),
                     r=[("x", c, half), ("rstd",), ("Acoef", j)], w=[("tmp", c % 2)])
                P.op("act", lambda e: e.activation(out=T.hT[:, c, cols], in_=tmp, func=AF.Identity,
                                                   bias=T.mod[:, 3 * j, c:c + 1], scale=1.0),
                     r=[("tmp", c % 2), ("mod",)], w=[("h", c, half)])
            else:
                P.op("dve", lambda e: e.scalar_tensor_tensor(out=tmp, in0=T.xT[:, c, cols], scalar=gvec[:, c:c + 1],
                                                             in1=T.rstd[:], op0=ALU.mult, op1=ALU.mult),
                     r=[("x", c, half), ("rstd",), ("fg",)], w=[("tmp", c % 2)])
                P.dma("sp", outbuf[c * 128:(c + 1) * 128, cols], tmp, "st_tmp%d" % (c % 2), r=[("tmp", c % 2)])


def t_slabA_load(P, T, slot, src):
    ncols = src.shape[1]
    P.dma("pool", T.wA[:, slot, :, 0:ncols], src.rearrange("(kc p) n -> p kc n", p=128), "ldA%d" % slot,
          w=[("A", slot)])


def ffn_items(P, T, wgu, wd, j):
    items = []
    NP = DFF // 512
    nh = T.nh

    def mk_gu(p, s):
        f0 = (p * 4 + s * 2) * 128

        def load(slots):
            t_slabA_load(P, T, slots[0], wgu[:, f0:f0 + 256])
            t_slabA_load(P, T, slots[1], wgu[:, DFF + f0:DFF + f0 + 256])

        def compute(slots):
            for fc in range(2):
                cj = s * 2 + fc
                for kind in range(2):
                    sl = slots[kind]
                    for half in range(nh):
                        bank = kind * 2 + (half % 2)
                        cols = slice(half * 512, (half + 1) * 512)
                        for kc in range(KC):
                            P.op("pe", lambda e: e.matmul(T.ps[bank][:], T.wA[:, sl, kc, fc * 128:(fc + 1) * 128],
                                                          T.hT[:, kc, cols], start=(kc == 0), stop=(kc == KC - 1)),
                                 r=[("A", sl), ("h", kc, half)], w=[("ps", bank)], signal=(kc == KC - 1))
                        if kind == 0:
                            P.op("act", lambda e: e.activation(out=T.sq[:, half % 2, :], in_=T.ps[bank][:], func=AF.Silu),
                                 r=[("ps", bank)], w=[("sq", half % 2)])
                        else:
                            P.op("dve", lambda e: e.tensor_tensor(out=T.act[:, p % 2, cj, cols], in0=T.ps[bank][:],
                                                                  in1=T.sq[:, half % 2, :], op=ALU.mult),
                                 r=[("ps", bank), ("sq", half % 2)], w=[("act", p % 2, cj, half)])
        return Item("A", 2, load, compute)

    def mk_down(p):
        def load(slots):
            for jj in range(4):
                r0 = (p * 4 + jj) * 128
                P.dma("pool", T.wD[:, slots[jj], :], wd[r0:r0 + 128, :], "ldD%d" % slots[jj], w=[("D", slots[jj])])

        def compute(slots):
            ctr = 0
            for mo in range(KC):
                for half in range(nh):
                    bank = 4 + (ctr % 4)
                    ctr += 1
                    cols = slice(half * 512, (half + 1) * 512)
                    for jj in range(4):
                        P.op("pe", lambda e: e.matmul(T.ps[bank][:], T.wD[:, slots[jj], mo * 128:(mo + 1) * 128],
                                                      T.act[:, p % 2, jj, cols], start=(jj == 0), stop=(jj == 3)),
                             r=[("D", slots[jj]), ("act", p % 2, jj, half)], w=[("ps", bank)], signal=(jj == 3))
                    P.op("dve", lambda e: e.scalar_tensor_tensor(out=T.xT[:, mo, cols], in0=T.ps[bank][:],
                                                                 scalar=T.hgate[:, j, mo:mo + 1], in1=T.xT[:, mo, cols],
                                                                 op0=ALU.mult, op1=ALU.add),
                         r=[("ps", bank), ("hgate", j), ("x", mo, half)], w=[("x", mo, half)])
        return Item("D", 4, load, compute)

    items.append(Item(None, 0, None, lambda s: t_norm(P, T, j)))
    order = []
    for p in range(NP):
        order.append(("g", p))
        if p >= 1:
            order.append(("d", p - 1))
    order.append(("d", NP - 1))
    for kind, p in order:
        if kind == "g":
            items.append(mk_gu(p, 0))
            items.append(mk_gu(p, 1))
        else:
            items.append(mk_down(p))
    return items


PROJ_SLABS = ([("dq", "fm", "A", 0 + 2 * i) for i in range(4)] + [("dk", "fm", "A", 8 + 2 * i) for i in range(4)] +
              [("dv", "tm", None, 256 * i) for i in range(4)] + [("nq", "fm", "B", 16 + 2 * i) for i in range(4)] +
              [("kc", "fm", None, 24), ("vc", "fm", None, 30), ("ks", "fm", "B", 26), ("vs", "tm", None, 1024),
               ("kw", "fm", "B", 28), ("vw", "tm", None, 1280), ("gl", "gl", None, 0)])
NQK = 32
NV = 1536


def t_alloc_proj(P, T):
    nt = T.nt
    T.rope = P.sb("rope", [128, 4, nt], F32)
    T.Rm = P.sb("Rm", [128, 2, 128], BF16)
    T.xb = P.sb("xb", [128, 2, 512], BF16)
    T.t2 = P.sb("t2", [128, 2, 512], F32)
    T.ost = P.sb("ost", [128, 4, 512], BF16)
    T.vst = P.sb("vst", [128, 2, 256], BF16)
    T.cnt = dict(ost=0, vst=0, gst=0, xb=0, pa=0, pb=0)
    T.pending = []


def t_load_proj_consts(P, T, rope_d, rm_d):
    P.dma_batch("sp", "ld_rope", [(T.rope[:, i, :], rope_d[i], [], [("rope", i)]) for i in range(4)])
    P.dma_batch("sp", "ld_rm", [(T.Rm[:, i, :], rm_d[i], [], [("Rm", i)]) for i in range(2)])


def proj_items(P, T, w_in, qk_d, vv_d, gt_d):
    items = [Item(None, 0, None, lambda s: t_norm(P, T, 1))]
    nh = T.nh

    def mk(sidx):
        name, lay, rt, base = PROJ_SLABS[sidx]
        c0 = 256 * sidx
        ncols = min(256, INC - c0)

        def load(slots):
            t_slabA_load(P, T, slots[0], w_in[:, c0:c0 + ncols])

        def compute(slots):
            sl = slots[0]
            if lay != "fm":
                while T.pending:
                    T.pending.pop(0)()
            if lay == "tm":
                for t in range(T.nt // 128):
                    bank = T.cnt["pa"] % 4
                    T.cnt["pa"] += 1
                    half = t // 4
                    for kc in range(KC):
                        P.op("pe", lambda e: e.matmul(T.ps[bank][:, 0:256], T.hT[:, kc, t * 128:(t + 1) * 128],
                                                      T.wA[:, sl, kc, 0:256], start=(kc == 0), stop=(kc == KC - 1)),
                             r=[("A", sl), ("h", kc, half)], w=[("ps", bank)], signal=(kc == KC - 1))
                    vs = T.cnt["vst"] % 2
                    T.cnt["vst"] += 1
                    P.op("act", lambda e: e.activation(out=T.vst[:, vs, :], in_=T.ps[bank][:, 0:256], func=AF.Copy),
                         r=[("ps", bank)], w=[("vst", vs)])
                    P.dma("sp", vv_d[t * 128:(t + 1) * 128, base:base + 256], T.vst[:, vs, :], "st_v%d" % vs,
                          r=[("vst", vs)])
                return
            if lay == "gl":
                for half in range(nh):
                    bank = T.cnt["pa"] % 4
                    T.cnt["pa"] += 1
                    cols = slice(half * 512, (half + 1) * 512)
                    for kc in range(KC):
                        P.op("pe", lambda e: e.matmul(T.ps[bank][:], T.wA[:, sl, kc, 0:128], T.hT[:, kc, cols],
                                                      start=(kc == 0), stop=(kc == KC - 1)),
                             r=[("A", sl), ("h", kc, half)], w=[("ps", bank)], signal=(kc == KC - 1))
                    gs = T.cnt["gst"] % 2
                    T.cnt["gst"] += 1
                    P.op("act", lambda e: e.activation(out=T.t2[0:24, gs, :], in_=T.ps[bank][0:24, :], func=AF.Sigmoid),
                         r=[("ps", bank)], w=[("t2", gs)])
                    P.dma("sp", gt_d[:, cols], T.t2[0:24, gs, :], "st_g%d" % gs, r=[("t2", gs)])
                return
            for fc in range(2):
                for half in range(nh):
                    bank = T.cnt["pa"] % 4
                    T.cnt["pa"] += 1
                    cols = slice(half * 512, (half + 1) * 512)
                    for kc in range(KC):
                        P.op("pe", lambda e: e.matmul(T.ps[bank][:], T.wA[:, sl, kc, fc * 128:(fc + 1) * 128],
                                                      T.hT[:, kc, cols], start=(kc == 0), stop=(kc == KC - 1)),
                             r=[("A", sl), ("h", kc, half)], w=[("ps", bank)], signal=(kc == KC - 1))
                    while T.pending:
                        T.pending.pop(0)()
                    os_ = T.cnt["ost"] % 4
                    T.cnt["ost"] += 1
                    if rt is None:
                        P.op("act", lambda e: e.activation(out=T.ost[:, os_, :], in_=T.ps[bank][:], func=AF.Copy),
                             r=[("ps", bank)], w=[("ost", os_)])
                        P.dma("sp", qk_d[(base + fc) * 128:(base + fc + 1) * 128, cols], T.ost[:, os_, :], "st_o%d" % os_,
                              r=[("ost", os_)])
                    else:
                        ri = 0 if rt == "A" else 1
                        xs = T.cnt["xb"] % 2
                        T.cnt["xb"] += 1
                        b2 = 4 + (T.cnt["pb"] % 4)
                        T.cnt["pb"] += 1
                        P.op("act", lambda e: e.activation(out=T.xb[:, xs, :], in_=T.ps[bank][:], func=AF.Copy),
                             r=[("ps", bank)], w=[("xb", xs)])

                        def tail(bank=bank, xs=xs, b2=b2, ri=ri, os_=os_, cols=cols, fc=fc):
                            P.op("pe", lambda e: e.matmul(T.ps[b2][:], T.Rm[:, ri, :], T.xb[:, xs, :], start=True, stop=True),
                                 r=[("xb", xs), ("Rm", ri)], w=[("ps", b2)])
                            P.op("dve", lambda e: e.tensor_tensor(out=T.tmp[:, xs, :], in0=T.ps[bank][:],
                                                                  in1=T.rope[:, 2 * ri, cols], op=ALU.mult),
                                 r=[("ps", bank), ("rope", 2 * ri), ("xb", xs)], w=[("tmp", xs)])
                            P.op("dve", lambda e: e.tensor_tensor(out=T.t2[:, xs, :], in0=T.ps[b2][:],
                                                                  in1=T.rope[:, 2 * ri + 1, cols], op=ALU.mult),
                                 r=[("ps", b2), ("rope", 2 * ri + 1)], w=[("t2", xs)])
                            P.op("dve", lambda e: e.tensor_tensor(out=T.ost[:, os_, :], in0=T.tmp[:, xs, :],
                                                                  in1=T.t2[:, xs, :], op=ALU.add),
                                 r=[("tmp", xs), ("t2", xs)], w=[("ost", os_)])
                            P.dma("sp", qk_d[(base + fc) * 128:(base + fc + 1) * 128, cols], T.ost[:, os_, :], "st_o%d" % os_,
                                  r=[("ost", os_)])
                        T.pending.append(tail)
        return Item("A", 1, load, compute)

    import os
    kinds = os.environ.get("PROJ_KINDS", "fmrope,fmplain,tm,gl").split(",")
    for sidx in range(len(PROJ_SLABS)):
        name, lay, rt, base = PROJ_SLABS[sidx]
        k = "tm" if lay == "tm" else ("gl" if lay == "gl" else ("fmrope" if rt else "fmplain"))
        if k in kinds:
            items.append(mk(sidx))

    def flush(slots):
        while T.pending:
            T.pending.pop(0)()
    items.append(Item(None, 0, None, flush))
    return items


def wo_items(P, T, w_o, oT_d):
    items = []

    def ld(s):
        P.dma_batch("sp", "ld_o", [(T.hT[:, c, :], oT_d[c * 128:(c + 1) * 128, :], [],
                                    [("h", c, half) for half in range(T.nh)]) for c in range(KC)])
    items.append(Item(None, 0, None, ld))

    def mk(si):
        def load(slots):
            t_slabA_load(P, T, slots[0], w_o[:, si * 256:(si + 1) * 256])

        def compute(slots):
            sl = slots[0]
            for fc in range(2):
                mo = si * 2 + fc
                for half in range(T.nh):
                    bank = 4 + (T.cnt["pb"] % 4)
                    T.cnt["pb"] += 1
                    cols = slice(half * 512, (half + 1) * 512)
                    for kc in range(KC):
                        P.op("pe", lambda e: e.matmul(T.ps[bank][:], T.wA[:, sl, kc, fc * 128:(fc + 1) * 128],
                                                      T.hT[:, kc, cols], start=(kc == 0), stop=(kc == KC - 1)),
                             r=[("A", sl), ("h", kc, half)], w=[("ps", bank)], signal=(kc == KC - 1))
                    P.op("dve", lambda e: e.scalar_tensor_tensor(out=T.xT[:, mo, cols], in0=T.ps[bank][:],
                                                                 scalar=T.hgate[:, 1, mo:mo + 1], in1=T.xT[:, mo, cols],
                                                                 op0=ALU.mult, op1=ALU.add),
                         r=[("ps", bank), ("hgate", 1), ("x", mo, half)], w=[("x", mo, half)])
        return Item("A", 1, load, compute)
    for si in range(8):
        items.append(mk(si))
    return items


def build_T(phases, nt=NT):
    nc = bass.Bass("TRN2", target_bir_lowering=False)
    stack = ExitStack()
    with stack:
        P = Prog(nc, stack)
        T = t_alloc(P, nt)
        T.cnt = dict(ost=0, vst=0, gst=0, xb=0, pa=0, pb=0)
        xin = nc.dram_tensor("xT_in", [D, nt], F32, kind="ExternalInput").ap()
        P.dma_batch("sp", "ld_x", [(T.xT[:, c, :], xin[c * 128:(c + 1) * 128, :], [],
                                    [("x", c, h) for h in range(T.nh)]) for c in range(KC)])
        items = []
        out_sems = []
        lay = 0
        if "wo" in phases or "ffn_b" in phases:
            modB = nc.dram_tensor("mod_b", [128, 9 * KC], F32, kind="ExternalInput").ap()
            ngB = nc.dram_tensor("ng_b", [128, 3 * KC], F32, kind="ExternalInput").ap()
            items.append(Item(None, 0, None, lambda s: t_load_mod(P, T, modB, ngB)))
        if "wo" in phases:
            w_o = nc.dram_tensor("w_o", [D, D], F32, kind="ExternalInput").ap()
            oT = nc.dram_tensor("oT", [D, nt], BF16, kind="ExternalInput").ap()
            items += wo_items(P, T, w_o, oT)
        if "ffn_b" in phases:
            wgu_b = nc.dram_tensor("wgu_b", [D, 2 * DFF], F32, kind="ExternalInput").ap()
            wd_b = nc.dram_tensor("wd_b", [DFF, D], F32, kind="ExternalInput").ap()
            items += ffn_items(P, T, wgu_b, wd_b, 2)
        if "ffn_a" in phases or "proj" in phases:
            modA = nc.dram_tensor("mod_a", [128, 9 * KC], F32, kind="ExternalInput").ap()
            ngA = nc.dram_tensor("ng_a", [128, 3 * KC], F32, kind="ExternalInput").ap()
            items.append(Item(None, 0, None, lambda s: t_load_mod(P, T, modA, ngA)))
        if "ffn_a" in phases:
            wgu_a = nc.dram_tensor("wgu_a", [D, 2 * DFF], F32, kind="ExternalInput").ap()
            wd_a = nc.dram_tensor("wd_a", [DFF, D], F32, kind="ExternalInput").ap()
            items += ffn_items(P, T, wgu_a, wd_a, 0)
        if "proj" in phases:
            t_alloc_proj(P, T)
            w_in = nc.dram_tensor("w_in", [D, INC], F32, kind="ExternalInput").ap()
            rope_d = nc.dram_tensor("rope_d", [4, 128, nt], F32, kind="ExternalInput").ap()
            rm_d = nc.dram_tensor("rm_d", [2, 128, 128], BF16, kind="ExternalInput").ap()
            qk_d = nc.dram_tensor("qk", [NQK * 128, nt], BF16, kind="ExternalOutput").ap()
            vv_d = nc.dram_tensor("vv", [nt, NV], BF16, kind="ExternalOutput").ap()
            gt_d = nc.dram_tensor("gt", [24, nt], F32, kind="ExternalOutput").ap()
            t_load_proj_consts(P, T, rope_d, rm_d)
            items += proj_items(P, T, w_in, qk_d, vv_d, gt_d)
        if "final" in phases:
            fg_d = nc.dram_tensor("fg_d", [128, KC], F32, kind="ExternalInput").ap()
            yT = nc.dram_tensor("yT", [D, nt], F32, kind="ExternalOutput").ap()
            T.fg = P.sb("fg", [128, KC], F32)
            P.dma("sp", T.fg[:], fg_d, "ld_fg", w=[("fg",)])
            items.append(Item(None, 0, None, lambda s: t_norm(P, T, 0, out_kind="final", gvec=T.fg, outbuf=yT)))
        if "xout" in phases:
            xout = nc.dram_tensor("xT_out", [D, nt], F32, kind="ExternalOutput").ap()

            def xo(s):
                P.dma_batch("sp", "st_x", [(xout[c * 128:(c + 1) * 128, :], T.xT[:, c, :],
                                            [("x", c, h) for h in range(T.nh)], []) for c in range(KC)])
            items.append(Item(None, 0, None, xo))
        run_stream(T.rings, items)
        P.finish("sp", [s for s in P.semobj if s.startswith("st_")])
        print("T kernel", phases, "ops", P.nops)
    return nc


def rope_tables(pos):
    nt = len(pos)
    out = np.zeros((4, 128, nt), np.float64)
    out[0] = 1.0
    out[2] = 1.0
    p = pos.astype(np.float64)
    for ti, (blk, half) in enumerate(((64, 8), (128, 16))):
        inv = np.exp(-math.log(ROPE_THETA) * np.arange(half, dtype=np.float32) / half).astype(np.float32)
        ang = (pos.astype(np.float32)[None, :] * inv[:, None]).astype(np.float32).astype(np.float64)
        for m in range(128):
            j = m % blk
            if j < half:
                out[2 * ti, m] = np.cos(ang[j])
                out[2 * ti + 1, m] = -np.sin(ang[j])
            elif j < 2 * half:
                out[2 * ti, m] = np.cos(ang[j - half])
                out[2 * ti + 1, m] = np.sin(ang[j - half])
    return out.astype(np.float32)


def rope_perm():
    rm = np.zeros((2, 128, 128), np.float32)
    for ti, (blk, half) in enumerate(((64, 8), (128, 16))):
        for m in range(128):
            j = m % blk
            if j < half:
                rm[ti, m + half, m] = 1.0
            elif j < 2 * half:
                rm[ti, m - half, m] = 1.0
    return rm.astype(NPBF)


def fm(v):
    v = np.asarray(v)
    lead = v.shape[:-1]
    n = v.shape[-1] // 128
    v = v.reshape(lead + (n, 128))
    return np.ascontiguousarray(np.moveaxis(v, -1, 0))


NSA_SCALE = 128 ** -0.5
DIFF_SCALE = 64 ** -0.5
NEGB = -30000.0
NCMP = 255
AX = mybir.AxisListType


def build_A(nqb=8):
    nc = bass.Bass("TRN2", target_bir_lowering=False)
    stack = ExitStack()
    SQ = nqb * 512
    with stack:
        P = Prog(nc, stack)

        def din(name, shape, dt):
            return nc.dram_tensor(name, shape, dt, kind="ExternalInput").ap()
        nq4_d = din("nq4", [512, S], BF16)
        kc_d = din("kc", [128, S], BF16)
        vc_d = din("vc", [128, S], BF16)
        ks_d = din("ks", [128, S], BF16)
        kw_d = din("kw", [128, S], BF16)
        vs_d = din("vs", [S, 128], BF16)
        vw_d = din("vw", [S, 128], BF16)
        dq_d = din("dq", [256, S], BF16)
        dk_d = din("dk", [256, S], BF16)
        dv_d = din("dv", [S, 256], BF16)
        g6_d = din("g6", [6, S], F32)
        w1_d = din("w1", [8192, 256], F32)
        w2_d = din("w2", [512, 128], F32)
        peT_d = din("peT", [128, 64], F32)
        lam_d = din("lam", [256], F32)
        subln_d = din("subln", [128, 1], F32)
        lamc_d = din("lamc", [128, 2], F32)
        cmask_d = din("cmask", [256, S], BF16)
        ov_d = din("ov", [256, 64], F32)
        btile_d = din("btile", [128, 32 * 64], F32)
        E_d = din("Emat", [128, 32 * 128], BF16)
        wm_d = din("wm", [128, 8 * 512], BF16)
        selG_d = din("selG", [32, 6 * 128], F32)
        ident_d = din("ident", [128, 128], F32)
        oT_d = nc.dram_tensor("oT", [512, S], BF16, kind="ExternalOutput").ap()

        arena = P.sb("arena", [128, 40960], BF16)
        nq4 = arena[:, 0:16384].rearrange("p (a b) -> p a b", a=4)
        kcT = arena[:, 16384:20480]
        vcT = arena[:, 20480:24576]
        ksT = arena[:, 24576:28672]
        kwT = arena[:, 28672:32768]
        vs = arena[:, 32768:36864].rearrange("p (a b) -> p a b", a=32)
        vw = arena[:, 36864:40960].rearrange("p (a b) -> p a b", a=32)
        dqz = arena[:, 0:16384].rearrange("p (h c b) -> p h c b", h=2, c=2)
        dkT = arena[:, 16384:24576].rearrange("p (a b) -> p a b", a=2)
        dv = arena[:, 24576:32768].rearrange("p (a b) -> p a b", a=32)

        oacc = P.sb("oacc", [128, 2, 2, 512], F32)
        biasT = P.sb("biasT", [128, 2, 512], BF16)
        cm_sb = P.sb("cm_sb", [128, 2, 2, 512], BF16)
        bt_sb = P.sb("bt_sb", [128, 2, 4, 64], F32)
        wm = P.sb("wm_sb", [128, 8, 512], BF16)
        Em = P.sb("Em", [128, 32, 128], BF16)
        ov = P.sb("ov_sb", [128, 2, 64], F32)
        selG = P.sb("selG_sb", [32, 6, 128], F32)
        identf = P.sb("identf", [128, 128], F32)
        onesb = P.sb("onesb", [128, 128], BF16)
        onesf = P.sb("onesf", [128, 128], F32)
        epsb = P.sb("epsb", [128, 1], F32)
        w1 = P.sb("w1_sb", [128, 32, 256], BF16)
        w2 = P.sb("w2_sb", [128, 2, 2, 128], BF16)
        peT = P.sb("peT_sb", [128, 2, 32], BF16)
        hidT = P.sb("hidT", [128, 2, 2, 256], BF16)
        cbias = P.sb("cbias", [128, 2, 2], F32)
        kcmpT = P.sb("kcmpT", [128, 256], BF16)
        vcmp = P.sb("vcmp", [128, 2, 128], BF16)
        pT = P.sb("pT", [128, 4, 512], BF16)
        em = P.sb("em", [128, 2, 512], F32)
        pf = P.sb("pf", [128, 2, 512], F32)
        pb = P.sb("pb", [128, 2, 512], BF16)
        psumT = P.sb("psumT", [128, 2, 512], F32)
        rden = P.sb("rden", [128, 2, 512], F32)
        tmpf = P.sb("tmpf", [128, 2, 512], F32)
        score = P.sb("score", [128, 64], F32)
        work = P.sb("work", [128, 64], F32)
        m8a = P.sb("m8a", [128, 8], F32)
        m8b = P.sb("m8b", [128, 8], F32)
        bval = P.sb("bval", [128, 128], F32)
        obf = P.sb("obf", [128, 2, 512], BF16)
        lamt = P.sb("lamt", [128, 256], F32)
        lprod = P.sb("lprod", [128, 2, 64], F32)
        lsc = P.sb("lsc", [128, 8], F32)
        subln = P.sb("subln_sb", [128, 1], F32)
        lamc = P.sb("lamc_sb", [128, 2], F32)
        ps = [P.psum("ps%d" % i, [128, 512], F32) for i in range(8)]
        cnt = dict(pT=0, ep=0, ob=0, st=0, ds=0)

        def mm(out, lhsT, rhs, start, stop, r, w, signal=True):
            P.op("pe", lambda e: e.matmul(out, lhsT, rhs, start=start, stop=stop), r=r, w=w, signal=signal)

        P.op("dve", lambda e: e.memset(onesb[:], 1.0), w=[("onesb",)])
        P.op("dve", lambda e: e.memset(onesf[:], 1.0), w=[("onesf",)])
        P.op("dve", lambda e: e.memset(epsb[:], EPS), w=[("epsb",)])
        P.op("dve", lambda e: e.memset(hidT[:], 0.0), w=[("hidT",)])
        P.op("dve", lambda e: e.memset(bval[:], 0.0), w=[("bval",)])
        P.dma("sp", wm[:].rearrange("p a b -> p (a b)"), wm_d, "ld_wm", w=[("wm",)])
        P.dma("sp", Em[:].rearrange("p a b -> p (a b)"), E_d, "ld_E", w=[("Em",)])
        P.dma("sp", ov[:], ov_d.rearrange("(c p) n -> p c n", p=128), "ld_ov", w=[("ov",)])
        P.dma("sp", selG[:].rearrange("p a b -> p (a b)"), selG_d, "ld_selG", w=[("selG",)])
        P.dma("sp", identf[:], ident_d, "ld_ident", w=[("identf",)])
        P.dma("sp", subln[:], subln_d, "ld_subln", w=[("subln",)])
        P.dma("sp", lamc[:], lamc_d, "ld_lamc", w=[("lamc",)])
        P.dma("sp", lamt[:], lam_d.partition_broadcast(128), "ld_lam", w=[("lamt",)])
        P.dma_batch("sp", "ld_nsa", [
            (nq4[:, j, :], nq4_d[j * 128:(j + 1) * 128, :], [], [("nq4", j)]) for j in range(4)] + [
            (kcT, kc_d, [], [("kcT",)]), (vcT, vc_d, [], [("vcT",)]), (ksT, ks_d, [], [("ksT",)]),
            (kwT, kw_d, [], [("kwT",)]),
            (vs, vs_d.rearrange("(t p) n -> p t n", p=128), [], [("vs",)]),
            (vw, vw_d.rearrange("(t p) n -> p t n", p=128), [], [("vw",)])])
        P.dma("pool", w2[:].rearrange("p j h n -> p (j h) n"), w2_d.rearrange("(a p) n -> p a n", p=128), "ld_w2",
              w=[("w2",)])
        P.dma("pool", peT[:].rearrange("p j l -> p (j l)"), peT_d, "ld_pe", w=[("peT",)])

        for i in range(2):
            P.op("dve", lambda e: e.tensor_tensor(out=lprod[:, i, :], in0=lamt[:, 128 * i:128 * i + 64],
                                                  in1=lamt[:, 128 * i + 64:128 * i + 128], op=ALU.mult),
                 r=[("lamt",)], w=[("lprod", i)])
            P.op("dve", lambda e: e.reduce_sum(out=lsc[:, i:i + 1], in_=lprod[:, i, :], axis=AX.X),
                 r=[("lprod", i)], w=[("lsc", i)])
            P.op("act", lambda e: e.activation(out=lsc[:, 2 + i:3 + i], in_=lsc[:, i:i + 1], func=AF.Exp),
                 r=[("lsc", i)], w=[("lsc", 2 + i)])
        P.op("dve", lambda e: e.tensor_tensor(out=lsc[:, 4:5], in0=lsc[:, 2:3], in1=lsc[:, 3:4], op=ALU.subtract),
             r=[("lsc", 2), ("lsc", 3)], w=[("lsc", 4)])
        P.op("dve", lambda e: e.tensor_tensor(out=lsc[:, 4:5], in0=lsc[:, 4:5], in1=lamc[:, 0:1], op=ALU.add),
             r=[("lsc", 4), ("lamc",)], w=[("lsc", 4)])
        P.op("dve", lambda e: e.tensor_scalar(out=lsc[:, 5:6], in0=lsc[:, 4:5], scalar1=-1.0, scalar2=None, op0=ALU.mult),
             r=[("lsc", 4)], w=[("lsc", 5)])
        P.op("dve", lambda e: e.tensor_tensor(out=lsc[:, 6:7], in0=subln[:], in1=lamc[:, 1:2], op=ALU.mult),
             r=[("subln",), ("lamc",)], w=[("lsc", 6)])

        for j in range(2):
            XT, xres = (kcT, ("kcT",)) if j == 0 else (vcT, ("vcT",))
            P.dma("pool", w1[:], w1_d[j * 4096:(j + 1) * 4096, :].rearrange("(l p) n -> p l n", p=128), "ld_w1",
                  w=[("w1",)])
            for hc in range(2):
                for l in range(32):
                    mm(ps[2][:, hc:hc + 1], w1[:, l, hc * 128:(hc + 1) * 128], peT[:, j, l:l + 1], l == 0, l == 31,
                       r=[("w1",), ("peT",)], w=[("ps", 2)], signal=(l == 31))
            P.op("act", lambda e: e.activation(out=cbias[:, j, :], in_=ps[2][:, 0:2], func=AF.Copy),
                 r=[("ps", 2)], w=[("cbias", j)])
            for hc in range(2):
                for l in range(32):
                    mm(ps[hc][:, 0:NCMP], w1[:, l, hc * 128:(hc + 1) * 128], XT[:, l:l + 16 * (NCMP - 1) + 1:16], l == 0, l == 31,
                       r=[("w1",), xres], w=[("ps", hc)], signal=(l == 31))
                P.op("act", lambda e: e.activation(out=hidT[:, j, hc, 0:NCMP], in_=ps[hc][:, 0:NCMP], func=AF.Silu,
                                                   bias=cbias[:, j, hc:hc + 1]),
                     r=[("ps", hc), ("cbias", j)], w=[("hidT",)])
            if j == 0:
                for hc in range(2):
                    mm(ps[3][:, 0:256], w2[:, 0, hc, :], hidT[:, 0, hc, :], hc == 0, hc == 1,
                       r=[("w2",), ("hidT",)], w=[("ps", 3)], signal=(hc == 1))
                P.op("act", lambda e: e.activation(out=kcmpT[:], in_=ps[3][:, 0:256], func=AF.Copy),
                     r=[("ps", 3)], w=[("kcmpT",)])
            else:
                for cc in range(2):
                    for hc in range(2):
                        mm(ps[3][:, cc * 128:(cc + 1) * 128], hidT[:, 1, hc, cc * 128:(cc + 1) * 128], w2[:, 1, hc, :],
                           hc == 0, hc == 1, r=[("w2",), ("hidT",)], w=[("ps", 3)], signal=(hc == 1))
                P.op("act", lambda e: e.activation(out=vcmp[:].rearrange("p a b -> p (a b)"), in_=ps[3][:, 0:256],
                                                   func=AF.Copy),
                     r=[("ps", 3)], w=[("vcmp",)])

        gsb6 = P.sb("gsb6", [128, 2, 6, 512], F32)
        rdenB = P.sb("rdenB", [128, 512], F32)
        deferred = []
        tile_clock = dict(i=0)

        def defer(delay, fn):
            deferred.append([tile_clock["i"] + delay, fn])

        def flush_deferred(all_=False):
            keep = []
            for due, fn in deferred:
                if all_ or due <= tile_clock["i"]:
                    fn()
                else:
                    keep.append([due, fn])
            deferred[:] = keep

        def gate_prefetch(hi, par, gidx, gslot):
            return

        def branch_finish(hi, par, accO, accD, gslot, first, es):
            if accD is not None:
                P.op("dve", lambda e: e.reciprocal(out=rden[:, es, :], in_=ps[accD][:]),
                     r=[("ps", accD)], w=[("rden", es)])
                P.op("dve", lambda e: e.tensor_tensor(out=rden[:, es, :], in0=rden[:, es, :], in1=gsb6[:, par, gslot, :], op=ALU.mult),
                     r=[("gsb6", par), ("rden", es)], w=[("rden", es)])
                fac, fres = rden[:, es, :], ("rden", es)
            else:
                fac, fres = gsb6[:, par, gslot, :], ("gsb6", par)
            if first:
                P.op("dve", lambda e: e.tensor_tensor(out=oacc[:, par, hi, :], in0=ps[accO][:], in1=fac, op=ALU.mult),
                     r=[("ps", accO), fres], w=[("oacc", par, hi)])
            else:
                P.op("dve", lambda e: e.tensor_tensor(out=tmpf[:, es, :], in0=ps[accO][:], in1=fac, op=ALU.mult),
                     r=[("ps", accO), fres], w=[("tmpf", es)])
                P.op("dve", lambda e: e.tensor_tensor(out=oacc[:, par, hi, :], in0=oacc[:, par, hi, :], in1=tmpf[:, es, :],
                                                      op=ALU.add),
                     r=[("oacc", par, hi), ("tmpf", es)], w=[("oacc", par, hi)])

        def causal_mask(r):
            if r < 0:
                return None
            return (4 + r, 128 * r, 512, 128 * r, 128 * r + 128)

        def window_mask(rw):
            if rw >= 4:
                return causal_mask(rw - 4)
            return (rw, 0, 128 * (rw + 1), 128 * rw, 128 * rw + 128)

        def make_tiles(kts, qT_ap, qres, kT, kres, vtile, vres, masks, scale, accO, accD, extra_bias=None, epilogue=None,
                       prologue=None, bias_res=None):
            n = len(kts)
            tiles = []
            for idx, kt in enumerate(kts):
                st = {}

                def emit_S(kt=kt, st=st, idx=idx):
                    if idx == 0 and prologue is not None:
                        prologue()
                    sbk = (0, 1, 6)[cnt["st"] % 3]
                    cnt["st"] += 1
                    st["sbk"] = sbk
                    ksl = slice(kt * 128, (kt + 1) * 128)
                    mk = masks(kt)
                    c0, c1 = (0, 512) if mk is None else (mk[1], mk[2])
                    if extra_bias is None:
                        mm(ps[sbk][:, c0:c1], kT[:, ksl], qT_ap[:, c0:c1], True, True, r=[kres, qres], w=[("ps", sbk)])
                    else:
                        mm(ps[sbk][:, c0:c1], kT[:, ksl], qT_ap[:, c0:c1], True, False, r=[kres, qres], w=[("ps", sbk)],
                           signal=False)
                        mm(ps[sbk][:, c0:c1], Em[:, kt, :], extra_bias[:, c0:c1], False, True, r=[("Em",), bias_res],
                           w=[("ps", sbk)])

                def emit_rest(kt=kt, st=st, idx=idx):
                    sbk = st["sbk"]
                    sl = cnt["pT"] % 4
                    cnt["pT"] += 1
                    mk = masks(kt)
                    c0, c1 = (0, 512) if mk is None else (mk[1], mk[2])
                    P.op("act", lambda e: e.activation(out=pT[:, sl, c0:c1], in_=ps[sbk][:, c0:c1], func=AF.Exp, scale=scale),
                         r=[("ps", sbk)], w=[("pT", sl)])
                    if mk is not None:
                        m0, m1 = mk[3], mk[4]
                        P.op("dve", lambda e: e.tensor_tensor(out=pT[:, sl, m0:m1], in0=pT[:, sl, m0:m1], in1=wm[:, mk[0], m0:m1],
                                                              op=ALU.mult),
                             r=[("pT", sl), ("wm",)], w=[("pT", sl)])
                    mm(ps[accO][:, c0:c1], vtile(kt), pT[:, sl, c0:c1], idx == 0, idx == n - 1, r=[vres, ("pT", sl)],
                       w=[("ps", accO)], signal=(idx == n - 1))
                    mm(ps[accD][:, c0:c1], onesb[:], pT[:, sl, c0:c1], idx == 0, idx == n - 1, r=[("onesb",), ("pT", sl)],
                       w=[("ps", accD)], signal=(idx == n - 1))
                    if idx == n - 1 and epilogue is not None:
                        epilogue()
                tiles.append((emit_S, emit_rest))
            return tiles

        def run_tiles(tiles, extra=()):
            extra = list(extra)
            ne, nt_ = len(extra), len(tiles)
            done = 0
            if nt_ == 0:
                for fn in extra:
                    fn()
                return
            tiles[0][0]()
            if nt_ > 1:
                tiles[1][0]()
            for i in range(nt_):
                if i + 2 < nt_:
                    tiles[i + 2][0]()
                tiles[i][1]()
                tile_clock["i"] += 1
                flush_deferred()
                want = ((i + 1) * ne + nt_ - 1) // nt_
                while done < min(want, ne):
                    extra[done]()
                    done += 1
            while done < ne:
                extra[done]()
                done += 1
            flush_deferred(all_=True)

        def phaseB_stages(qb):
            par = qb % 2
            qc = slice(qb * 512, (qb + 1) * 512)
            ncc = 1 if qb <= 3 else 2
            stg = []

            def ld():
                P.dma("sp", cm_sb[:, par, :, :], cmask_d[:, qc].rearrange("(c p) n -> p c n", p=128), "ld_cm%d" % par,
                      w=[("cm", par)])
                P.dma("sp", bt_sb[:, par, :, :].rearrange("p a b -> p (a b)"), btile_d[:, qb * 256:(qb + 1) * 256],
                      "ld_bt%d" % par, w=[("bt", par)])
                P.dma_batch("sp", "ld_g6%d" % par, [(gsb6[:, par, r, :], g6_d[r, qc].partition_broadcast(128), [],
                                                      [("gsb6", par)]) for r in range(6)])
            stg.append(ld)
            for j in range(4):
                for cc in range(ncc):
                    def s1(j=j, cc=cc):
                        mm(ps[7][:], kcmpT[:, cc * 128:(cc + 1) * 128], nq4[:, j, qc], True, True,
                           r=[("kcmpT",), ("nq4", j)], w=[("ps", 7)])
                        P.op("act", lambda e: e.activation(out=em[:, cc, :], in_=ps[7][:], func=AF.Exp, scale=NSA_SCALE),
                             r=[("ps", 7)], w=[("em", cc)])
                        P.op("dve", lambda e: e.tensor_tensor(out=em[:, cc, :], in0=em[:, cc, :], in1=cm_sb[:, par, cc, :],
                                                              op=ALU.mult),
                             r=[("em", cc), ("cm", par)], w=[("em", cc)])
                    stg.append(s1)
                    stg.append(lambda: None)

                def s2(j=j):
                    for cc in range(ncc):
                        mm(ps[7][:], onesf[:], em[:, cc, :], cc == 0, cc == ncc - 1, r=[("onesf",), ("em", cc)],
                           w=[("ps", 7)], signal=(cc == ncc - 1))
                    P.op("dve", lambda e: e.tensor_scalar(out=rdenB[:], in0=ps[7][:], scalar1=1e-30, scalar2=None, op0=ALU.max),
                         r=[("ps", 7)], w=[("rdenB",)])
                    P.op("dve", lambda e: e.reciprocal(out=rdenB[:], in_=rdenB[:]), r=[("rdenB",)], w=[("rdenB",)])
                stg.append(s2)

                def s3(j=j):
                    for cc in range(ncc):
                        dst = psumT if j == 0 else pf
                        dres = ("psumT", cc) if j == 0 else ("pf", cc)
                        P.op("dve", lambda e: e.tensor_tensor(out=dst[:, cc, :], in0=em[:, cc, :], in1=rdenB[:], op=ALU.mult),
                             r=[("em", cc), ("rdenB",)], w=[dres])
                        if j < 2:
                            P.op("dve", lambda e: e.tensor_copy(out=pb[:, cc, :], in_=dst[:, cc, :]), r=[dres], w=[("pb", cc)])
                        if j > 0:
                            P.op("dve", lambda e: e.tensor_tensor(out=psumT[:, cc, :], in0=psumT[:, cc, :], in1=pf[:, cc, :],
                                                                  op=ALU.add),
                                 r=[("psumT", cc), ("pf", cc)], w=[("psumT", cc)])
                stg.append(s3)
                stg.append(lambda: None)
                if j < 2:
                    def s4(j=j):
                        for cc in range(ncc):
                            mm(ps[7][:], vcmp[:, cc, :], pb[:, cc, :], cc == 0, cc == ncc - 1, r=[("vcmp",), ("pb", cc)],
                               w=[("ps", 7)], signal=(cc == ncc - 1))
                        branch_finish(j, par, 7, None, j * 3 + 0, True, 0)
                    stg.append(s4)
            for qt in range(4):
                def s5(qt=qt):
                    for cc in range(ncc):
                        mm(ps[7][:, 0:64], psumT[:, cc, qt * 128:(qt + 1) * 128], ov[:, cc, :], cc == 0,
                           cc == ncc - 1, r=[("psumT", cc), ("ov",)], w=[("ps", 7)], signal=(cc == ncc - 1))
                    P.op("dve", lambda e: e.tensor_tensor(out=score[:], in0=ps[7][:, 0:64],
                                                          in1=bt_sb[:, par, qt, :], op=ALU.add),
                         r=[("ps", 7), ("bt", par)], w=[("score",)])
                    P.op("dve", lambda e: e.max(out=m8a[:], in_=score[:]), r=[("score",)], w=[("m8a",)])
                    P.op("dve", lambda e: e.match_replace(out=work[:], in_to_replace=m8a[:], in_values=score[:], imm_value=-3.0e38),
                         r=[("score",), ("m8a",)], w=[("work",)])
                    P.op("dve", lambda e: e.max(out=m8b[:], in_=work[:]), r=[("work",)], w=[("m8b",)])
                    P.op("dve", lambda e: e.tensor_scalar(out=bval[:, 0:64], in0=score[:], scalar1=m8b[:, 7:8], scalar2=NEGB,
                                                          op0=ALU.is_lt, op1=ALU.mult),
                         r=[("score",), ("m8b",)], w=[("bval",)])
                stg.append(s5)
                stg.append(lambda: None)
                stg.append(lambda: None)
                stg.append(lambda: None)

                def s6(qt=qt):
                    mm(ps[7][:, 128:256], bval[:], identf[:], True, True, r=[("bval",), ("identf",)], w=[("ps", 7)])
                    P.op("act", lambda e: e.activation(out=biasT[:, par, qt * 128:(qt + 1) * 128], in_=ps[7][:, 128:256],
                                                       func=AF.Copy),
                         r=[("ps", 7)], w=[("biasT", par)])
                stg.append(s6)
                stg.append(lambda: None)
            return stg

        def cd_tiles(qb):
            par = qb % 2
            qc = slice(qb * 512, (qb + 1) * 512)
            tl = []
            for hi in range(2):
                ao, ad = (2, 3) if hi == 0 else (4, 5)
                tl += make_tiles(list(range(4 * qb + 4)), nq4[:, hi, qc], ("nq4", hi), ksT, ("ksT",),
                                 lambda kt: vs[:, kt, :], ("vs",),
                                 lambda kt, qb=qb: causal_mask(kt - 4 * qb), NSA_SCALE, ao, ad,
                                 extra_bias=biasT[:, par, :], bias_res=("biasT", par),
                                 epilogue=(lambda hi=hi, ao=ao, ad=ad, par=par: branch_finish(hi, par, ao, ad, hi * 3 + 1, False, hi)))
            for hi in range(2):
                ao, ad = (2, 3) if hi == 0 else (4, 5)
                kts = [kt for kt in range(4 * qb - 4, 4 * qb + 4) if kt >= 0]
                if qb >= 1:
                    kts = [4 * qb - 1] + [kt for kt in kts if kt != 4 * qb - 1]

                def epi(hi=hi, ao=ao, ad=ad, par=par, qc=qc):
                    branch_finish(hi, par, ao, ad, hi * 3 + 2, False, hi)
                    os_ = cnt["ob"] % 2
                    cnt["ob"] += 1
                    P.op("dve", lambda e: e.tensor_copy(out=obf[:, os_, :], in_=oacc[:, par, hi, :]),
                         r=[("oacc", par, hi)], w=[("obf", os_)])
                    P.dma("sp", oT_d[(2 + hi) * 128:(3 + hi) * 128, qc], obf[:, os_, :], "st_ob%d" % os_, r=[("obf", os_)])
                tl += make_tiles(kts, nq4[:, hi, qc], ("nq4", hi), kwT, ("kwT",), lambda kt: vw[:, kt, :], ("vw",),
                                 lambda kt, qb=qb: window_mask(kt - (4 * qb - 4)), NSA_SCALE, ao, ad,
                                 epilogue=epi)
            return tl

        for fn in phaseB_stages(0):
            fn()
        for qb in range(nqb):
            run_tiles(cd_tiles(qb), extra=(phaseB_stages(qb + 1) if qb + 1 < nqb else ()))

        P.barrier("sp")
        P.barrier("dve")
        P.op("dve", lambda e: e.memset(arena[:, 0:16384], 0.0), w=[("dqz",)])
        lst = []
        for h in range(2):
            for c in range(2):
                lst.append((dqz[64 * c:64 * c + 64, h, c, :], dq_d[h * 128 + 64 * c:h * 128 + 64 * c + 64, :], [], [("dqz",)]))
            lst.append((dkT[:, h, :], dk_d[h * 128:(h + 1) * 128, :], [], [("dkT",)]))
        lst.append((dv, dv_d.rearrange("(t p) n -> p t n", p=128), [], [("dv",)]))
        P.dma_batch("sp", "ld_diff", lst)
        tl = []
        for h in range(2):
            for qb in range(nqb):
                qc = slice(qb * 512, (qb + 1) * 512)
                acc = [(2, 3), (4, 5)]

                def epi(h=h, qb=qb, qc=qc, acc=acc):
                    flush_deferred(all_=True)
                    for c in range(2):
                        ao, ad = acc[c]
                        P.op("dve", lambda e: e.reciprocal(out=rden[:, c, :], in_=ps[ad][:]),
                             r=[("ps", ad)], w=[("rden", c)])
                        P.op("dve", lambda e: e.tensor_tensor(out=tmpf[:, c, :], in0=ps[ao][:], in1=rden[:, c, :], op=ALU.mult),
                             r=[("ps", ao), ("rden", c)], w=[("tmpf", c)])
                    P.op("dve", lambda e: e.scalar_tensor_tensor(out=tmpf[:, 0, :], in0=tmpf[:, 1, :], scalar=lsc[:, 5:6],
                                                                 in1=tmpf[:, 0, :], op0=ALU.mult, op1=ALU.add),
                         r=[("tmpf", 0), ("tmpf", 1), ("lsc", 5)], w=[("tmpf", 0)])

                    def d1():
                        P.op("act", lambda e: e.activation(out=em[:, 0, :], in_=tmpf[:, 0, :], func=AF.Square),
                             r=[("tmpf", 0)], w=[("em", 0)])

                    def d2():
                        mm(ps[7][:], onesf[:], em[:, 0, :], True, True, r=[("onesf",), ("em", 0)], w=[("ps", 7)])

                    def d3():
                        P.op("act", lambda e: e.activation(out=em[:, 1, :], in_=ps[7][:], func=AF.Ln, scale=1.0 / 128, bias=epsb[:]),
                             r=[("ps", 7), ("epsb",)], w=[("em", 1)])
                        P.op("act", lambda e: e.activation(out=em[:, 1, :], in_=em[:, 1, :], func=AF.Exp, scale=-0.5),
                             r=[("em", 1)], w=[("em", 1)])

                    def d4():
                        os_ = cnt["ob"] % 2
                        cnt["ob"] += 1
                        P.op("dve", lambda e: e.scalar_tensor_tensor(out=obf[:, os_, :], in0=tmpf[:, 0, :], scalar=lsc[:, 6:7],
                                                                     in1=em[:, 1, :], op0=ALU.mult, op1=ALU.mult),
                             r=[("tmpf", 0), ("lsc", 6), ("em", 1)], w=[("obf", os_)])
                        P.dma("sp", oT_d[h * 128:(h + 1) * 128, qc], obf[:, os_, :], "st_ob%d" % os_, r=[("obf", os_)])
                    defer(3, d1)
                    defer(4, d2)
                    defer(6, d3)
                    defer(7, d4)
                for c in range(2):
                    ao, ad = acc[c]
                    tl += make_tiles(list(range(4 * qb + 4)), dqz[:, h, c, qc], ("dqz",), dkT[:, h, :], ("dkT",),
                                     lambda kt, h=h: dv[:, kt, h * 128:(h + 1) * 128], ("dv",),
                                     lambda kt, qb=qb: causal_mask(kt - 4 * qb), DIFF_SCALE, ao, ad,
                                     epilogue=(epi if c == 1 else None))
        run_tiles(tl)
        P.finish("sp", [s for s in P.semobj if s.startswith("st_")])
        print("A kernel ops", P.nops)
    return nc


def attn_consts():
    cm = np.zeros((256, S), np.float32)
    t = np.arange(S)
    for c in range(NCMP):
        cm[c] = (16 * c + 31 <= t)
    cs = np.arange(NCMP) * 16
    ss = np.arange(64) * 64
    ovm = np.clip(np.minimum(cs[:, None] + 32, ss[None, :] + 64) - np.maximum(cs[:, None], ss[None, :]), 0, None) / 32.0
    ov = np.zeros((256, 64), np.float32)
    ov[:NCMP] = ovm
    bt = np.zeros((128, 32, 64), np.float32)
    for tile in range(32):
        tt = tile * 128 + np.arange(128)
        m = np.arange(64)[None, :]
        valid = m * 64 <= tt[:, None]
        cur = (tt // 64)[:, None]
        forced = (m == 0) | (m == cur) | (m == cur - 1)
        bt[:, tile, :] = np.where(valid, np.where(forced, 1e6, 0.0), -1e30)
    E = np.zeros((128, 32, 128), np.float32)
    for kt in range(32):
        E[2 * kt, kt, 0:64] = 1.0
        E[2 * kt + 1, kt, 64:128] = 1.0
    wmk = np.zeros((128, 8, 512), np.float32)
    p = np.arange(128)[:, None]
    q = np.arange(512)[None, :]
    for r in range(8):
        k = (r - 4) * 128 + p
        wmk[:, r, :] = ((q - k >= 0) & (q - k < 512))
    selG = np.zeros((32, 6, 128), np.float32)
    for r in range(6):
        selG[r, r, :] = 1.0
    return dict(cmask=cm.astype(NPBF), ov=ov, btile=bt.reshape(128, 32 * 64), Emat=E.reshape(128, 32 * 128).astype(NPBF),
                wm=wmk.reshape(128, 8 * 512).astype(NPBF), selG=selG.reshape(32, 6 * 128),
                ident=np.eye(128, dtype=np.float32))


MCOLS = 9 * D // NCORE
MCH = MCOLS // 128


def build_M():
    nc = bass.Bass("TRN2", target_bir_lowering=False)
    stack = ExitStack()
    with stack:
        P = Prog(nc, stack)
        cT_d = nc.dram_tensor("cT", [128, KC * B], F32, kind="ExternalInput").ap()
        wada_d = nc.dram_tensor("wada", [DEPTH * D, MCOLS], F32, kind="ExternalInput").ap()
        bT_d = nc.dram_tensor("bT", [128, DEPTH * MCH], F32, kind="ExternalInput").ap()
        modp_d = nc.dram_tensor("modp", [128, DEPTH * MCH * B], F32, kind="ExternalOutput").ap()
        cT = P.sb("cT_sb", [128, KC, B], F32)
        cs = P.sb("cs_sb", [128, KC, B], BF16)
        bT = P.sb("bT_sb", [128, DEPTH * MCH], F32)
        wA = P.sb("wA", [128, 4, KC, 256], BF16)
        osb = P.sb("osb", [128, DEPTH * MCH, B], F32)
        ps = [P.psum("ps%d" % i, [128, 512], F32) for i in range(2)]
        rings = {"A": dict(size=4, next=0, last=[None] * 4)}
        P.dma("sp", cT[:].rearrange("p a b -> p (a b)"), cT_d, "ld_c", w=[("cT",)])
        P.dma("sp", bT[:], bT_d, "ld_b", w=[("bT",)])
        P.op("act", lambda e: e.activation(out=cs[:], in_=cT[:], func=AF.Silu), r=[("cT",)], w=[("cs",)])
        items = []
        cntr = dict(b=0)

        def mk(l, s):
            def load(slots):
                P.dma("pool", wA[:, slots[0], :, :],
                      wada_d[l * D:(l + 1) * D, s * 256:(s + 1) * 256].rearrange("(kc p) n -> p kc n", p=128),
                      "ldA%d" % slots[0], w=[("A", slots[0])])

            def compute(slots):
                sl = slots[0]
                for fc in range(2):
                    ch = l * MCH + s * 2 + fc
                    bank = cntr["b"] % 2
                    cntr["b"] += 1
                    for kc in range(KC):
                        P.op("pe", lambda e: e.matmul(ps[bank][:, 0:B], wA[:, sl, kc, fc * 128:(fc + 1) * 128], cs[:, kc, :],
                                                      start=(kc == 0), stop=(kc == KC - 1)),
                             r=[("A", sl), ("cs",)], w=[("ps", bank)], signal=(kc == KC - 1))
                    P.op("act", lambda e: e.activation(out=osb[:, ch, :], in_=ps[bank][:, 0:B], func=AF.Identity,
                                                       bias=bT[:, ch:ch + 1], scale=1.0),
                         r=[("ps", bank), ("bT",)], w=[("osb",)])
            return Item("A", 1, load, compute)
        for l in range(DEPTH):
            for s in range(MCH // 2):
                items.append(mk(l, s))
        run_stream(rings, items)
        P.dma("sp", modp_d, osb[:].rearrange("p a b -> p (a b)"), "st_mod", r=[("osb",)])
        P.finish("sp", ["st_mod"])
        print("M kernel ops", P.nops)
    return nc


_CACHE = {}


def _prog(key, fn):
    if key not in _CACHE:
        _CACHE[key] = fn()
    return _CACHE[key]


def _run(nc, in_maps):
    res = bass_utils.run_bass_kernel_spmd(nc, in_maps, core_ids=list(range(NCORE)))
    return res.results


def kernel(x, c, w_ada, b_ada, norm_g, ffn_w_gu, ffn_w_d, w_in, w_o, diff_lam, diff_subln, cmp_pe, cmp_w1, cmp_w2, final_g):
    f32 = np.float32
    x = np.asarray(x, f32)
    c = np.asarray(c, f32)
    ncM = _prog("M", build_M)
    cT = np.ascontiguousarray(c.reshape(B, KC, 128).transpose(2, 1, 0)).reshape(128, KC * B)
    in_maps = []
    for core in range(NCORE):
        cols = slice(core * MCOLS, (core + 1) * MCOLS)
        wa = np.ascontiguousarray(np.asarray(w_ada)[:, :, cols]).reshape(DEPTH * D, MCOLS)
        bt = np.ascontiguousarray(np.asarray(b_ada)[:, cols].reshape(DEPTH, MCH, 128).transpose(2, 0, 1)).reshape(128, DEPTH * MCH)
        in_maps.append({"cT": cT, "wada": wa, "bT": bt})
    resM = _run(ncM, in_maps)
    mod = np.zeros((B, DEPTH, 9 * D), f32)
    for core in range(NCORE):
        mp = resM[core]["modp"].reshape(128, DEPTH, MCH, B)
        mod[:, :, core * MCOLS:(core + 1) * MCOLS] = mp.transpose(3, 1, 2, 0).reshape(B, DEPTH, MCOLS)
    modT = [[np.ascontiguousarray(fm(mod[b, l].reshape(9, D))).reshape(128, 9 * KC) for l in range(DEPTH)] for b in range(B)]
    ngT = [np.ascontiguousarray(fm(np.asarray(norm_g[l], f32))).reshape(128, 3 * KC) for l in range(DEPTH)]
    rm = rope_perm()
    ropes = [rope_tables(np.arange(i * NT, (i + 1) * NT)) for i in range(4)]
    aconst = attn_consts()

    xT = [np.ascontiguousarray(x[core // 4, (core % 4) * NT:(core % 4 + 1) * NT].T) for core in range(NCORE)]
    oT = None
    for l in range(DEPTH + 1):
        phases = []
        if l > 0:
            phases += ["wo", "ffn_b"]
        if l < DEPTH:
            phases += ["ffn_a", "proj", "xout"]
        else:
            phases += ["final"]
        ncT = _prog("T" + ",".join(phases), lambda: build_T(phases))
        in_maps = []
        for core in range(NCORE):
            b, i = core // 4, core % 4
            m = {"xT_in": xT[core]}
            if l > 0:
                m["mod_b"] = modT[b][l - 1]
                m["ng_b"] = ngT[l - 1]
                m["w_o"] = np.asarray(w_o[l - 1], f32)
                m["oT"] = oT[core]
                m["wgu_b"] = np.asarray(ffn_w_gu[l - 1, 1], f32)
                m["wd_b"] = np.asarray(ffn_w_d[l - 1, 1], f32)
            if l < DEPTH:
                m["mod_a"] = modT[b][l]
                m["ng_a"] = ngT[l]
                m["wgu_a"] = np.asarray(ffn_w_gu[l, 0], f32)
                m["wd_a"] = np.asarray(ffn_w_d[l, 0], f32)
                m["w_in"] = np.asarray(w_in[l], f32)
                m["rope_d"] = ropes[i]
                m["rm_d"] = rm
            else:
                m["fg_d"] = np.ascontiguousarray(fm(np.asarray(final_g, f32)))
            in_maps.append(m)
        resT = _run(ncT, in_maps)
        if l == DEPTH:
            out = np.zeros((B, S, D), f32)
            for core in range(NCORE):
                b, i = core // 4, core % 4
                out[b, i * NT:(i + 1) * NT, :] = resT[core]["yT"].T
            return out
        xT = [resT[core]["xT_out"] for core in range(NCORE)]
        ncA = _prog("A", build_A)
        lam_init = 0.8 - 0.6 * math.exp(-0.3 * l)
        in_maps = []
        for core in range(NCORE):
            b, hg = core // 4, core % 4
            g = hg // 2
            qk = np.concatenate([resT[b * 4 + i]["qk"] for i in range(4)], axis=1)
            vv = np.concatenate([resT[b * 4 + i]["vv"] for i in range(4)], axis=0)
            gt = np.concatenate([resT[b * 4 + i]["gt"] for i in range(4)], axis=1)

            def ch(i0, n=1):
                return qk[i0 * 128:(i0 + n) * 128]
            own = [2 * hg, 2 * hg + 1]
            oth = [h for h in range(4 * g, 4 * g + 4) if h not in own]
            m = dict(aconst)
            m["nq4"] = np.concatenate([ch(16 + h) for h in own + oth], 0)
            m["kc"] = ch(24 + g)
            m["vc"] = ch(30 + g)
            m["ks"] = ch(26 + g)
            m["kw"] = ch(28 + g)
            m["vs"] = vv[:, 1024 + 128 * g:1024 + 128 * (g + 1)]
            m["vw"] = vv[:, 1280 + 128 * g:1280 + 128 * (g + 1)]
            m["dq"] = ch(2 * hg, 2)
            m["dk"] = ch(8 + 2 * hg, 2)
            m["dv"] = vv[:, 256 * hg:256 * (hg + 1)]
            m["g6"] = gt[6 * hg:6 * hg + 6]
            m["w1"] = np.asarray(cmp_w1[l], f32).reshape(8192, 256)
            m["w2"] = np.asarray(cmp_w2[l], f32).reshape(512, 128)
            pe = np.asarray(cmp_pe[l], f32)
            m["peT"] = np.concatenate([pe[0].T, pe[1].T], 1)
            m["lam"] = np.asarray(diff_lam[l], f32).reshape(256)
            m["subln"] = np.asarray(diff_subln[l], f32).reshape(128, 1)
            m["lamc"] = np.tile(np.array([[lam_init, 1.0 - lam_init]], f32), (128, 1))
            in_maps.append({k: np.ascontiguousarray(v) for k, v in m.items()})
        resA = _run(ncA, in_maps)
        oT = []
        for core in range(NCORE):
            b, i = core // 4, core % 4
            full = np.zeros((D, NT), dtype=NPBF)
            for hg in range(4):
                o = resA[b * 4 + hg]["oT"][:, i * NT:(i + 1) * NT]
                full[256 * hg:256 * (hg + 1)] = o[0:256]
                full[1024 + 256 * hg:1024 + 256 * (hg + 1)] = o[256:512]
            oT.append(full)
```
